# Optimizing a Trainium2 kernel written in Bass

```python
import math
import jax
import jax.numpy as jnp
from jax import lax
import numpy as np

D_MODEL = 1024
BATCH = 8
SEQ = 2048
DEPTH = 4
DEC_BATCH = 32
DEC_SEQ = 4
PAST_LEN = 8192
PAGE_SIZE = 128

N_A = DEPTH // 2
N_B = DEPTH - N_A
HEAD_DIM = 64
N_HEADS = D_MODEL // HEAD_DIM
N_KV = N_HEADS // 4
GQA = N_HEADS // N_KV
WIDTH = N_HEADS * HEAD_DIM
KV_WIDTH = N_KV * HEAD_DIM
PLE_DIM = 256
N_BUCKETS = 32
REL_MAX_DIST = 4096
CMP_BLOCK = 32
CMP_STRIDE = 16
CMP_HIDDEN = 256
SLC_BLOCK = 64
N_SEL = 16
WIN_A = 512
DIL_PATTERNS = ((128, 1), (512, 4), (2048, 16))
WIN_B = 2048
Q_BLOCK = 128
SLC_Q_BLOCK = 16
FORCE_BONUS = 1000.0
NEG = -1e30
EPS = 1e-6
SCALE = HEAD_DIM ** -0.5
A_IN = WIDTH + 6 * KV_WIDTH + 3 * N_HEADS + 3 * WIDTH
B_IN = 3 * WIDTH + WIDTH

kernel_name = 'yoco_nsa_dilated_step'


def rms_norm(x, g):
    xf = x.astype(jnp.float32)
    y = xf * lax.rsqrt(jnp.mean(xf * xf, axis=-1, keepdims=True) + EPS)
    return (y * g.astype(jnp.float32)).astype(x.dtype)


def rel_bucket(dist):
    n = jnp.maximum(dist, 0)
    exact = N_BUCKETS // 2
    nf = jnp.maximum(n, 1).astype(jnp.float32)
    big = exact + (jnp.log(nf / exact) / math.log(REL_MAX_DIST / exact) * (N_BUCKETS - exact)).astype(jnp.int32)
    return jnp.where(n < exact, n, jnp.minimum(big, N_BUCKETS - 1))


def head_bias(table, dist):
    b = table.astype(jnp.float32)[rel_bucket(dist)]
    b = b.reshape(dist.shape + (N_KV, GQA))
    return jnp.moveaxis(b, (-2, -1), (0, 1))


def masked_softmax(logits, mask):
    m = jnp.max(jnp.where(mask, logits, NEG), axis=-1, keepdims=True)
    m = jnp.where(m > 0.5 * NEG, m, 0.0)
    e = jnp.exp(jnp.where(mask, logits - m, NEG))
    den = jnp.sum(e, axis=-1, keepdims=True)
    ok = den > 0
    safe = jnp.where(ok, den, 1.0)
    probs = e / safe
    lse = jnp.where(ok, m + jnp.log(safe), NEG)
    return probs, lse[..., 0]


def dense_attend(q, qpos, k, v, kpos, table, window):
    dist = qpos[:, None] - kpos[None, :]
    mask = dist >= 0
    if window is not None:
        mask = mask & (dist <= window) & (kpos >= 0)[None, :]
    logits = jnp.einsum('btgrd,bkgd->bgrtk', q.astype(jnp.float32), k.astype(jnp.float32)) * SCALE
    logits = logits + head_bias(table, dist)[None]
    probs, lse = masked_softmax(logits, mask)
    out = jnp.einsum('bgrtk,bkgd->btgrd', probs.astype(v.dtype), v)
    return out, probs, lse


def compress(raw, w1, b1, w2, b2):
    bsz, L = raw.shape[:2]
    nc = (L - CMP_BLOCK) // CMP_STRIDE + 1
    ch = raw[:, :(nc + 1) * CMP_STRIDE].reshape(bsz, nc + 1, CMP_STRIDE, N_KV, HEAD_DIM)
    ch = ch.transpose(0, 1, 3, 2, 4).reshape(bsz, nc + 1, N_KV, CMP_STRIDE * HEAD_DIM)
    hid = jax.nn.silu(jnp.einsum('bcgf,fh->bcgh', ch[:, :nc], w1[0])
                      + jnp.einsum('bcgf,fh->bcgh', ch[:, 1:], w1[1]) + b1)
    return jnp.einsum('bcgh,hd->bcgd', hid, w2) + b2


def select_blocks(probs, qpos, n_blocks):
    nc = probs.shape[-1]
    cs = jnp.arange(nc) * CMP_STRIDE
    ss = jnp.arange(n_blocks) * SLC_BLOCK
    ov = jnp.clip(jnp.minimum(cs[:, None] + CMP_BLOCK, ss[None, :] + SLC_BLOCK)
                  - jnp.maximum(cs[:, None], ss[None, :]), 0, None).astype(jnp.float32) / CMP_BLOCK
    score = jnp.einsum('bgrtc,cs->btgs', probs.astype(jnp.float32), ov)
    cur = (qpos // SLC_BLOCK)[:, None]
    j = jnp.arange(n_blocks)[None, :]
    forced = (j == 0) | (j == cur) | (j == cur - 1)
    score = score + jnp.where(forced, FORCE_BONUS, 0.0)[None, :, None, :]
    score = jnp.where((j <= cur)[None, :, None, :], score, -1.0)
    _, idx = lax.top_k(score, min(N_SEL, n_blocks))
    return idx


def slc_attend(q, qpos, kb, vb, idx, table):
    bsz, tq, g, nsel = idx.shape
    bi = jnp.arange(bsz)[:, None, None, None]
    gi = jnp.arange(g)[None, None, :, None]
    kg = kb[bi, gi, idx].reshape(bsz, tq, g, nsel * SLC_BLOCK, HEAD_DIM)
    vg = vb[bi, gi, idx].reshape(bsz, tq, g, nsel * SLC_BLOCK, HEAD_DIM)
    kpos = (idx[..., None] * SLC_BLOCK + jnp.arange(SLC_BLOCK)).reshape(bsz, tq, g, nsel * SLC_BLOCK)
    dist = qpos[None, :, None, None] - kpos
    mask = (dist >= 0)[:, :, :, None, :]
    tab = table.astype(jnp.float32).reshape(N_BUCKETS, N_KV, GQA).transpose(1, 0, 2)
    bias = jnp.moveaxis(tab[gi, rel_bucket(dist)], -1, 3)
    logits = jnp.einsum('btgrd,btgkd->btgrk', q.astype(jnp.float32), kg.astype(jnp.float32)) * SCALE + bias
    probs, _ = masked_softmax(logits, mask)
    return jnp.einsum('btgrk,btgkd->btgrd', probs.astype(vg.dtype), vg)


def window_prompt(q, k, v, table):
    bsz, S = q.shape[:2]
    pad = ((0, 0), (WIN_A, 0), (0, 0), (0, 0))
    kp = jnp.pad(k, pad)
    vp = jnp.pad(v, pad)

    def blk(b):
        s0 = b * Q_BLOCK
        q_b = lax.dynamic_slice_in_dim(q, s0, Q_BLOCK, axis=1)
        k_b = lax.dynamic_slice_in_dim(kp, s0, Q_BLOCK + WIN_A, axis=1)
        v_b = lax.dynamic_slice_in_dim(vp, s0, Q_BLOCK + WIN_A, axis=1)
        qpos = s0 + jnp.arange(Q_BLOCK)
        kpos = s0 - WIN_A + jnp.arange(Q_BLOCK + WIN_A)
        return dense_attend(q_b, qpos, k_b, v_b, kpos, table, WIN_A)[0]

    o = lax.map(blk, jnp.arange(S // Q_BLOCK))
    return jnp.moveaxis(o, 0, 1).reshape(bsz, S, N_KV, GQA, HEAD_DIM)


def a_project(xn, w_in, gate_b, qk_norm):
    bsz, T = xn.shape[:2]
    u = xn @ w_in
    o1 = WIDTH
    o2 = o1 + 6 * KV_WIDTH
    o3 = o2 + 3 * N_HEADS
    q = rms_norm(u[..., :o1].reshape(bsz, T, N_KV, GQA, HEAD_DIM), qk_norm[0])
    kv = u[..., o1:o2].reshape(bsz, T, 6, N_KV, HEAD_DIM)
    kc, vc = kv[:, :, 0], kv[:, :, 1]
    ks, vs = rms_norm(kv[:, :, 2], qk_norm[2]), kv[:, :, 3]
    kw, vw = rms_norm(kv[:, :, 4], qk_norm[3]), kv[:, :, 5]
    gates = jax.nn.sigmoid(u[..., o2:o3].reshape(bsz, T, 3, N_KV, GQA) + gate_b.reshape(3, N_KV, GQA))
    z = u[..., o3:].reshape(bsz, T, 3, N_KV, GQA, HEAD_DIM)
    return q, kc, vc, ks, vs, kw, vw, gates, z


def nsa_combine(o_c, o_s, o_w, gates, z, w_out):
    o = jnp.stack([o_c, o_s, o_w], axis=2)
    y = jnp.sum(gates[..., None] * o * jax.nn.silu(z), axis=2)
    return y.reshape(y.shape[0], y.shape[1], WIDTH) @ w_out


def nsa_compressed(q, qpos, raw_k, raw_v, w, i):
    kcc = rms_norm(compress(raw_k, w['a_phi_w1'][i, 0], w['a_phi_b1'][i, 0], w['a_phi_w2'][i, 0], w['a_phi_b2'][i, 0]),
                   w['a_qk_norm'][i, 1])
    vcc = compress(raw_v, w['a_phi_w1'][i, 1], w['a_phi_b1'][i, 1], w['a_phi_w2'][i, 1], w['a_phi_b2'][i, 1])
    kend = jnp.arange(kcc.shape[1]) * CMP_STRIDE + CMP_BLOCK - 1
    o_c, p_c, _ = dense_attend(q, qpos, kcc, vcc, kend, w['rel_bias'], None)
    return o_c, p_c


def nsa_prompt(xn, w, i):
    table = w['rel_bias']
    q, kc, vc, ks, vs, kw, vw, gates, z = a_project(xn, w['a_w_in'][i], w['a_gate_b'][i], w['a_qk_norm'][i])
    bsz, S = xn.shape[:2]
    pos = jnp.arange(S)
    o_c, p_c = nsa_compressed(q, pos, kc, vc, w, i)
    ns = S // SLC_BLOCK
    idx = select_blocks(p_c, pos, ns)
    kb = ks.reshape(bsz, ns, SLC_BLOCK, N_KV, HEAD_DIM).transpose(0, 3, 1, 2, 4)
    vb = vs.reshape(bsz, ns, SLC_BLOCK, N_KV, HEAD_DIM).transpose(0, 3, 1, 2, 4)
    nq = S // SLC_Q_BLOCK
    qc = jnp.moveaxis(q.reshape(bsz, nq, SLC_Q_BLOCK, N_KV, GQA, HEAD_DIM), 1, 0)
    pc = pos.reshape(nq, SLC_Q_BLOCK)
    ic = jnp.moveaxis(idx.reshape(bsz, nq, SLC_Q_BLOCK, N_KV, idx.shape[-1]), 1, 0)
    o_s = lax.map(lambda a: slc_attend(a[0], a[1], kb, vb, a[2], table), (qc, pc, ic))
    o_s = jnp.moveaxis(o_s, 0, 1).reshape(bsz, S, N_KV, GQA, HEAD_DIM)
    o_w = window_prompt(q, kw, vw, table)
    y = nsa_combine(o_c, o_s, o_w, gates, z, w['a_w_out'][i])
    keep = min(WIN_A, S)
    return y, jnp.stack([kc, vc], 2), jnp.stack([ks, vs], 2), jnp.stack([kw, vw], 2)[:, S - keep:]


def nsa_sample(xn, cache_cmp, cache_slc, win_buf, page_table, w, i):
    table = w['rel_bias']
    q, kc, vc, ks, vs, kw, vw, gates, z = a_project(xn, w['a_w_in'][i], w['a_gate_b'][i], w['a_qk_norm'][i])
    bsz, T = xn.shape[:2]
    past = page_table.shape[1] * PAGE_SIZE
    qpos = past + jnp.arange(T)
    new_c = jnp.stack([kc, vc], 2)
    new_s = jnp.stack([ks, vs], 2)
    new_w = jnp.stack([kw, vw], 2)
    full_c = jnp.concatenate([cache_cmp[page_table].reshape(bsz, past, 2, N_KV, HEAD_DIM), new_c], axis=1)
    o_c, p_c = nsa_compressed(q, qpos, full_c[:, :, 0], full_c[:, :, 1], w, i)
    L = past + T
    ns = -(-L // SLC_BLOCK)
    full_s = jnp.concatenate([cache_slc[page_table].reshape(bsz, past, 2, N_KV, HEAD_DIM), new_s], axis=1)
    full_s = jnp.pad(full_s, ((0, 0), (0, ns * SLC_BLOCK - L), (0, 0), (0, 0), (0, 0)))
    idx = select_blocks(p_c, qpos, ns)
    kb = full_s[:, :, 0].reshape(bsz, ns, SLC_BLOCK, N_KV, HEAD_DIM).transpose(0, 3, 1, 2, 4)
    vb = full_s[:, :, 1].reshape(bsz, ns, SLC_BLOCK, N_KV, HEAD_DIM).transpose(0, 3, 1, 2, 4)
    o_s = slc_attend(q, qpos, kb, vb, idx, table)
    wb = win_buf.shape[1]
    full_w = jnp.concatenate([win_buf, new_w], axis=1)
    kpos = past - wb + jnp.arange(wb + T)
    o_w = dense_attend(q, qpos, full_w[:, :, 0], full_w[:, :, 1], kpos, table, WIN_A)[0]
    y = nsa_combine(o_c, o_s, o_w, gates, z, w['a_w_out'][i])
    return y, new_c, new_s, full_w[:, T:]


def shared_kv(h, w):
    bsz, T = h.shape[:2]
    kv = (rms_norm(h, w['kv_norm_g']) @ w['b_w_kv']).reshape(bsz, T, 2, N_KV, HEAD_DIM)
    return rms_norm(kv[:, :, 0], w['b_k_norm']), kv[:, :, 1]


def b_project(xn, w, j):
    bsz, T = xn.shape[:2]
    u = xn @ w['b_w_in'][j]
    q3 = rms_norm(u[..., :3 * WIDTH].reshape(bsz, T, 3, N_KV, GQA, HEAD_DIM), w['b_q_norm'][j][:, None, None, :])
    return q3, u[..., 3 * WIDTH:]


def merge_groups(outs, lses):
    alpha = jax.nn.softmax(jnp.stack(lses, 0), axis=0)
    return jnp.einsum('nbtgr,nbtgrd->btgrd', alpha.astype(outs[0].dtype), jnp.stack(outs, 0))


def b_output(o, z, w, j):
    bsz, T = o.shape[:2]
    return (o.reshape(bsz, T, WIDTH) * jax.nn.silu(z)) @ w['b_w_out'][j]


def dilated_prompt(q3, k, v, table):
    bsz, S = q3.shape[:2]
    nb = S // Q_BLOCK
    outs, lses = [], []
    for g, (win, d) in enumerate(DIL_PATTERNS):
        n = win // d
        qbm = Q_BLOCK // d
        sd = S // d
        qg = q3[:, :, g].reshape(bsz, sd, d, N_KV, GQA, HEAD_DIM)
        pad = ((0, 0), (n, 0), (0, 0), (0, 0), (0, 0))
        kr = jnp.pad(k.reshape(bsz, sd, d, N_KV, HEAD_DIM), pad)
        vr = jnp.pad(v.reshape(bsz, sd, d, N_KV, HEAD_DIM), pad)

        def blk(b):
            q_b = lax.dynamic_slice_in_dim(qg, b * qbm, qbm, axis=1)
            k_b = lax.dynamic_slice_in_dim(kr, b * qbm, qbm + n, axis=1)
            v_b = lax.dynamic_slice_in_dim(vr, b * qbm, qbm + n, axis=1)
            mq = b * qbm + jnp.arange(qbm)
            mk = b * qbm - n + jnp.arange(qbm + n)
            dm = mq[:, None] - mk[None, :]
            mask = (dm >= 0) & (dm <= n) & (mk >= 0)[None, :]
            logits = jnp.einsum('bmcgrd,bkcgd->bcgrmk', q_b.astype(jnp.float32), k_b.astype(jnp.float32)) * SCALE
            logits = logits + head_bias(table, dm * d)[None, None]
            probs, lse = masked_softmax(logits, mask)
            out = jnp.einsum('bcgrmk,bkcgd->bmcgrd', probs.astype(v_b.dtype), v_b)
            return out, jnp.transpose(lse, (0, 4, 1, 2, 3))

        o, l = lax.map(blk, jnp.arange(nb))
        outs.append(jnp.moveaxis(o, 0, 1).reshape(bsz, S, N_KV, GQA, HEAD_DIM))
        lses.append(jnp.moveaxis(l, 0, 1).reshape(bsz, S, N_KV, GQA))
    return merge_groups(outs, lses)


def dilated_sample(q3, kvbuf, table):
    bsz, T = q3.shape[:2]
    wb = kvbuf.shape[1] - T
    kb, vb = kvbuf[:, :, 0], kvbuf[:, :, 1]
    outs, lses = [], []
    for g, (win, d) in enumerate(DIL_PATTERNS):
        n = win // d
        steps = d * jnp.arange(n + 1)
        idx = wb + jnp.arange(T)[:, None] - steps[None, :]
        valid = idx >= 0
        idxc = jnp.clip(idx, 0, None)
        kg = kb[:, idxc]
        vg = vb[:, idxc]
        logits = jnp.einsum('btgrd,btkgd->btgrk', q3[:, :, g].astype(jnp.float32), kg.astype(jnp.float32)) * SCALE
        logits = logits + head_bias(table, steps)[None, None]
        probs, lse = masked_softmax(logits, valid[None, :, None, None, :])
        outs.append(jnp.einsum('btgrk,btkgd->btgrd', probs.astype(vg.dtype), vg))
        lses.append(lse)
    return merge_groups(outs, lses)


def ple_add(h, p_i, w, i):
    return h + (p_i @ w['ple_w'][i]) * jax.nn.sigmoid(h @ w['ple_gate_w'][i])


def run_prompt(x, p, w):
    h = x
    rows_c, rows_s, rows_w = [], [], []
    k_sh, v_sh = None, None
    for i in range(DEPTH):
        xn = rms_norm(h, w['norm_g'][i])
        if i < N_A:
            y, rc, rs, rw = nsa_prompt(xn, w, i)
            rows_c.append(rc)
            rows_s.append(rs)
            rows_w.append(rw)
        else:
            j = i - N_A
            q3, z = b_project(xn, w, j)
            y = b_output(dilated_prompt(q3, k_sh, v_sh, w['rel_bias']), z, w, j)
        h = ple_add(h + y, p[i], w, i)
        if i == N_A - 1:
            k_sh, v_sh = shared_kv(h, w)
    S = x.shape[1]
    b_state = jnp.stack([k_sh, v_sh], 2)[:, S - min(WIN_B, S):]
    return h, jnp.stack(rows_c), jnp.stack(rows_s), jnp.stack(rows_w), b_state


def run_sample(x, p, cache_a_cmp, cache_a_slc, state_a_win, state_b_win, page_table, w):
    h = x
    T = x.shape[1]
    rows_c, rows_s, rows_w = [], [], []
    kvbuf = None
    for i in range(DEPTH):
        xn = rms_norm(h, w['norm_g'][i])
        if i < N_A:
            y, rc, rs, rw = nsa_sample(xn, cache_a_cmp[i], cache_a_slc[i], state_a_win[i], page_table, w, i)
            rows_c.append(rc)
            rows_s.append(rs)
            rows_w.append(rw)
        else:
            j = i - N_A
            q3, z = b_project(xn, w, j)
            y = b_output(dilated_sample(q3, kvbuf, w['rel_bias']), z, w, j)
        h = ple_add(h + y, p[i], w, i)
        if i == N_A - 1:
            k_new, v_new = shared_kv(h, w)
            kvbuf = jnp.concatenate([state_b_win, jnp.stack([k_new, v_new], 2)], axis=1)
    return h, jnp.stack(rows_c), jnp.stack(rows_s), jnp.stack(rows_w), kvbuf[:, T:]


def setup_inputs(seed: int = 0) -> dict:
    key = jax.random.key(seed)
    ks = jax.random.split(key, 32)
    n_pages = PAST_LEN // PAGE_SIZE
    n_used = DEC_BATCH * n_pages
    n_phys = n_used + max(1, n_used // 4)
    wa = min(WIN_A, PAST_LEN)
    wbb = min(WIN_B, PAST_LEN)

    def nrm(k, shape, s):
        return jax.random.normal(k, shape, jnp.float32) * s

    def gain(k, shape):
        return 1.0 + 0.01 * jax.random.normal(k, shape, jnp.float32)

    kv_row = (2, N_KV, HEAD_DIM)
    return {
        'x_prompt': nrm(ks[0], (BATCH, SEQ, D_MODEL), 1.0),
        'x_sample': nrm(ks[1], (DEC_BATCH, DEC_SEQ, D_MODEL), 1.0),
        'cache_a_cmp': nrm(ks[2], (N_A, n_phys, PAGE_SIZE) + kv_row, 1.0),
        'cache_a_slc': nrm(ks[3], (N_A, n_phys, PAGE_SIZE) + kv_row, 1.0),
        'state_a_win': nrm(ks[4], (N_A, DEC_BATCH, wa) + kv_row, 1.0),
        'state_b_win': nrm(ks[5], (DEC_BATCH, wbb) + kv_row, 1.0),
        'page_table': jax.random.permutation(ks[6], n_phys)[:n_used].reshape(DEC_BATCH, n_pages).astype(jnp.int32),
        'p_prompt': nrm(ks[7], (DEPTH, BATCH, SEQ, PLE_DIM), 1.0),
        'p_sample': nrm(ks[8], (DEPTH, DEC_BATCH, DEC_SEQ, PLE_DIM), 1.0),
        'rel_bias': nrm(ks[9], (N_BUCKETS, N_HEADS), 0.2),
        'norm_g': gain(ks[10], (DEPTH, D_MODEL)),
        'a_w_in': nrm(ks[11], (N_A, D_MODEL, A_IN), D_MODEL ** -0.5),
        'a_gate_b': nrm(ks[12], (N_A, 3, N_HEADS), 0.1),
        'a_qk_norm': gain(ks[13], (N_A, 4, HEAD_DIM)),
        'a_phi_w1': nrm(ks[14], (N_A, 2, 2, CMP_STRIDE * HEAD_DIM, CMP_HIDDEN), (2 * CMP_STRIDE * HEAD_DIM) ** -0.5),
        'a_phi_b1': nrm(ks[15], (N_A, 2, CMP_HIDDEN), 0.02),
        'a_phi_w2': nrm(ks[16], (N_A, 2, CMP_HIDDEN, HEAD_DIM), CMP_HIDDEN ** -0.5),
        'a_phi_b2': nrm(ks[17], (N_A, 2, HEAD_DIM), 0.02),
        'a_w_out': nrm(ks[18], (N_A, WIDTH, D_MODEL), WIDTH ** -0.5),
        'kv_norm_g': gain(ks[19], (D_MODEL,)),
        'b_w_kv': nrm(ks[20], (D_MODEL, 2 * KV_WIDTH), D_MODEL ** -0.5),
        'b_k_norm': gain(ks[21], (HEAD_DIM,)),
        'b_w_in': nrm(ks[22], (N_B, D_MODEL, B_IN), D_MODEL ** -0.5),
        'b_q_norm': gain(ks[23], (N_B, 3, HEAD_DIM)),
        'b_w_out': nrm(ks[24], (N_B, WIDTH, D_MODEL), WIDTH ** -0.5),
        'ple_w': nrm(ks[25], (DEPTH, PLE_DIM, D_MODEL), PLE_DIM ** -0.5),
        'ple_gate_w': nrm(ks[26], (DEPTH, D_MODEL, D_MODEL), D_MODEL ** -0.5),
    }


def reference(x_prompt, x_sample, cache_a_cmp, cache_a_slc, state_a_win, state_b_win, page_table,
              p_prompt, p_sample, rel_bias, norm_g, a_w_in, a_gate_b, a_qk_norm, a_phi_w1, a_phi_b1,
              a_phi_w2, a_phi_b2, a_w_out, kv_norm_g, b_w_kv, b_k_norm, b_w_in, b_q_norm, b_w_out,
              ple_w, ple_gate_w):
    w = dict(rel_bias=rel_bias, norm_g=norm_g, a_w_in=a_w_in, a_gate_b=a_gate_b, a_qk_norm=a_qk_norm,
             a_phi_w1=a_phi_w1, a_phi_b1=a_phi_b1, a_phi_w2=a_phi_w2, a_phi_b2=a_phi_b2, a_w_out=a_w_out,
             kv_norm_g=kv_norm_g, b_w_kv=b_w_kv, b_k_norm=b_k_norm, b_w_in=b_w_in, b_q_norm=b_q_norm,
             b_w_out=b_w_out, ple_w=ple_w, ple_gate_w=ple_gate_w)
    y_prompt, pr_c, pr_s, pr_w, pr_b = run_prompt(x_prompt, p_prompt, w)
    y_sample, sm_c, sm_s, sm_w, sm_b = run_sample(x_sample, p_sample, cache_a_cmp, cache_a_slc,
                                                  state_a_win, state_b_win, page_table, w)
    return (y_prompt, y_sample, pr_c, pr_s, pr_w, pr_b, sm_c, sm_s, sm_w, sm_b)
```

```python
import math
import numpy as np
import ml_dtypes
import concourse.bass as bass
import concourse.mybir as mybir
from concourse.bass_utils import run_bass_kernel_spmd

F32 = mybir.dt.float32
BF16 = mybir.dt.bfloat16
I32 = mybir.dt.int32
AF = mybir.ActivationFunctionType
ALU = mybir.AluOpType
AX = mybir.AxisListType

D = 1024
S = 2048
NTP = 16
NT = 17
NSAMP = 16
NTOK = S + NSAMP
EPS = 1e-6
SCALE = 0.125
MASKV = -30000.0
NDS = 40
PAST = 8192
NPHYS = 2560

STAGE = 99


def tile_info(t):
    if t < NTP:
        return 128, t * 128
    return NSAMP, S


class Sch:
    def __init__(self, nc):
        self.nc = nc
        self.E = {'pe': nc.tensor, 'act': nc.scalar, 'dve': nc.vector, 'pool': nc.gpsimd, 'sp': nc.sync}
        self.sems = {}
        self.ccnt = {}
        for e in ('pe', 'act', 'dve', 'pool'):
            self.sems['c_' + e] = nc.alloc_semaphore('c_' + e)
            self.ccnt[e] = 0
        self.dcnt = [0] * NDS
        for i in range(NDS):
            self.sems['d%d' % i] = nc.alloc_semaphore('d%d' % i)
        self.di = 0
        self.lastw = {}
        self.readers = {}
        self.waited = {e: {} for e in self.E}
        self.n_inst = 0

    def _need(self, eng, reads, writes):
        toks = []
        for r in reads:
            t = self.lastw.get(r)
            if t is not None and not (t[2] == 'pe' and eng == 'pe'):
                toks.append(t)
        for w in writes:
            t = self.lastw.get(w)
            if t is not None and t[2] != eng:
                toks.append(t)
            rd = self.readers.get(w)
            if rd:
                for sem, (val, src) in rd.items():
                    if src != eng:
                        toks.append((sem, val, src))
        return toks

    def _wait(self, eng, toks):
        wd = self.waited[eng]
        best = {}
        for sem, val, src in toks:
            if wd.get(sem, 0) >= val:
                continue
            if best.get(sem, 0) < val:
                best[sem] = val
        for sem, val in best.items():
            self.E[eng].wait_ge(self.sems[sem], val)
            wd[sem] = val
            self.n_inst += 1

    def _record(self, tok, reads, writes):
        for r in reads:
            d = self.readers.setdefault(r, {})
            d[tok[0]] = (tok[1], tok[2])
        for w in writes:
            self.lastw[w] = tok
            self.readers[w] = {}

    def op(self, eng, fn, reads=(), writes=()):
        self._wait(eng, self._need(eng, reads, writes))
        inst = fn(self.E[eng])
        self.ccnt[eng] += 1
        inst.then_inc(self.sems['c_' + eng], 1)
        self._record(('c_' + eng, self.ccnt[eng], eng), reads, writes)
        self.n_inst += 1

    def dma(self, q, out, in_, reads=(), writes=(), **kw):
        i = self.di
        self.di = (self.di + 1) % NDS
        name = 'd%d' % i
        toks = self._need(q, reads, writes)
        if self.dcnt[i] > 0:
            toks.append((name, self.dcnt[i], 'dma'))
        self._wait(q, toks)
        inst = self.E[q].dma_start(out=out, in_=in_, **kw)
        self.dcnt[i] += 16
        inst.then_inc(self.sems[name], 16)
        self._record((name, self.dcnt[i], 'dma'), reads, writes)
        self.n_inst += 1

    def idma(self, out, in_, idx_ap, reads=(), writes=()):
        q = 'pool'
        i = self.di
        self.di = (self.di + 1) % NDS
        name = 'd%d' % i
        toks = self._need(q, reads, writes)
        if self.dcnt[i] > 0:
            toks.append((name, self.dcnt[i], 'dma'))
        self._wait(q, toks)
        inst = self.E[q].indirect_dma_start(out=out, out_offset=None, in_=in_,
                                            in_offset=bass.IndirectOffsetOnAxis(ap=idx_ap, axis=0))
        self.dcnt[i] += 16
        inst.then_inc(self.sems[name], 16)
        self._record((name, self.dcnt[i], 'dma'), reads, writes)
        self.n_inst += 1

    def raw_tok_wait(self, eng, res_list):
        toks = []
        for r in res_list:
            t = self.lastw.get(r)
            if t is not None:
                toks.append(t)
        self._wait(eng, toks)

    def barrier(self):
        toks = []
        for e in ('pe', 'act', 'dve', 'pool'):
            if self.ccnt[e] > 0:
                toks.append(('c_' + e, self.ccnt[e], 'x'))
        for i in range(NDS):
            if self.dcnt[i] > 0:
                toks.append(('d%d' % i, self.dcnt[i], 'dma'))
        for e in self.E:
            self._wait(e, toks)


class Ring:
    def __init__(self, items):
        self.items = items
        self.i = 0

    def next(self):
        it = self.items[self.i]
        self.i = (self.i + 1) % len(self.items)
        return it


def _bucket_table(nmax):
    import jax
    import jax.numpy as jnp
    cpu = jax.devices('cpu')[0]
    with jax.default_device(cpu):
        n = jnp.arange(nmax, dtype=jnp.int32)
        exact = 16
        nf = jnp.maximum(n, 1).astype(jnp.float32)
        big = exact + (jnp.log(nf / exact) / math.log(4096 / exact) * (32 - exact)).astype(jnp.int32)
        b = jnp.where(n < exact, n, jnp.minimum(big, 31))
        return np.asarray(b).astype(np.int64)


def host_constants():
    c = {}
    bk = _bucket_table(8448)

    def onehot(ns, valid):
        L = len(ns)
        oh = np.zeros((33, L), np.float32)
        nn = np.clip(ns, 0, len(bk) - 1)
        idx = np.where(valid, bk[nn], 32)
        oh[idx, np.arange(L)] = 1.0
        return oh

    n = np.arange(8448) - 127
    c['oh_slc'] = onehot(n, n >= 0)
    n = np.arange(768) - 127
    c['oh_win'] = onehot(n, (n >= 0) & (n <= 512))
    n = np.arange(4096) - 2063
    c['oh_cmp'] = onehot(n, n >= 0)
    for d, win, L in ((1, 128, 384), (4, 512, 768), (16, 2048, 2304)):
        n = np.arange(L) - 127
        c['oh_d%d' % d] = onehot(n, (n >= 0) & (n <= win) & (n % d == 0))
    c['identb'] = np.eye(128, dtype=np.float32).astype(ml_dtypes.bfloat16)
    c['identf'] = np.eye(128, dtype=np.float32)
    c['antib'] = np.ascontiguousarray(np.eye(128, dtype=np.float32)[::-1]).astype(ml_dtypes.bfloat16)
    t = np.arange(S)
    cur = t // 64
    j = np.arange(32)[None, :]
    valid = (j <= cur[:, None])
    forced = (j == 0) | (j == cur[:, None]) | (j == cur[:, None] - 1)
    A = valid.astype(np.float32)
    Bc = np.where(valid, np.where(forced, 1000.0, 0.0), -1.0).astype(np.float32)
    c['selA'] = np.ascontiguousarray(A.reshape(16, 128, 32).transpose(1, 0, 2))
    c['selB'] = np.ascontiguousarray(Bc.reshape(16, 128, 32).transpose(1, 0, 2))
    cs = np.arange(128) * 16
    ss = np.arange(32) * 64
    ov = np.clip(np.minimum(cs[:, None] + 32, ss[None, :] + 64) - np.maximum(cs[:, None], ss[None, :]), 0, None) / 32.0
    ov[127] = 0
    c['ov_p'] = ov.astype(np.float32).astype(ml_dtypes.bfloat16)
    es = np.zeros((32, 16, 128), np.float32)
    for a in range(16):
        es[2 * a, a, :64] = 1
        es[2 * a + 1, a, 64:] = 1
    c['esel'] = es.astype(ml_dtypes.bfloat16)
    cs = np.arange(512) * 16
    ss = np.arange(128) * 64
    ovs = np.clip(np.minimum(cs[:, None] + 32, ss[None, :] + 64) - np.maximum(cs[:, None], ss[None, :]), 0, None) / 32.0
    ovs[511] = 0
    c['ov_s'] = np.ascontiguousarray(ovs.reshape(4, 128, 128).transpose(1, 0, 2)).astype(np.float32).astype(ml_dtypes.bfloat16)
    sb_ = np.zeros((4, 136), np.float32)
    sb_[:, [0, 127, 128]] = 1000.0
    sb_[:, 129:] = -1.0
    c['selB_s'] = sb_
    e2 = np.zeros((2, 128), np.float32)
    e2[0, :64] = 1
    e2[1, 64:] = 1
    c['e2'] = e2.astype(ml_dtypes.bfloat16)
    c['pidx'] = np.arange(128, dtype=np.float32).reshape(128, 1)
    return c


class Prog:
    def __init__(self):
        self.nc = bass.Bass("TRN2", target_bir_lowering=False)
        self.sch = Sch(self.nc)
        self.din_names = []

    def din(self, name, shape, dt=F32):
        self.din_names.append(name)
        return self.nc.dram_tensor(name, list(shape), dt, kind="ExternalInput").ap()

    def dout(self, name, shape, dt=F32):
        return self.nc.dram_tensor(name, list(shape), dt, kind="ExternalOutput").ap()

    def dint(self, name, shape, dt=F32):
        return self.nc.dram_tensor(name, list(shape), dt, kind="Internal").ap()

    def sb(self, name, shape, dt=F32):
        return self.nc.alloc_sbuf_tensor(name, list(shape), dt).ap()


def AP(ap, off, dims):
    return bass.AP(ap.tensor, ap.offset + off, [list(d) for d in dims])


def build():
    pg = Prog()
    nc = pg.nc
    sch = pg.sch
    hc = host_constants()

    x_p = pg.din('x_p', [S, D])
    x_s = pg.din('x_s', [NSAMP, D])
    p_p = pg.din('p_p', [4, S, 256])
    p_s = pg.din('p_s', [4, NSAMP, 256])
    cache_cmp = pg.din('cache_cmp', [2, NPHYS * 128, 512])
    cache_slc = pg.din('cache_slc', [2, NPHYS * 128, 512])
    st_a = pg.din('st_a', [2, 4, 512, 512])
    st_b = pg.din('st_b', [4, 2048, 512])
    ptab = pg.din('ptab', [4, 64], I32)
    rel_bias = pg.din('rel_bias', [32, 16])
    norm_g = pg.din('norm_g', [4, D])
    a_w_in = pg.din('a_w_in', [2, D, 5680])
    a_gate_b = pg.din('a_gate_b', [2, 48])
    a_qk_norm = pg.din('a_qk_norm', [2, 4, 64])
    a_phi_w1 = pg.din('a_phi_w1', [2, 2, 2, 1024, 256])
    a_phi_b1 = pg.din('a_phi_b1', [2, 2, 256])
    a_phi_w2 = pg.din('a_phi_w2', [2, 2, 256, 64])
    a_phi_b2 = pg.din('a_phi_b2', [2, 2, 64])
    a_w_out = pg.din('a_w_out', [2, D, D])
    kv_norm_g = pg.din('kv_norm_g', [D])
    b_w_kv = pg.din('b_w_kv', [D, 512])
    b_k_norm = pg.din('b_k_norm', [64])
    b_w_in = pg.din('b_w_in', [2, D, 4096])
    b_q_norm = pg.din('b_q_norm', [2, 3, 64])
    b_w_out = pg.din('b_w_out', [2, D, D])
    ple_w = pg.din('ple_w', [4, 256, D])
    ple_gate_w = pg.din('ple_gate_w', [4, D, D])
    cdram = {}
    for k, v in hc.items():
        dt = BF16 if v.dtype == ml_dtypes.bfloat16 else F32
        cdram[k] = pg.din('c_' + k, v.shape, dt)

    y_p = pg.dout('y_p', [S, D])
    y_s = pg.dout('y_s', [NSAMP, D])
    o_pc = pg.dout('o_pc', [2, S, 512])
    o_ps = pg.dout('o_ps', [2, S, 512])
    o_pw = pg.dout('o_pw', [2, 512, 512])
    o_pb = pg.dout('o_pb', [S, 512])
    o_sc = pg.dout('o_sc', [2, NSAMP, 512])
    o_ss = pg.dout('o_ss', [2, NSAMP, 512])
    o_sw = pg.dout('o_sw', [2, 4, 512, 512])
    o_sb = pg.dout('o_sb', [4, 2048, 512])

    gz_d = pg.dint('gz_d', [NTOK, 3072])
    fd = {}
    for k in ('slc', 'win', 'cmp', 'd1', 'd4', 'd16'):
        fd[k] = pg.dint('fd_' + k, [16, hc['oh_' + k].shape[1]], BF16)

    def psum(name, dt=F32, n=512):
        return nc.alloc_psum_tensor(name, [128, n], dt).ap()
    G = Ring([(psum('G0'), 'G0'), (psum('G1'), 'G1')])
    SS = Ring([(psum('S0'), 'S0'), (psum('S1'), 'S1')])
    OO = Ring([(psum('O0'), 'O0'), (psum('O1'), 'O1')])
    SS4 = Ring(SS.items + G.items)
    TB = (psum('TB', BF16, 1024), 'TB')
    TF = (psum('TF'), 'TF')

    identb = pg.sb('identb', [128, 128], BF16)
    antib = pg.sb('antib', [128, 128], BF16)
    tab33 = pg.sb('tab33', [33, 16], F32)
    xT = pg.sb('xT', [128, 8, NTOK], BF16)
    qT = pg.sb('qT', [128, 8, NTOK], BF16)
    ksT = pg.sb('ksT', [128, 2, S], BF16)
    kwT = pg.sb('kwT', [128, 2, S], BF16)
    vs_aug = pg.sb('vs_aug', [128, 16, 4, 65], BF16)
    vw_aug = pg.sb('vw_aug', [128, 16, 4, 65], BF16)
    gates = pg.sb('gates', [128, NT, 48], F32)
    wbf_i = [0]
    hin_i = [0]
    small = pg.sb('small', [128, 64], F32)
    xnb = pg.sb('xnb', [128, D], BF16)
    sq = pg.sb('sq', [128, 512], F32)
    tmpf = pg.sb('tmpf', [128, 512], F32)
    qnb = pg.sb('qnb', [128, 512], BF16)
    rowb = [pg.sb('rowb%d' % i, [128, 512], F32) for i in range(2)]
    rowb_i = [0]
    gq = pg.sb('gq', [128, 4, 64], F32)
    gbq = pg.sb('gbq', [128, 3, 64], F32)
    gateb = pg.sb('gateb', [128, 48], F32)
    kccT = pg.sb('kccT', [128, 2, 128], BF16)
    vcc_aug = pg.sb('vcc_aug', [128, 4, 97], BF16)
    w2s = pg.sb('w2s', [128, 2, 64], BF16)
    b1s = pg.sb('b1s', [128, 2], F32)
    b2s = pg.sb('b2s', [128, 64], F32)
    hidT = pg.sb('hidT', [128, 2, 128], BF16)
    ovp = pg.sb('ovp', [128, 32], BF16)
    Pt_i = [0]
    yacc = pg.sb('yacc', [128, 256], F32)
    ybf = pg.sb('ybf', [128, 256], BF16)
    osb = pg.sb('osb', [128, 4, 100], F32)
    rden = pg.sb('rden', [128, 4], F32)
    sc1 = pg.sb('sc1', [128, 32], F32)
    sc2 = pg.sb('sc2', [128, 32], F32)
    mx8 = pg.sb('mx8', [128, 16], F32)
    mnegb = pg.sb('mnegb', [128, 32], BF16)
    mnegT = pg.sb('mnegT', [32, 128], BF16)
    h2_i = [0]

    from contextlib import ExitStack
    LT = {}
    _uid = [0]

    class Scope:
        def __init__(self, spec, name=None):
            self.spec = spec
            self.es = ExitStack()
            self.name = name

        def __enter__(self):
            if self.name:
                self.es.enter_context(nc.named_scope(self.name))
            for name, shape, dt in self.spec:
                _uid[0] += 1
                h = self.es.enter_context(nc.sbuf_tensor('%s_u%d' % (name, _uid[0]), list(shape), dt))
                LT[name] = h.ap() if hasattr(h, 'ap') and callable(getattr(h, 'ap')) else h
            return self

        def __exit__(self, *a):
            sch.barrier()
            self.es.close()
            return False

    SC_NORM = [('gt', [128, D], F32), ('junk', [128, D], F32), ('hin0', [128, D], F32), ('hin1', [128, D], F32)]
    SC_PROJ = [('wst', [128, 8, 512], F32), ('wbf', [128, 2, 8, 512], BF16),
               ('kcT', [128, 2, S], BF16), ('vcT', [128, 2, S], BF16)]
    SC_ATT = [('G1_0', [128, 4, 2048], BF16), ('G1_1', [128, 4, 2048], BF16), ('G2_0', [128, 4, 640], BF16),
              ('G2_1', [128, 4, 640], BF16), ('G3_0', [128, 4, 128], BF16), ('G3_1', [128, 4, 128], BF16),
              ('gzt0', [128, 3, 256], F32), ('gzt1', [128, 3, 256], F32), ('selA', [128, 16, 32], F32), ('selB', [128, 16, 32], F32),
              ('esel', [32, 16, 128], BF16), ('Pt0', [128, 512], BF16), ('Pt1', [128, 512], BF16), ('Pt2', [128, 512], BF16), ('Pt3', [128, 512], BF16)]
    SC_OUT = [('wst', [128, 8, 512], F32), ('wo_bf', [128, 8, D], BF16), ('wg_bf', [128, 8, D], BF16), ('wp_bf', [128, 2, D], BF16),
              ('h1', [128, D], F32), ('h1b', [128, D], BF16), ('h1T', [128, 8, 128], BF16), ('sg', [128, D], F32),
              ('h2_0', [128, D], F32), ('hin0', [128, D], F32),
              ('pin', [128, 256], F32), ('pinb', [128, 256], BF16), ('pT', [128, 2, 128], BF16)]

    sch.dma('sp', identb[:], cdram['identb'][:], writes=['identb'])
    sch.dma('sp', antib[:], cdram['antib'][:], writes=['identb'])
    sch.dma('sp', ovp[:], cdram['ov_p'][:], writes=['ovp'])
    sch.op('dve', lambda e: e.memset(tab33[:], MASKV), writes=['tab33'])
    sch.dma('sp', tab33[0:32, :], rel_bias[:], writes=['tab33'])
    with Scope([('ohs', [33, 512], F32), ('fstage', [16, 512], BF16)]):
        ohs, fstage = LT['ohs'], LT['fstage']
        for k in ('slc', 'win', 'cmp', 'd1', 'd4', 'd16'):
            L = hc['oh_' + k].shape[1]
            for c0 in range(0, L, 512):
                cw = min(512, L - c0)
                sch.dma('sp', ohs[:, :cw], cdram['oh_' + k][:, c0:c0 + cw], writes=['ohs'])
                g, gr = G.next()
                sch.op('pe', lambda e: e.matmul(g[0:16, :cw], lhsT=tab33[:, :], rhs=ohs[:, :cw], start=True, stop=True),
                       reads=['tab33', 'ohs'], writes=[gr])
                sch.op('act', lambda e: e.copy(out=fstage[:, :cw], in_=g[0:16, :cw]), reads=[gr], writes=['fstage'])
                sch.dma('sp', fd[k][:, c0:c0 + cw], fstage[:, :cw], reads=['fstage'], writes=['fd_' + k])

    def bc_row(ap1, P=128):
        n = ap1.shape[-1]
        return AP(ap1, 0, [[0, P], [1, n]])

    def load_w(W2, c0, cw, nk=8):
        wst = LT['wst']
        wbf = [LT['wbf'][:, 0], LT['wbf'][:, 1]]
        N = W2.shape[1]
        src = AP(W2, c0, [[N, 128], [128 * N, nk], [1, cw]])
        sch.dma('sp', wst[:, :nk, :cw], src, writes=['wst'])
        i = wbf_i[0]
        wbf_i[0] ^= 1
        sch.op('pool', lambda e: e.tensor_copy(out=wbf[i][:, :nk, :cw], in_=wst[:, :nk, :cw]),
               reads=['wst'], writes=['wbf%d' % i])
        return wbf[i], 'wbf%d' % i

    def gemm(lhsT_of, lres, W2, chunks, tiles, consumer, nk=8):
        for ci, (c0, cw) in enumerate(chunks):
            wb, wres = load_w(W2, c0, cw, nk)
            for t in tiles:
                P, tok0 = tile_info(t)
                g, gr = G.next()
                for kc in range(nk):
                    sch.op('pe', lambda e: e.matmul(g[:P, :cw], lhsT=lhsT_of(kc, t), rhs=wb[:, kc, :cw],
                                                    start=(kc == 0), stop=(kc == nk - 1)),
                           reads=[lres, wres], writes=[gr])
                consumer(ci, t, P, tok0, g, gr)

    def rstd_from_ss(P, n, ss_ap, out_ap, inv):
        sch.op('dve', lambda e: e.tensor_scalar(out=ss_ap, in0=ss_ap, scalar1=inv, scalar2=EPS,
                                                op0=ALU.mult, op1=ALU.add), reads=['small'], writes=['small'])
        sch.op('act', lambda e: e.sqrt(out=ss_ap, in_=ss_ap), reads=['small'], writes=['small'])
        sch.op('dve', lambda e: e.reciprocal(out=out_ap, in_=ss_ap), reads=['small'], writes=['small'])

    def headnorm(P, src3, nh, gain3, out3, out_res, src_res, extra_reads=()):
        sq3 = sq[:P, :nh * 64].rearrange("p (h d) -> p h d", d=64)
        sch.op('act', lambda e: e.activation(out=sq3, in_=src3, func=AF.Square), reads=[src_res], writes=['sq'])
        sch.op('dve', lambda e: e.tensor_reduce(out=small[:P, 0:nh], in_=sq3, axis=AX.X, op=ALU.add),
               reads=['sq'], writes=['small'])
        rstd_from_ss(P, nh, small[:P, 0:nh], small[:P, 16:16 + nh], 1.0 / 64)
        t3 = tmpf[:P, :nh * 64].rearrange("p (h d) -> p h d", d=64)
        sch.op('dve', lambda e: e.tensor_tensor(out=t3, in0=src3, in1=small[:P, 16:16 + nh].to_broadcast((P, nh, 64)) if False else AP(small, 16, [[64, P], [1, nh], [0, 64]]),
                                                op=ALU.mult), reads=[src_res, 'small'], writes=['tmpf'])
        sch.op('dve', lambda e: e.tensor_tensor(out=out3, in0=t3, in1=gain3, op=ALU.mult),
               reads=['tmpf'] + list(extra_reads), writes=[out_res])

    def transposes_to(P, src2, src_res, nblk, dst3, dst_res):
        tb, tbr = TB
        for b in range(nblk):
            sch.op('pe', lambda e: e.transpose(out=tb[:, b * 128:b * 128 + P], in_=src2[:, b * 128:(b + 1) * 128],
                                               identity=identb[:P, :P]),
                   reads=[src_res, 'identb'], writes=[tbr])
        tb3 = AP(tb, 0, [[1024, 128], [128, nblk], [1, P]])
        sch.op('act', lambda e: e.copy(out=dst3, in_=tb3), reads=[tbr], writes=[dst_res])

    def phase_norm(src_of, gain_ap):
        gt = LT['gt']
        junk = LT['junk']
        hin = [LT['hin0'], LT['hin1']]
        sch.dma('sp', gt[:], bc_row(gain_ap), writes=['gt'])
        for t in range(NT):
            P, tok0 = tile_info(t)
            i = hin_i[0]
            hin_i[0] ^= 1
            hr = 'hin%d' % i
            sch.dma('sp', hin[i][:P], src_of(t), reads=['hd%d' % t], writes=[hr])
            sch.op('act', lambda e: e.activation(out=junk[:P], in_=hin[i][:P], func=AF.Square,
                                                 accum_out=small[:P, 0:1]), reads=[hr], writes=['junk', 'small'])
            rstd_from_ss(P, 1, small[:P, 0:1], small[:P, 1:2], 1.0 / D)
            sch.op('dve', lambda e: e.scalar_tensor_tensor(out=xnb[:P], in0=hin[i][:P], scalar=small[:P, 1:2],
                                                           in1=gt[:P], op0=ALU.mult, op1=ALU.mult),
                   reads=[hr, 'small', 'gt'], writes=['xnb'])
            transposes_to(P, xnb[:P], 'xnb', 8, xT[:, :, tok0:tok0 + P], 'xT')

    def h_src(layer):
        def f(t):
            P, tok0 = tile_info(t)
            if layer == 0:
                return x_p[tok0:tok0 + P, :] if t < NTP else x_s[:, :]
            return y_p[tok0:tok0 + P, :] if t < NTP else y_s[:, :]
        return f

    def a_project(li):
        W = a_w_in[li]
        kcT = LT['kcT']
        vcT = LT['vcT']
        sch.dma('sp', gq[:].rearrange("p a d -> p (a d)"), bc_row(a_qk_norm[li].rearrange("a d -> (a d)")), writes=['gq'])
        sch.op('dve', lambda e: e.tensor_scalar(out=gq[:, 0, :], in0=gq[:, 0, :], scalar1=SCALE, scalar2=None,
                                                op0=ALU.mult), reads=['gq'], writes=['gq'])
        sch.dma('sp', gateb[:], bc_row(a_gate_b[li]), writes=['gateb'])
        chunks = [(0, 512), (512, 512), (1024, 512), (1536, 512), (2048, 512), (2560, 48)] + \
                 [(2608 + 512 * z, 512) for z in range(6)]

        def consumer(ci, t, P, tok0, g, gr):
            if ci < 2:
                src3 = g[:P, :512].rearrange("p (h d) -> p h d", d=64)
                gain3 = AP(gq, 0, [[256, P], [0, 8], [1, 64]])
                out4 = AP(qnb, 0, [[512, P], [64, 2], [128, 4], [1, 64]])
                sq3 = sq[:P, :512].rearrange("p (h d) -> p h d", d=64)
                sch.op('act', lambda e: e.activation(out=sq3, in_=src3, func=AF.Square), reads=[gr], writes=['sq'])
                sch.op('dve', lambda e: e.tensor_reduce(out=small[:P, 0:8], in_=sq3, axis=AX.X, op=ALU.add),
                       reads=['sq'], writes=['small'])
                rstd_from_ss(P, 8, small[:P, 0:8], small[:P, 16:24], 1.0 / 64)
                t3 = tmpf[:P, :512].rearrange("p (h d) -> p h d", d=64)
                sch.op('dve', lambda e: e.tensor_tensor(out=t3, in0=src3, in1=AP(small, 16, [[64, P], [1, 8], [0, 64]]),
                                                        op=ALU.mult), reads=[gr, 'small'], writes=['tmpf'])
                t4 = tmpf[:P, :512].rearrange("p (a r d) -> p a r d", a=2, r=4)
                g4 = AP(gq, 0, [[256, P], [0, 2], [0, 4], [1, 64]])
                sch.op('dve', lambda e: e.tensor_tensor(out=out4, in0=t4, in1=g4, op=ALU.mult),
                       reads=['tmpf', 'gq'], writes=['qnb'])
                transposes_to(P, qnb[:P], 'qnb', 4, qT[:, 4 * ci:4 * ci + 4, tok0:tok0 + P], 'qT')
            elif ci < 5:
                kind = ci - 2
                ri = rowb_i[0]
                rowb_i[0] ^= 1
                rb = rowb[ri]
                rr = 'rowb%d' % ri
                if kind == 0:
                    sch.op('act', lambda e: e.copy(out=rb[:P, :], in_=g[:P, :512]), reads=[gr], writes=[rr])
                else:
                    src3 = g[:P, 0:256].rearrange("p (h d) -> p h d", d=64)
                    gain3 = AP(gq, (1 + kind) * 64, [[256, P], [0, 4], [1, 64]])
                    out3 = rb[:P, 0:256].rearrange("p (h d) -> p h d", d=64)
                    headnorm(P, src3, 4, gain3, out3, rr, gr, extra_reads=['gq'])
                    sch.op('act', lambda e: e.copy(out=rb[:P, 256:512], in_=g[:P, 256:512]), reads=[gr], writes=[rr])
                if t < NTP:
                    if kind == 0:
                        sch.dma('pool', o_pc[li, tok0:tok0 + P, :], rb[:P, :], reads=[rr], writes=['o_pc'])
                    elif kind == 1:
                        sch.dma('pool', o_ps[li, tok0:tok0 + P, :], rb[:P, :], reads=[rr], writes=['o_ps'])
                    elif t >= 12:
                        sch.dma('pool', o_pw[li, tok0 - 1536:tok0 - 1536 + P, :], rb[:P, :], reads=[rr], writes=['o_pw'])
                else:
                    if kind == 0:
                        sch.dma('pool', o_sc[li, :, :], rb[:P, :], reads=[rr], writes=['o_sc'])
                    elif kind == 1:
                        sch.dma('pool', o_ss[li, :, :], rb[:P, :], reads=[rr], writes=['o_ss'])
                    else:
                        for b in range(4):
                            sch.dma('pool', o_sw[li, b, 508:512, :], rb[4 * b:4 * b + 4, :], reads=[rr], writes=['o_sw'])
                if t < NTP:
                    sch.op('dve', lambda e: e.tensor_copy(out=qnb[:P, :], in_=rb[:P, :]), reads=[rr], writes=['qnb'])
                    if kind == 0:
                        transposes_to(P, qnb[:P, 0:256], 'qnb', 2, kcT[:, :, tok0:tok0 + P], 'kcT')
                        transposes_to(P, qnb[:P, 256:512], 'qnb', 2, vcT[:, :, tok0:tok0 + P], 'vcT')
                    else:
                        kt, ktr, va, var = (ksT, 'ksT', vs_aug, 'vs_aug') if kind == 1 else (kwT, 'kwT', vw_aug, 'vw_aug')
                        transposes_to(P, qnb[:P, 0:256], 'qnb', 2, kt[:, :, tok0:tok0 + P], ktr)
                        sch.op('pool', lambda e: e.tensor_copy(out=va[:, t, :, 0:64],
                                                                in_=qnb[:, 256:512].rearrange("p (g d) -> p g d", d=64)),
                               reads=['qnb'], writes=[var])
            elif ci == 5:
                sch.op('dve', lambda e: e.tensor_tensor(out=gates[:P, t, :], in0=g[:P, :48], in1=gateb[:P, :], op=ALU.add),
                       reads=[gr, 'gateb'], writes=['gates'])
                sch.op('act', lambda e: e.activation(out=gates[:P, t, :], in_=gates[:P, t, :], func=AF.Sigmoid),
                       reads=['gates'], writes=['gates'])
            else:
                zc = ci - 6
                br, hh = zc // 2, zc % 2
                ri = rowb_i[0]
                rowb_i[0] ^= 1
                rb = rowb[ri]
                rr = 'rowb%d' % ri
                sch.op('act', lambda e: e.activation(out=tmpf[:P, :], in_=g[:P, :512], func=AF.Silu), reads=[gr], writes=['tmpf'])
                gb = AP(gates, t * 48 + br * 16 + hh * 8, [[NT * 48, P], [1, 8], [0, 64]])
                sch.op('dve', lambda e: e.tensor_tensor(out=rb[:P, :].rearrange("p (h d) -> p h d", d=64),
                                                        in0=tmpf[:P, :].rearrange("p (h d) -> p h d", d=64), in1=gb, op=ALU.mult),
                       reads=['tmpf', 'gates'], writes=[rr])
                sch.dma('pool', gz_d[tok0:tok0 + P, zc * 512:(zc + 1) * 512], rb[:P, :], reads=[rr], writes=['gz_d'])

        gemm(lambda kc, t: xT[:, kc, tile_info(t)[1]:tile_info(t)[1] + tile_info(t)[0]], 'xT', W, chunks, list(range(NT)), consumer)

    def compress_prompt(li):
        wst = LT['wst']
        w1s = LT['wbf'].rearrange("p a k (b h) -> p (a k b) h", h=256)
        kcT = LT['kcT']
        vcT = LT['vcT']
        sch.op('dve', lambda e: e.memset(kccT[:], 0.0), writes=['kccT'])
        sch.op('dve', lambda e: e.memset(vcc_aug[:], 0.0), writes=['vcc_aug'])
        for kv in range(2):
            srcT, sres = (kcT, 'kcT') if kv == 0 else (vcT, 'vcT')
            for half in range(2):
                for ab in range(2):
                    src = AP(a_phi_w1[li, kv, ab], 0, [[256, 64], [64 * 256, 16], [1, 256]])
                    sch.dma('sp', wst[half * 64:half * 64 + 64, 0:8, :].rearrange("p a (b h) -> p (a b) h", h=256), src, writes=['wst'])
                    sch.op('pool', lambda e: e.tensor_copy(out=w1s[half * 64:half * 64 + 64, ab * 16:(ab + 1) * 16, :],
                                                           in_=wst[half * 64:half * 64 + 64, 0:8, :].rearrange("p a (b h) -> p (a b) h", h=256)),
                           reads=['wst'], writes=['wbf0', 'wbf1'])
            sch.dma('sp', wst[:, 0, 0:128].rearrange("p (a d) -> p a d", d=64),
                    AP(a_phi_w2[li, kv], 0, [[64, 128], [128 * 64, 2], [1, 64]]), writes=['wst'])
            sch.op('pool', lambda e: e.tensor_copy(out=w2s[:], in_=wst[:, 0, 0:128].rearrange("p (a d) -> p a d", d=64)),
                   reads=['wst'], writes=['w2s'])
            for hh_ in range(2):
                sch.dma('sp', b1s[:, hh_:hh_ + 1], AP(a_phi_b1[li, kv], hh_ * 128, [[1, 128], [1, 1]]), writes=['b1s'])
            sch.dma('sp', b2s[:], bc_row(a_phi_b2[li, kv]), writes=['b2s'])
            for g in range(4):
                half, ch = g % 2, g // 2
                hs = slice(half * 64, half * 64 + 64)
                for hh in range(2):
                    ps, psr = SS.next()
                    n = 0
                    for ab in range(2):
                        for s in range(16):
                            rhs = AP(srcT, half * 64 * (2 * S) + ch * S + 16 * ab + s, [[2 * S, 64], [16, 127]])
                            sch.op('pe', lambda e: e.matmul(ps[:, 0:127], lhsT=w1s[hs, ab * 16 + s, hh * 128:(hh + 1) * 128],
                                                            rhs=rhs, start=(n == 0), stop=(n == 31)),
                                   reads=['wbf0', 'wbf1', sres], writes=[psr])
                            n += 1
                    sch.op('act', lambda e: e.activation(out=hidT[:, hh, 0:127], in_=ps[:, 0:127], func=AF.Silu,
                                                         bias=b1s[:, hh:hh + 1]), reads=[psr, 'b1s'], writes=['hidT'])
                po, por = OO.next()
                for hh in range(2):
                    sch.op('pe', lambda e: e.matmul(po[0:127, 0:64], lhsT=hidT[:, hh, 0:127], rhs=w2s[:, hh, :],
                                                    start=(hh == 0), stop=(hh == 1)), reads=['hidT', 'w2s'], writes=[por])
                sch.op('dve', lambda e: e.tensor_tensor(out=tmpf[0:127, 0:64], in0=po[0:127, 0:64], in1=b2s[0:127, :], op=ALU.add),
                       reads=[por, 'b2s'], writes=['tmpf'])
                if kv == 0:
                    src3 = tmpf[0:127, 0:64].rearrange("p (h d) -> p h d", d=64)
                    gain3 = AP(gq, 64, [[256, 127], [0, 1], [1, 64]])
                    sq3 = sq[:127, :64].rearrange("p (h d) -> p h d", d=64)
                    sch.op('act', lambda e: e.activation(out=sq3, in_=src3, func=AF.Square), reads=['tmpf'], writes=['sq'])
                    sch.op('dve', lambda e: e.tensor_reduce(out=small[:127, 0:1], in_=sq3, axis=AX.X, op=ALU.add),
                           reads=['sq'], writes=['small'])
                    rstd_from_ss(127, 1, small[:127, 0:1], small[:127, 16:17], 1.0 / 64)
                    if half == 0:
                        sch.op('dve', lambda e: e.memset(qnb[:, 0:128], 0.0), writes=['qnb'])
                    sch.op('dve', lambda e: e.scalar_tensor_tensor(out=qnb[0:127, half * 64:half * 64 + 64], in0=tmpf[0:127, 0:64],
                                                                   scalar=small[0:127, 16:17], in1=AP(gq, 64, [[256, 127], [1, 64]]),
                                                                   op0=ALU.mult, op1=ALU.mult),
                           reads=['tmpf', 'small', 'gq'], writes=['qnb'])
                    if half == 1:
                        transposes_to(128, qnb[:, 0:128], 'qnb', 1, kccT[:, ch:ch + 1, :], 'kccT')
                else:
                    sch.op('act', lambda e: e.copy(out=vcc_aug[0:127, g, 0:64], in_=tmpf[0:127, 0:64]), reads=['tmpf'], writes=['vcc_aug'])
        for g in range(4):
            sch.op('dve', lambda e: e.memset(vcc_aug[0:127, g, 64:65], 1.0), writes=['vcc_aug'])
            sch.op('pool', lambda e: e.tensor_copy(out=vcc_aug[:, g, 65:97], in_=ovp[:, :]), reads=['ovp'], writes=['vcc_aug'])

    pend = [None]
    cur_SS = [SS]

    def flush_pv():
        p = pend[0]
        pend[0] = None
        if p is not None:
            p[0]()
            if p[1] is not None:
                p[1]()

    def att_tile(nk, P, kT_ap, kres, qT_rhs, G_rhs, gres, extra, v_rhs, vres, o, ores, first, last, vw, score=None, after=None):
        N = 4 * P
        Pt = [LT['Pt0'], LT['Pt1'], LT['Pt2'], LT['Pt3']]
        s, sr = cur_SS[0].next()
        nmm = 2 + len(extra)
        sch.op('pe', lambda e: e.matmul(s[:nk, :N], lhsT=kT_ap, rhs=qT_rhs, start=True, stop=False),
               reads=[kres, 'qT'], writes=[sr])
        sch.op('pe', lambda e: e.matmul(s[:nk, :N], lhsT=antib[0:nk, 128 - nk:128], rhs=G_rhs, start=False, stop=(nmm == 2)),
               reads=['identb', gres], writes=[sr])
        for xi, xt in enumerate(extra):
            if len(xt) == 3:
                xl, xr, xres = xt
                sout = s[:nk, :N]
            else:
                hf_, xl, xr, xres = xt
                sout = s[64 * hf_:64 * hf_ + 64, :N]
            sch.op('pe', lambda e: e.matmul(sout, lhsT=xl, rhs=xr, start=False, stop=(xi == len(extra) - 1)),
                   reads=xres, writes=[sr])
        pi = Pt_i[0]
        Pt_i[0] = (pi + 1) % 4
        pt = Pt[pi]
        pr = 'Pt%d' % pi
        sch.op('act', lambda e: e.activation(out=pt[:nk, :N], in_=s[:nk, :N], func=AF.Exp), reads=[sr], writes=[pr])

        def pv():
            for r in range(4):
                sch.op('pe', lambda e: e.matmul(o[:P, r * 128:r * 128 + vw], lhsT=pt[:nk, r * P:(r + 1) * P], rhs=v_rhs,
                                                start=(first and r == 0), stop=(last and r == 3)), reads=[pr, vres], writes=[ores])
            if score is not None:
                srhs, sres2, o2, o2res = score
                for r in range(4):
                    sch.op('pe', lambda e: e.matmul(o2[:P, r * 128:(r + 1) * 128], lhsT=pt[:nk, r * P:(r + 1) * P], rhs=srhs,
                                                    start=(first and r == 0), stop=(last and r == 3)), reads=[pr, sres2], writes=[o2res])
        flush_pv()
        pend[0] = (pv, after)

    def toeplitz_load(dst, dres, fdk, base_off, pstep, nh_off, L, width):
        src = AP(fd[fdk], nh_off * L + base_off, [[pstep, 128], [L, 4], [1, width]])
        sch.dma('sp', dst, src, reads=['fd_' + fdk], writes=[dres])

    def a_attention_prompt(li):
        cur_SS[0] = SS4
        G1 = [LT['G1_0'], LT['G1_1']]
        G2 = [LT['G2_0'], LT['G2_1']]
        G3 = [LT['G3_0'], LT['G3_1']]
        gzt = [LT['gzt0'], LT['gzt1']]
        selA, selB, esel = LT['selA'], LT['selB'], LT['esel']
        sch.dma('sp', selA[:], cdram['selA'][:], writes=['selA'])
        sch.dma('sp', selB[:], cdram['selB'][:], writes=['selB'])
        sch.dma('sp', esel[:], cdram['esel'][:], writes=['esel'])
        Lslc = hc['oh_slc'].shape[1]
        for g in range(4):
            half, ch = g % 2, g // 2
            hs = slice(half * 64, half * 64 + 64)
            gi = g % 2
            toeplitz_load(G1[gi][:], 'G1_%d' % gi, 'slc', 0, 1, 4 * g, Lslc, 2048)
            toeplitz_load(G2[gi][:], 'G2_%d' % gi, 'win', 0, 1, 4 * g, 768, 640)
            for b in range(NTP):
                P, tok0 = 128, b * 128
                qrhs = qT[hs, 4 * ch:4 * ch + 4, tok0:tok0 + P]
                bi = b % 2
                toeplitz_load(G3[bi][:], 'G3_%d' % bi, 'cmp', tok0, 16, 4 * g, 4096, 128)
                sch.dma('sp', gzt[bi][:], AP(gz_d, tok0 * 3072 + g * 256, [[3072, 128], [1024, 3], [1, 256]]),
                        reads=['gz_d'], writes=['gzt%d' % bi])
                o, ores = OO.next()

                def after_cmp(o=o, ores=ores, bi=bi, b=b):
                    o3 = AP(o, 0, [[512, 128], [128, 4], [1, 97]])
                    sch.op('dve', lambda e: e.tensor_scalar(out=rden[:, :], in0=AP(o, 64, [[512, 128], [128, 4]]), scalar1=1e-30,
                                                            scalar2=None, op0=ALU.max), reads=[ores], writes=['rden'])
                    sch.op('dve', lambda e: e.reciprocal(out=rden[:, :], in_=rden[:, :]), reads=['rden'], writes=['rden'])
                    sch.op('dve', lambda e: e.tensor_tensor(out=osb[:, :, 0:97], in0=o3, in1=AP(rden, 0, [[4, 128], [1, 4], [0, 97]]),
                                                            op=ALU.mult), reads=[ores, 'rden'], writes=['osb'])
                    sch.op('dve', lambda e: e.tensor_tensor(out=yacc[:, :].rearrange("p (r d) -> p r d", d=64), in0=osb[:, :, 0:64],
                                                            in1=gzt[bi][:, 0, :].rearrange("p (r d) -> p r d", d=64), op=ALU.mult),
                           reads=['osb', 'gzt%d' % bi], writes=['yacc'])
                    sch.op('dve', lambda e: e.tensor_reduce(out=sc1[:, :], in_=AP(osb, 65, [[400, 128], [1, 32], [100, 4]]),
                                                            axis=AX.X, op=ALU.add), reads=['osb'], writes=['sc1'])
                    sch.op('dve', lambda e: e.tensor_tensor(out=sc1[:, :], in0=sc1[:, :], in1=selA[:, b, :], op=ALU.mult),
                           reads=['sc1', 'selA'], writes=['sc1'])
                    sch.op('dve', lambda e: e.tensor_tensor(out=sc1[:, :], in0=sc1[:, :], in1=selB[:, b, :], op=ALU.add),
                           reads=['sc1', 'selB'], writes=['sc1'])
                    sch.op('dve', lambda e: e.max(out=mx8[:, 0:8], in_=sc1[:, :]), reads=['sc1'], writes=['mx8'])
                    sch.op('dve', lambda e: e.match_replace(out=sc2[:, :], in_to_replace=mx8[:, 0:8], in_values=sc1[:, :], imm_value=-2.0),
                           reads=['sc1', 'mx8'], writes=['sc2'])
                    sch.op('dve', lambda e: e.max(out=mx8[:, 8:16], in_=sc2[:, :]), reads=['sc2'], writes=['mx8'])
                    sch.op('dve', lambda e: e.tensor_scalar(out=sc2[:, :], in0=sc1[:, :], scalar1=mx8[:, 15:16], scalar2=None,
                                                            op0=ALU.is_ge), reads=['sc1', 'mx8'], writes=['sc2'])
                    sch.op('dve', lambda e: e.tensor_scalar(out=mnegb[:, :], in0=sc2[:, :], scalar1=-1.0, scalar2=-MASKV,
                                                            op0=ALU.add, op1=ALU.mult), reads=['sc2'], writes=['mnegb'])
                    tb, tbr = TB
                    sch.op('pe', lambda e: e.transpose(out=tb[0:32, 0:128], in_=mnegb[:, :], identity=identb[:, :]),
                           reads=['mnegb', 'identb'], writes=[tbr])
                    sch.op('act', lambda e: e.copy(out=mnegT[:, :], in_=tb[0:32, 0:128]), reads=[tbr], writes=['mnegT'])
                att_tile(128, P, kccT[hs, ch, :], 'kccT', qrhs, G3[bi][:, :, :], 'G3_%d' % bi, [],
                         vcc_aug[:, g, :], 'vcc_aug', o, ores, True, True, 97, after=after_cmp)
                o, ores = OO.next()
                a0 = max(0, b - 4)

                def after_win(o=o, ores=ores, bi=bi):
                    branch_out(o, ores, gzt[bi][:, 2, :], 'gzt%d' % bi, False)
                for a in range(a0, b + 1):
                    att_tile(128, P, kwT[hs, ch, a * 128:(a + 1) * 128], 'kwT', qrhs,
                             G2[gi][:, :, 128 * (b - a):128 * (b - a) + 128], 'G2_%d' % gi, [],
                             vw_aug[:, a, g, :], 'vw_aug', o, ores, a == a0, a == b, 65, after=(after_win if a == b else None))
                o, ores = OO.next()
                mrhs = AP(mnegT, 0, [[128, 32], [0, 4], [1, 128]])

                def after_slc(o=o, ores=ores, bi=bi, g=g, tok0=tok0, P=P):
                    branch_out(o, ores, gzt[bi][:, 1, :], 'gzt%d' % bi, False)
                    sch.op('act', lambda e: e.copy(out=ybf[:, :], in_=yacc[:, :]), reads=['yacc'], writes=['ybf'])
                    transposes_to(128, ybf[:, :], 'ybf', 2, xT[:, 2 * g:2 * g + 2, tok0:tok0 + P], 'xT')
                for a in range(b + 1):
                    att_tile(128, P, ksT[hs, ch, a * 128:(a + 1) * 128], 'ksT', qrhs,
                             G1[gi][:, :, 128 * (b - a):128 * (b - a) + 128], 'G1_%d' % gi,
                             [(esel[:, a, :], mrhs, ['esel', 'mnegT'])],
                             vs_aug[:, a, g, :], 'vs_aug', o, ores, a == 0, a == b, 65, after=(after_slc if a == b else None))
        flush_pv()
        cur_SS[0] = SS

    def branch_out(o, ores, gz2, gzres, first, P=128):
        o3 = AP(o, 0, [[512, P], [128, 4], [1, 64]])
        sch.op('dve', lambda e: e.tensor_scalar(out=rden[:P, :], in0=AP(o, 64, [[512, P], [128, 4]]), scalar1=1e-30,
                                                scalar2=None, op0=ALU.max), reads=[ores], writes=['rden'])
        sch.op('dve', lambda e: e.reciprocal(out=rden[:P, :], in_=rden[:P, :]), reads=['rden'], writes=['rden'])
        sch.op('dve', lambda e: e.tensor_tensor(out=osb[:P, :, 0:64], in0=o3, in1=AP(rden, 0, [[4, P], [1, 4], [0, 64]]),
                                                op=ALU.mult), reads=[ores, 'rden'], writes=['osb'])
        sch.op('dve', lambda e: e.tensor_tensor(out=osb[:P, :, 0:64], in0=osb[:P, :, 0:64],
                                                in1=gz2.rearrange("p (r d) -> p r d", d=64), op=ALU.mult),
               reads=['osb', gzres], writes=['osb'])
        y3 = yacc[:P, :].rearrange("p (r d) -> p r d", d=64)
        if first:
            sch.op('dve', lambda e: e.tensor_copy(out=y3, in_=osb[:P, :, 0:64]), reads=['osb'], writes=['yacc'])
        else:
            sch.op('dve', lambda e: e.tensor_tensor(out=y3, in0=y3, in1=osb[:P, :, 0:64], op=ALU.add),
                   reads=['osb', 'yacc'], writes=['yacc'])

    def load_w_full(dst, dres, W2, nk):
        wst = LT['wst']
        N = W2.shape[1]
        for c0 in range(0, N, 512):
            src = AP(W2, c0, [[N, 128], [128 * N, nk], [1, 512]])
            sch.dma('sp', wst[:, :nk, :], src, writes=['wst'])
            sch.op('pool', lambda e: e.tensor_copy(out=dst[:, :, c0:c0 + 512], in_=wst[:, :nk, :]), reads=['wst'], writes=[dres])

    def out_phase(layer, wout2, yT=None):
        yT = xT if yT is None else yT
        wo_bf, wg_bf, wp_bf, h1, h1b, h1T, sg = (LT[k] for k in ('wo_bf', 'wg_bf', 'wp_bf', 'h1', 'h1b', 'h1T', 'sg'))
        h2 = [LT['h2_0'], LT['h2_0']]
        hin = [LT['hin0'], LT['hin0']]
        pin, pinb, pT = LT['pin'], LT['pinb'], LT['pT']
        load_w_full(wo_bf, 'wo_bf', wout2, 8)
        load_w_full(wg_bf, 'wg_bf', ple_gate_w[layer], 8)
        load_w_full(wp_bf, 'wp_bf', ple_w[layer], 2)
        hs = h_src(layer)
        for t in range(NT):
            P, tok0 = tile_info(t)
            i = hin_i[0]
            hin_i[0] ^= 1
            hr = 'hin0'
            sch.dma('sp', hin[i][:P], hs(t), reads=['hd%d' % t], writes=[hr])
            psrc = p_p[layer, tok0:tok0 + P, :] if t < NTP else p_s[layer, :, :]
            sch.dma('sp', pin[:P], psrc, writes=['pin'])
            for c in range(2):
                g, gr = G.next()
                for kc in range(8):
                    sch.op('pe', lambda e: e.matmul(g[:P, :], lhsT=yT[:, kc, tok0:tok0 + P], rhs=wo_bf[:, kc, c * 512:(c + 1) * 512],
                                                    start=(kc == 0), stop=(kc == 7)), reads=['xT', 'qT', 'wo_bf'], writes=[gr])
                sch.op('dve', lambda e: e.tensor_tensor(out=h1[:P, c * 512:(c + 1) * 512], in0=g[:P, :], in1=hin[i][:P, c * 512:(c + 1) * 512],
                                                        op=ALU.add), reads=[gr, hr], writes=['h1'])
            sch.op('act', lambda e: e.copy(out=h1b[:P, :], in_=h1[:P, :]), reads=['h1'], writes=['h1b'])
            transposes_to(P, h1b[:P], 'h1b', 8, h1T[:, :, :P], 'h1T')
            sch.op('dve', lambda e: e.tensor_copy(out=pinb[:P, :], in_=pin[:P, :]), reads=['pin'], writes=['pinb'])
            transposes_to(P, pinb[:P], 'pinb', 2, pT[:, :, :P], 'pT')
            j = h2_i[0]
            h2_i[0] ^= 1
            h2r = 'h2_0'
            for c in range(2):
                g, gr = G.next()
                for kc in range(8):
                    sch.op('pe', lambda e: e.matmul(g[:P, :], lhsT=h1T[:, kc, :P], rhs=wg_bf[:, kc, c * 512:(c + 1) * 512],
                                                    start=(kc == 0), stop=(kc == 7)), reads=['h1T', 'wg_bf'], writes=[gr])
                sch.op('act', lambda e: e.activation(out=sg[:P, c * 512:(c + 1) * 512], in_=g[:P, :], func=AF.Sigmoid),
                       reads=[gr], writes=['sg'])
                g, gr = G.next()
                for kc in range(2):
                    sch.op('pe', lambda e: e.matmul(g[:P, :], lhsT=pT[:, kc, :P], rhs=wp_bf[:, kc, c * 512:(c + 1) * 512],
                                                    start=(kc == 0), stop=(kc == 1)), reads=['pT', 'wp_bf'], writes=[gr])
                sch.op('dve', lambda e: e.tensor_tensor(out=sg[:P, c * 512:(c + 1) * 512], in0=g[:P, :], in1=sg[:P, c * 512:(c + 1) * 512],
                                                        op=ALU.mult), reads=[gr, 'sg'], writes=['sg'])
                sch.op('dve', lambda e: e.tensor_tensor(out=h2[j][:P, c * 512:(c + 1) * 512], in0=sg[:P, c * 512:(c + 1) * 512],
                                                        in1=h1[:P, c * 512:(c + 1) * 512], op=ALU.add), reads=['sg', 'h1'], writes=[h2r])
            dst = y_p[tok0:tok0 + P, :] if t < NTP else y_s[:, :]
            sch.dma('pool', dst, h2[j][:P, :], reads=[h2r], writes=['hd%d' % t])


    def shared_kv_phase():
        sch.dma('sp', gq[:, 1, :], bc_row(b_k_norm), writes=['gq'])

        def consumer(ci, t, P, tok0, g, gr):
            ri = rowb_i[0]
            rowb_i[0] ^= 1
            rb = rowb[ri]
            rr = 'rowb%d' % ri
            src3 = g[:P, 0:256].rearrange("p (h d) -> p h d", d=64)
            gain3 = AP(gq, 64, [[256, P], [0, 4], [1, 64]])
            out3 = rb[:P, 0:256].rearrange("p (h d) -> p h d", d=64)
            headnorm(P, src3, 4, gain3, out3, rr, gr, extra_reads=['gq'])
            sch.op('act', lambda e: e.copy(out=rb[:P, 256:512], in_=g[:P, 256:512]), reads=[gr], writes=[rr])
            if t < NTP:
                sch.dma('pool', o_pb[tok0:tok0 + P, :], rb[:P, :], reads=[rr], writes=['o_pb'])
                sch.op('dve', lambda e: e.tensor_copy(out=qnb[:P, :], in_=rb[:P, :]), reads=[rr], writes=['qnb'])
                transposes_to(P, qnb[:P, 0:256], 'qnb', 2, ksT[:, :, tok0:tok0 + P], 'ksT')
                sch.op('pool', lambda e: e.tensor_copy(out=vs_aug[:, t, :, 0:64],
                                                        in_=qnb[:, 256:512].rearrange("p (g d) -> p g d", d=64)),
                       reads=['qnb'], writes=['vs_aug'])
            else:
                for b in range(4):
                    sch.dma('pool', o_sb[b, 2044:2048, :], rb[4 * b:4 * b + 4, :], reads=[rr], writes=['o_sb'])
        gemm(lambda kc, t: xT[:, kc, tile_info(t)[1]:tile_info(t)[1] + tile_info(t)[0]], 'xT', b_w_kv, [(0, 512)],
             list(range(NT)), consumer)

    def q_consume(P, g, gr, gain_off, dst3, dres):
        src3 = g[:P, :512].rearrange("p (h d) -> p h d", d=64)
        out4 = AP(qnb, 0, [[512, P], [64, 2], [128, 4], [1, 64]])
        sq3 = sq[:P, :512].rearrange("p (h d) -> p h d", d=64)
        sch.op('act', lambda e: e.activation(out=sq3, in_=src3, func=AF.Square), reads=[gr], writes=['sq'])
        sch.op('dve', lambda e: e.tensor_reduce(out=small[:P, 0:8], in_=sq3, axis=AX.X, op=ALU.add),
               reads=['sq'], writes=['small'])
        rstd_from_ss(P, 8, small[:P, 0:8], small[:P, 16:24], 1.0 / 64)
        t3 = tmpf[:P, :512].rearrange("p (h d) -> p h d", d=64)
        sch.op('dve', lambda e: e.tensor_tensor(out=t3, in0=src3, in1=AP(small, 16, [[64, P], [1, 8], [0, 64]]),
                                                op=ALU.mult), reads=[gr, 'small'], writes=['tmpf'])
        t4 = tmpf[:P, :512].rearrange("p (a r d) -> p a r d", a=2, r=4)
        g4 = AP(gbq, gain_off, [[192, P], [0, 2], [0, 4], [1, 64]])
        sch.op('dve', lambda e: e.tensor_tensor(out=out4, in0=t4, in1=g4, op=ALU.mult),
               reads=['tmpf', 'gbq'], writes=['qnb'])
        transposes_to(P, qnb[:P], 'qnb', 4, dst3, dres)

    def b_z_phase(j):
        W = b_w_in[j]

        def consumer(ci, t, P, tok0, g, gr):
            ri = rowb_i[0]
            rowb_i[0] ^= 1
            rb = rowb[ri]
            rr = 'rowb%d' % ri
            sch.op('act', lambda e: e.activation(out=rb[:P, :], in_=g[:P, :512], func=AF.Silu), reads=[gr], writes=[rr])
            sch.dma('pool', gz_d[tok0:tok0 + P, ci * 512:(ci + 1) * 512], rb[:P, :], reads=[rr], writes=['gz_d'])
        gemm(lambda kc, t: xT[:, kc, tile_info(t)[1]:tile_info(t)[1] + tile_info(t)[0]], 'xT', W,
             [(3072, 512), (3584, 512)], list(range(NT)), consumer)

    def b_q_phase(j, gp):
        W = b_w_in[j]
        q3T = LT['q3T']
        sch.dma('sp', gbq[:].rearrange("p a d -> p (a d)"), bc_row(b_q_norm[j].rearrange("a d -> (a d)")), writes=['gbq'])
        sch.op('dve', lambda e: e.tensor_scalar(out=gbq[:], in0=gbq[:], scalar1=SCALE, scalar2=None, op0=ALU.mult),
               reads=['gbq'], writes=['gbq'])

        def consumer(ci, t, P, tok0, g, gr):
            q_consume(P, g, gr, ci * 64, q3T[:, 4 * ci:4 * ci + 4, tok0:tok0 + P], 'qT')
        gemm(lambda kc, t: xT[:, kc, tile_info(t)[1]:tile_info(t)[1] + tile_info(t)[0]], 'xT', W,
             [(grp * 1024 + gp * 512, 512) for grp in range(3)], list(range(NT)), consumer)

    def b_attention_prompt(j, gp):
        cur_SS[0] = SS4
        q3T = LT['q3T']
        Gd = {1: LT['Gd1'], 4: LT['Gd4'], 16: LT['Gd16']}
        gzt = [LT['gzt0'], LT['gzt1']]
        Ld = {1: 384, 4: 768, 16: 2304}
        Wd = {1: 256, 4: 640, 16: 2048}
        for g in (2 * gp, 2 * gp + 1):
            half, ch = g % 2, g // 2
            hs = slice(half * 64, half * 64 + 64)
            for d in (1, 4, 16):
                toeplitz_load(Gd[d][:], 'Gd%d' % d, 'd%d' % d, 0, 1, 4 * g, Ld[d], Wd[d])
            for b in range(NTP):
                P, tok0 = 128, b * 128
                bi = b % 2
                sch.dma('sp', gzt[bi][:, 0, :], gz_d[tok0:tok0 + P, g * 256:(g + 1) * 256], reads=['gz_d'], writes=['gzt%d' % bi])
                o, ores = OO.next()
                tiles = []
                for grp, d, na in ((0, 1, 2), (1, 4, 5), (2, 16, 99)):
                    for a in range(max(0, b - na + 1), b + 1):
                        tiles.append((grp, d, a))
                def after_b(o=o, ores=ores, bi=bi, g=g, tok0=tok0, P=P):
                    branch_out(o, ores, gzt[bi][:, 0, :], 'gzt%d' % bi, True)
                    sch.op('act', lambda e: e.copy(out=ybf[:, :], in_=yacc[:, :]), reads=['yacc'], writes=['ybf'])
                    transposes_to(128, ybf[:, :], 'ybf', 2, qT[:, 2 * g:2 * g + 2, tok0:tok0 + P], 'qT')
                for ti, (grp, d, a) in enumerate(tiles):
                    qrhs = q3T[hs, 4 * grp:4 * grp + 4, tok0:tok0 + P]
                    att_tile(128, P, ksT[hs, ch, a * 128:(a + 1) * 128], 'ksT', qrhs,
                             Gd[d][:, :, 128 * (b - a):128 * (b - a) + 128], 'Gd%d' % d, [],
                             vs_aug[:, a, g, :], 'vs_aug', o, ores, ti == 0, ti == len(tiles) - 1, 65,
                             after=(after_b if ti == len(tiles) - 1 else None))
        flush_pv()
        cur_SS[0] = SS

    SC_PROJB = [('wst', [128, 8, 512], F32), ('wbf', [128, 2, 8, 512], BF16)]
    SC_Q3 = [('q3T', [128, 12, NTOK], BF16)]
    SC_ATTB = [('Gd1', [128, 4, 256], BF16), ('Gd4', [128, 4, 640], BF16), ('Gd16', [128, 4, 2048], BF16),
               ('gzt0', [128, 1, 256], F32), ('gzt1', [128, 1, 256], F32),
               ('Pt0', [128, 512], BF16), ('Pt1', [128, 512], BF16), ('Pt2', [128, 512], BF16), ('Pt3', [128, 512], BF16)]


    GsT = pg.sb('GsT', [128, 58, 16, 4], BF16)
    msk_d = pg.dint('msk_d', [4, 2048], BF16)
    cache_cmp_flat = AP(cache_cmp, 0, [[512, 2 * NPHYS * 128], [1, 512]])
    cache_slc_flat = AP(cache_slc, 0, [[512, 2 * NPHYS * 128], [1, 512]])
    TI_SLC, TI_WIN, TI_CMP, TI_B = 0, 25, 30, 34

    def load_gs(ti, fdk, L, base, pstep, nk):
        src = AP(fd[fdk], base, [[pstep, nk], [L, 16], [1, 4]])
        sch.dma('sp', GsT[:nk, ti, :, :], src, reads=['fd_' + fdk], writes=['GsT'])

    def sample_tables():
        Ls = hc['oh_slc'].shape[1]
        for a in range(40, 64):
            load_gs(TI_SLC + a - 40, 'slc', Ls, 8192 - 128 * a, 1, 128)
        load_gs(TI_SLC + 24, 'slc', Ls, 124, 1, 4)
        for a in range(4):
            load_gs(TI_WIN + a, 'win', 768, 512 - 128 * a, 1, 128)
        load_gs(TI_WIN + 4, 'win', 768, 124, 1, 4)
        for ct in range(4):
            load_gs(TI_CMP + ct, 'slc', Ls, 6256 - 2048 * ct, 16, 128)
        ti = TI_B
        for d, L, a0 in ((1, 384, 15), (4, 768, 12), (16, 2304, 0)):
            for a in range(a0, 16):
                load_gs(ti, 'd%d' % d, L, 2048 - 128 * a, 1, 128)
                ti += 1
            load_gs(ti, 'd%d' % d, L, 124, 1, 4)
            ti += 1

    SC_PREP = [('rows0', [128, 512], F32), ('rows1', [128, 512], F32), ('kb0', [128, 256], BF16), ('kb1', [128, 256], BF16),
               ('kTt0', [128, 2, 128], BF16), ('kTt1', [128, 2, 128], BF16), ('kTt2', [128, 2, 128], BF16),
               ('vat0', [128, 4, 65], BF16), ('vat1', [128, 4, 65], BF16), ('vat2', [128, 4, 65], BF16),
               ('Pt0', [128, 512], BF16), ('Pt1', [128, 512], BF16), ('Pt2', [128, 512], BF16), ('Pt3', [128, 512], BF16),
               ('gzs', [4, 1024], F32), ('ys', [4, 4, 256], F32)]
    SC_SAMPA = SC_PREP + [('idx', [128, 256], I32), ('ptab_i', [128, 256], I32), ('pidx', [128, 1], F32),
                          ('w1s', [128, 32, 256], BF16), ('w2s2', [128, 2, 2, 64], BF16), ('b1s2', [128, 4], F32),
                          ('b2s2', [128, 2, 64], F32), ('wst4', [128, 4, 256], F32), ('chT', [128, 4, 2176], BF16),
                          ('rowsb', [128, 512], BF16), ('kccT_s', [128, 2, 512], BF16), ('vcc_s', [128, 4, 4, 65], BF16),
                          ('ovs', [128, 4, 128], BF16), ('selBs', [4, 136], F32), ('sc1s', [4, 136], F32), ('sc2s', [4, 136], F32),
                          ('scs', [4, 128], F32), ('mx8s', [4, 16], F32), ('mneg_s', [4, 4, 128], BF16), ('Xm', [1, 2048], BF16),
                          ('e2', [1, 64], BF16)]
    prep_i = [0, 0]

    def prep_tile(load_fn, nk):
        ri = prep_i[0]
        prep_i[0] ^= 1
        ki = prep_i[1]
        prep_i[1] = (ki + 1) % 3
        rows, kb, kTt, vat = LT['rows%d' % ri], LT['kb%d' % ri], LT['kTt%d' % ki], LT['vat%d' % ki]
        rr, kbr, ktr, var = 'rows%d' % ri, 'kb%d' % ri, 'kTt%d' % ki, 'vat%d' % ki
        load_fn(rows[:nk, :], rr)
        sch.op('dve', lambda e: e.tensor_copy(out=kb[:nk, :], in_=rows[:nk, 0:256]), reads=[rr], writes=[kbr])
        sch.op('act', lambda e: e.copy(out=vat[:nk, :, 0:64], in_=rows[:nk, 256:512].rearrange("p (g d) -> p g d", d=64)),
               reads=[rr], writes=[var])
        transposes_to(nk, kb[:nk, :], kbr, 2, kTt[:, :, :nk], ktr)
        return kTt, ktr, vat, var

    def dram_loader(src_ap, res=()):
        def f(rows_ap, rr):
            sch.dma('sp', rows_ap, src_ap, reads=list(res), writes=[rr])
        return f

    def page_loader(flat, col):
        def f(rows_ap, rr):
            sch.idma(rows_ap, flat, LT['idx'][:, col:col + 1], reads=['idx'], writes=[rr])
        return f

    def branch_out_s(o, ores, g, first):
        P = 4
        gzs, ys = LT['gzs'], LT['ys']
        o3 = AP(o, 0, [[512, P], [128, 4], [1, 64]])
        sch.op('dve', lambda e: e.tensor_scalar(out=rden[:P, :], in0=AP(o, 64, [[512, P], [128, 4]]), scalar1=1e-30,
                                                scalar2=None, op0=ALU.max), reads=[ores], writes=['rden'])
        sch.op('dve', lambda e: e.reciprocal(out=rden[:P, :], in_=rden[:P, :]), reads=['rden'], writes=['rden'])
        sch.op('dve', lambda e: e.tensor_tensor(out=osb[:P, :, 0:64], in0=o3, in1=AP(rden, 0, [[4, P], [1, 4], [0, 64]]),
                                                op=ALU.mult), reads=[ores, 'rden'], writes=['osb'])
        gz3 = gzs[:P, g * 256:(g + 1) * 256].rearrange("p (r d) -> p r d", d=64)
        sch.op('dve', lambda e: e.tensor_tensor(out=osb[:P, :, 0:64], in0=osb[:P, :, 0:64], in1=gz3, op=ALU.mult),
               reads=['osb', 'gzs'], writes=['osb'])
        y3 = ys[:P, g, :].rearrange("p (r d) -> p r d", d=64)
        if first:
            sch.op('dve', lambda e: e.tensor_copy(out=y3, in_=osb[:P, :, 0:64]), reads=['osb'], writes=['ys'])
        else:
            sch.op('dve', lambda e: e.tensor_tensor(out=y3, in0=y3, in1=osb[:P, :, 0:64], op=ALU.add),
                   reads=['osb', 'ys'], writes=['ys'])

    def samp_finish(bs, ydst, yres, groups=(0, 1, 2, 3)):
        ys = LT['ys']
        qs = S + 4 * bs
        for g in groups:
            sch.op('act', lambda e: e.copy(out=ybf[:4, :], in_=ys[:4, g, :]), reads=['ys'], writes=['ybf'])
            transposes_to(4, ybf[:4, :], 'ybf', 2, ydst[:, 2 * g:2 * g + 2, qs:qs + 4], yres)

    def load_gz(bs, br):
        sch.dma('sp', LT['gzs'][:4, :], gz_d[S + 4 * bs:S + 4 * bs + 4, br * 1024:(br + 1) * 1024], reads=['gz_d'], writes=['gzs'])

    def samp_multi(bs, tiles, qsrc, qblk_of, groups, masked):
        acc = {0: G.items[0], 1: G.items[1], 2: OO.items[0], 3: OO.items[1]}
        qs = S + 4 * bs
        for ti, (load_fn, nk, tab, ea, grp) in enumerate(tiles):
            kTt, ktr, vat, var = prep_tile(load_fn, nk)
            for g in groups:
                half, ch = g % 2, g // 2
                hs = slice(half * 64, half * 64 + 64)
                qb = qblk_of(g, grp)
                qrhs = qsrc[hs, qb:qb + 4, qs:qs + 4]
                extra = []
                if masked and ea is not None:
                    extra = [(hf_, LT['e2'][0:1, 0:64], AP(LT['Xm'], g * 128 + 2 * ea + hf_, [[2048, 1], [0, 4], [512, 4]]), ['e2', 'Xm'])
                             for hf_ in range(2)]
                o, ores = acc[g]
                att_tile(nk, 4, kTt[hs, ch, :nk], ktr, qrhs, GsT[:nk, tab, 4 * g:4 * g + 4, :], 'GsT', extra,
                         vat[:nk, g, :], var, o, ores, ti == 0, ti == len(tiles) - 1, 65)
        flush_pv()
        return acc

    def init_vat():
        for i_ in range(3):
            sch.op('dve', lambda e: e.memset(LT['vat%d' % i_][:, :, 64:65], 1.0), writes=['vat%d' % i_])

    def a_sample(li):
        init_vat()
        idx, ptab_i, pidx = LT['idx'], LT['ptab_i'], LT['pidx']
        w1s, w2s2, b1s2, b2s2, wst4 = LT['w1s'], LT['w2s2'], LT['b1s2'], LT['b2s2'], LT['wst4']
        chT, rowsb, kccT_s, vcc_s, ovs = LT['chT'], LT['rowsb'], LT['kccT_s'], LT['vcc_s'], LT['ovs']
        selBs, sc1s, sc2s, scs, mx8s, mneg_s, Xm, e2 = (LT[k] for k in ('selBs', 'sc1s', 'sc2s', 'scs', 'mx8s', 'mneg_s', 'Xm', 'e2'))
        sch.dma('sp', ptab_i[:, :], AP(ptab, 0, [[0, 128], [1, 256]]), writes=['ptab_i'])
        sch.dma('sp', pidx[:, :], cdram['pidx'][:, :], writes=['pidx'])
        sch.dma('sp', ovs[:], cdram['ov_s'][:], writes=['ovs'])
        sch.dma('sp', selBs[:], cdram['selB_s'][:], writes=['selBs'])
        sch.op('dve', lambda e: e.memset(e2[:], 1.0), writes=['e2'])
        sch.op('dve', lambda e: e.tensor_scalar(out=idx[:, :], in0=ptab_i[:, :], scalar1=128.0, scalar2=pidx[:, 0:1],
                                                op0=ALU.mult, op1=ALU.add), reads=['ptab_i', 'pidx'], writes=['idx'])
        if li == 1:
            sch.op('dve', lambda e: e.tensor_scalar(out=idx[:, :], in0=idx[:, :], scalar1=float(NPHYS * 128), scalar2=None,
                                                    op0=ALU.add), reads=['idx'], writes=['idx'])
        for kv in range(2):
            sch.dma('sp', wst4[:, 0, 0:128].rearrange("p (a d) -> p a d", d=64),
                    AP(a_phi_w2[li, kv], 0, [[64, 128], [128 * 64, 2], [1, 64]]), writes=['wst4'])
            sch.op('pool', lambda e: e.tensor_copy(out=w2s2[:, kv, :, :], in_=wst4[:, 0, 0:128].rearrange("p (a d) -> p a d", d=64)),
                   reads=['wst4'], writes=['w2s2'])
            for hh_ in range(2):
                sch.dma('sp', b1s2[:, 2 * kv + hh_:2 * kv + hh_ + 1], AP(a_phi_b1[li, kv], hh_ * 128, [[1, 128], [1, 1]]), writes=['b1s2'])
            sch.dma('sp', b2s2[:, kv, :], bc_row(a_phi_b2[li, kv]), writes=['b2s2'])

        ksv = ksT[:, :, :].rearrange("p a (b h) -> p (a b) h", h=256)
        kwv = kwT[:, :, :].rearrange("p a (b h) -> p (a b) h", h=256)

        def w1_of(kv, ab):
            if kv == 0:
                return w1s[:, ab * 16:(ab + 1) * 16, :], 'w1s'
            return (ksv, 'ksT') if ab == 0 else (kwv, 'kwT')

        for kv in range(2):
            for ab in range(2):
                wv, wres = w1_of(kv, ab)
                for sq4 in range(4):
                    for half in range(2):
                        src = AP(a_phi_w1[li, kv, ab], sq4 * 4 * 64 * 256, [[256, 64], [64 * 256, 4], [1, 256]])
                        sch.dma('sp', wst4[half * 64:half * 64 + 64, :, :], src, writes=['wst4'])
                    sch.op('pool', lambda e: e.tensor_copy(out=wv[:, sq4 * 4:sq4 * 4 + 4, :], in_=wst4[:, :, :]),
                           reads=['wst4'], writes=[wres])

        for bs in range(4):
            qs = S + 4 * bs
            sch.op('dve', lambda e: e.memset(kccT_s[:], 0.0), writes=['kccT_s'])
            sch.op('dve', lambda e: e.memset(vcc_s[:], 0.0), writes=['vcc_s'])
            for ct in range(4):
                npg = 17 if ct < 3 else 16
                ncol = 128 if ct < 3 else 127
                for pi in range(npg):
                    ri = prep_i[0]
                    prep_i[0] ^= 1
                    rows, rr = LT['rows%d' % ri], 'rows%d' % ri
                    sch.idma(rows[:, :], cache_cmp_flat, idx[:, bs * 64 + 16 * ct + pi:bs * 64 + 16 * ct + pi + 1],
                             reads=['idx'], writes=[rr])
                    if pi % 2 == 0:
                        sch.op('dve', lambda e: e.tensor_copy(out=rowsb[:, :], in_=rows[:, :]), reads=[rr], writes=['rowsb'])
                    else:
                        sch.op('act', lambda e: e.copy(out=rowsb[:, :], in_=rows[:, :]), reads=[rr], writes=['rowsb'])
                    transposes_to(128, rowsb[:, :], 'rowsb', 4, chT[:, :, pi * 128:(pi + 1) * 128], 'chT')
                for kv in range(2):
                    for g in range(4):
                        half, ch = g % 2, g // 2
                        hs = slice(half * 64, half * 64 + 64)
                        blk = kv * 2 + ch
                        for hh in range(2):
                            ps, psr = SS.next()
                            n = 0
                            for ab in range(2):
                                for s_ in range(16):
                                    rhs = AP(chT, half * 64 * (4 * 2176) + blk * 2176 + 16 * ab + s_, [[4 * 2176, 64], [16, ncol]])
                                    wv_, wres_ = w1_of(kv, ab)
                                    sch.op('pe', lambda e: e.matmul(ps[:, 0:ncol], lhsT=wv_[hs, s_, hh * 128:(hh + 1) * 128],
                                                                    rhs=rhs, start=(n == 0), stop=(n == 31)),
                                           reads=[wres_, 'chT'], writes=[psr])
                                    n += 1
                            sch.op('act', lambda e: e.activation(out=hidT[:, hh, 0:ncol], in_=ps[:, 0:ncol], func=AF.Silu,
                                                                 bias=b1s2[:, 2 * kv + hh:2 * kv + hh + 1]), reads=[psr, 'b1s2'], writes=['hidT'])
                        po, por = OO.next()
                        for hh in range(2):
                            sch.op('pe', lambda e: e.matmul(po[0:ncol, 0:64], lhsT=hidT[:, hh, 0:ncol], rhs=w2s2[:, kv, hh, :],
                                                            start=(hh == 0), stop=(hh == 1)), reads=['hidT', 'w2s2'], writes=[por])
                        sch.op('dve', lambda e: e.tensor_tensor(out=tmpf[0:ncol, 0:64], in0=po[0:ncol, 0:64], in1=b2s2[0:ncol, kv, :], op=ALU.add),
                               reads=[por, 'b2s2'], writes=['tmpf'])
                        if kv == 0:
                            src3 = tmpf[0:ncol, 0:64].rearrange("p (h d) -> p h d", d=64)
                            sq3 = sq[:ncol, :64].rearrange("p (h d) -> p h d", d=64)
                            sch.op('act', lambda e: e.activation(out=sq3, in_=src3, func=AF.Square), reads=['tmpf'], writes=['sq'])
                            sch.op('dve', lambda e: e.tensor_reduce(out=small[:ncol, 0:1], in_=sq3, axis=AX.X, op=ALU.add),
                                   reads=['sq'], writes=['small'])
                            rstd_from_ss(ncol, 1, small[:ncol, 0:1], small[:ncol, 16:17], 1.0 / 64)
                            if half == 0:
                                sch.op('dve', lambda e: e.memset(qnb[:, 0:128], 0.0), writes=['qnb'])
                            sch.op('dve', lambda e: e.scalar_tensor_tensor(out=qnb[0:ncol, half * 64:half * 64 + 64], in0=tmpf[0:ncol, 0:64],
                                                                           scalar=small[0:ncol, 16:17], in1=AP(gq, 64, [[256, ncol], [1, 64]]),
                                                                           op0=ALU.mult, op1=ALU.mult),
                                   reads=['tmpf', 'small', 'gq'], writes=['qnb'])
                            if half == 1:
                                transposes_to(128, qnb[:, 0:128], 'qnb', 1, kccT_s[:, ch:ch + 1, ct * 128:(ct + 1) * 128], 'kccT_s')
                        else:
                            sch.op('act', lambda e: e.copy(out=vcc_s[0:ncol, ct, g, 0:64], in_=tmpf[0:ncol, 0:64]), reads=['tmpf'], writes=['vcc_s'])
                            sch.op('dve', lambda e: e.memset(vcc_s[0:ncol, ct, g, 64:65], 1.0), writes=['vcc_s'])
            load_gz(bs, 0)
            for g in range(4):
                half, ch = g % 2, g // 2
                hs = slice(half * 64, half * 64 + 64)
                qrhs = qT[hs, 4 * ch:4 * ch + 4, qs:qs + 4]
                o, ores = OO.next()
                o2, o2res = TF
                for ct in range(4):
                    att_tile(128, 4, kccT_s[hs, ch, ct * 128:(ct + 1) * 128], 'kccT_s', qrhs, GsT[:, TI_CMP + ct, 4 * g:4 * g + 4, :], 'GsT',
                             [], vcc_s[:, ct, g, :], 'vcc_s', o, ores, ct == 0, ct == 3, 65,
                             score=(ovs[:, ct, :], 'ovs', o2, o2res))
                flush_pv()
                branch_out_s(o, ores, g, True)
                sch.op('dve', lambda e: e.tensor_scalar(out=scs[:, :], in0=o2[:4, 0:128], scalar1=rden[:4, 0:1], scalar2=None, op0=ALU.mult),
                       reads=[o2res, 'rden'], writes=['scs'])
                for r in range(1, 4):
                    sch.op('dve', lambda e: e.scalar_tensor_tensor(out=scs[:, :], in0=o2[:4, r * 128:(r + 1) * 128], scalar=rden[:4, r:r + 1],
                                                                   in1=scs[:, :], op0=ALU.mult, op1=ALU.add),
                           reads=[o2res, 'rden', 'scs'], writes=['scs'])
                sch.op('dve', lambda e: e.tensor_copy(out=sc1s[:, :], in_=selBs[:, :]), reads=['selBs'], writes=['sc1s'])
                sch.op('dve', lambda e: e.tensor_tensor(out=sc1s[:, 0:128], in0=scs[:, :], in1=selBs[:, 0:128], op=ALU.add),
                       reads=['scs', 'selBs'], writes=['sc1s'])
                sch.op('dve', lambda e: e.max(out=mx8s[:, 0:8], in_=sc1s[:, :]), reads=['sc1s'], writes=['mx8s'])
                sch.op('dve', lambda e: e.match_replace(out=sc2s[:, :], in_to_replace=mx8s[:, 0:8], in_values=sc1s[:, :], imm_value=-2.0),
                       reads=['sc1s', 'mx8s'], writes=['sc2s'])
                sch.op('dve', lambda e: e.max(out=mx8s[:, 8:16], in_=sc2s[:, :]), reads=['sc2s'], writes=['mx8s'])
                sch.op('dve', lambda e: e.tensor_scalar(out=sc2s[:, :], in0=sc1s[:, :], scalar1=mx8s[:, 15:16], scalar2=None,
                                                        op0=ALU.is_ge), reads=['sc1s', 'mx8s'], writes=['sc2s'])
                sch.op('dve', lambda e: e.tensor_scalar(out=mneg_s[:, g, :], in0=sc2s[:, 0:128], scalar1=-1.0, scalar2=-MASKV,
                                                        op0=ALU.add, op1=ALU.mult), reads=['sc2s'], writes=['mneg_s'])
            sch.dma('sp', AP(msk_d, bs * 2048, [[512, 4], [128, 4], [1, 128]]), mneg_s[:, :, :], reads=['mneg_s'], writes=['msk_d'])
            sch.dma('sp', Xm[0:1, :], AP(msk_d, bs * 2048, [[0, 1], [1, 2048]]), reads=['msk_d'], writes=['Xm'])
            load_gz(bs, 1)
            tiles = []
            for a in range(64):
                tab = TI_SLC + max(a, 40) - 40
                tiles.append((page_loader(cache_slc_flat, bs * 64 + a), 128, tab, a, 0))
            tiles.append((dram_loader(o_ss[li, 4 * bs:4 * bs + 4, :], ('o_ss',)), 4, TI_SLC + 24, None, 0))
            acc = samp_multi(bs, tiles, qT, lambda g, grp: 4 * (g // 2), (0, 1, 2, 3), True)
            for g in range(4):
                branch_out_s(acc[g][0], acc[g][1], g, False)
            load_gz(bs, 2)
            tiles = []
            for a in range(4):
                tiles.append((dram_loader(st_a[li, bs, 128 * a:128 * a + 128, :]), 128, TI_WIN + a, None, 0))
            tiles.append((dram_loader(o_sw[li, bs, 508:512, :], ('o_sw',)), 4, TI_WIN + 4, None, 0))
            acc = samp_multi(bs, tiles, qT, lambda g, grp: 4 * (g // 2), (0, 1, 2, 3), False)
            for g in range(4):
                branch_out_s(acc[g][0], acc[g][1], g, False)
            samp_finish(bs, xT, 'xT')

    def b_sample(j, gp):
        init_vat()
        q3T = LT['q3T']
        groups = (2 * gp, 2 * gp + 1)
        for bs in range(4):
            sch.dma('sp', LT['gzs'][:4, :], gz_d[S + 4 * bs:S + 4 * bs + 4, 0:1024], reads=['gz_d'], writes=['gzs'])
            tiles = []
            ti = TI_B
            for grp, (d, a0) in enumerate(((1, 15), (4, 12), (16, 0))):
                for a in range(a0, 16):
                    tiles.append((dram_loader(st_b[bs, 128 * a:128 * a + 128, :]), 128, ti, None, grp))
                    ti += 1
                tiles.append((dram_loader(o_sb[bs, 2044:2048, :], ('o_sb',)), 4, ti, None, grp))
                ti += 1
            acc = samp_multi(bs, tiles, q3T, lambda g, grp: 4 * grp, groups, False)
            for g in groups:
                branch_out_s(acc[g][0], acc[g][1], g, True)
            samp_finish(bs, qT, 'qT', groups)

    def state_copies():
        for li in range(2):
            for b in range(4):
                sch.dma('pool', o_sw[li, b, 0:508, :], st_a[li, b, 4:512, :], writes=['o_sw'])
        for b in range(4):
            sch.dma('pool', o_sb[b, 0:2044, :], st_b[b, 4:2048, :], writes=['o_sb'])

    sch.op('dve', lambda e: e.memset(vs_aug[:, :, :, 64:65], 1.0), writes=['vs_aug'])
    sch.op('dve', lambda e: e.memset(vw_aug[:, :, :, 64:65], 1.0), writes=['vw_aug'])

    state_copies()
    sample_tables()
    for layer in range(4):
        if layer >= STAGE:
            break
        with Scope(SC_NORM, 'norm'):
            phase_norm(h_src(layer), norm_g[layer])
        if layer < 2:
            with Scope(SC_PROJ, 'projA'):
                a_project(layer)
                compress_prompt(layer)
            with Scope(SC_ATT, 'attA'):
                a_attention_prompt(layer)
            with Scope(SC_SAMPA, 'sampA'):
                a_sample(layer)
            with Scope(SC_OUT, 'out'):
                out_phase(layer, a_w_out[layer])
            if layer == 1 and STAGE > 2:
                with Scope(SC_NORM, 'norm'):
                    phase_norm(h_src(2), kv_norm_g)
                with Scope(SC_PROJB, 'projB'):
                    shared_kv_phase()
        else:
            j = layer - 2
            with Scope(SC_PROJB, 'projB'):
                b_z_phase(j)
            with Scope(SC_Q3):
                for gp in range(2):
                    with Scope(SC_PROJB, 'projB'):
                        b_q_phase(j, gp)
                    if gp == 1:
                        pass
                    with Scope(SC_ATTB, 'attB'):
                        b_attention_prompt(j, gp)
                    with Scope(SC_PREP, 'sampB'):
                        b_sample(j, gp)
            with Scope(SC_OUT, 'out'):
                out_phase(layer, b_w_out[j], qT)

    sch.barrier()
    return pg, hc


_CACHE = {}


def kernel(**inputs):
    if 'prog' not in _CACHE:
        _CACHE['prog'] = build()
    pg, hc = _CACHE['prog']
    f = lambda a: np.ascontiguousarray(np.asarray(a))
    x_prompt = f(inputs['x_prompt'])
    x_sample = f(inputs['x_sample'])
    cache_cmp = f(inputs['cache_a_cmp']).reshape(2, NPHYS * 128, 512)
    cache_slc = f(inputs['cache_a_slc']).reshape(2, NPHYS * 128, 512)
    st_a = f(inputs['state_a_win'])
    st_b = f(inputs['state_b_win'])
    ptab = f(inputs['page_table']).astype(np.int32)
    p_prompt = f(inputs['p_prompt'])
    p_sample = f(inputs['p_sample'])
    shared = {
        'cache_cmp': cache_cmp, 'cache_slc': cache_slc,
        'rel_bias': f(inputs['rel_bias']), 'norm_g': f(inputs['norm_g']), 'a_w_in': f(inputs['a_w_in']),
        'a_gate_b': f(inputs['a_gate_b']).reshape(2, 48), 'a_qk_norm': f(inputs['a_qk_norm']),
        'a_phi_w1': f(inputs['a_phi_w1']), 'a_phi_b1': f(inputs['a_phi_b1']), 'a_phi_w2': f(inputs['a_phi_w2']),
        'a_phi_b2': f(inputs['a_phi_b2']), 'a_w_out': f(inputs['a_w_out']), 'kv_norm_g': f(inputs['kv_norm_g']),
        'b_w_kv': f(inputs['b_w_kv']), 'b_k_norm': f(inputs['b_k_norm']), 'b_w_in': f(inputs['b_w_in']),
        'b_q_norm': f(inputs['b_q_norm']), 'b_w_out': f(inputs['b_w_out']), 'ple_w': f(inputs['ple_w']),
        'ple_gate_w': f(inputs['ple_gate_w']),
    }
    for k, v in hc.items():
        shared['c_' + k] = v
    in_maps = []
    for c in range(8):
        m = dict(shared)
        m['x_p'] = x_prompt[c]
        m['x_s'] = x_sample[4 * c:4 * c + 4].reshape(16, D)
        m['p_p'] = p_prompt[:, c]
        m['p_s'] = p_sample[:, 4 * c:4 * c + 4].reshape(4, 16, 256)
        m['st_a'] = st_a[:, 4 * c:4 * c + 4].reshape(2, 4, 512, 512)
        m['st_b'] = st_b[4 * c:4 * c + 4].reshape(4, 2048, 512)
        m['ptab'] = ptab[4 * c:4 * c + 4]
        in_maps.append({k: m[k] for k in pg.din_names})
    res = run_bass_kernel_spmd(pg.nc, in_maps, core_ids=list(range(8)))
    R = res.results
    cat = lambda k, ax=0: np.stack([np.asarray(r[k]) for r in R], axis=ax)
    y_prompt = cat('y_p').reshape(8, S, D)
    y_sample = cat('y_s').reshape(32, 4, D)
    pr_c = cat('o_pc', 1).reshape(2, 8, S, 2, 4, 64)
    pr_s = cat('o_ps', 1).reshape(2, 8, S, 2, 4, 64)
    pr_w = cat('o_pw', 1).reshape(2, 8, 512, 2, 4, 64)
    pr_b = cat('o_pb').reshape(8, S, 2, 4, 64)
    sm_c = cat('o_sc', 1).reshape(2, 32, 4, 2, 4, 64)
    sm_s = cat('o_ss', 1).reshape(2, 32, 4, 2, 4, 64)
    sm_w = cat('o_sw', 1).reshape(2, 32, 512, 2, 4, 64)
    sm_b = cat('o_sb').reshape(32, 2048, 2, 4, 64)
    return (y_prompt, y_sample, pr_c, pr_s, pr_w, pr_b, sm_c, sm_s, sm_w, sm_b)
```

```python
import math
import numpy as np
import ml_dtypes
import concourse.bass as bass
import concourse.mybir as mybir
from concourse.bass_utils import run_bass_kernel_spmd

F32 = mybir.dt.float32
BF16 = mybir.dt.bfloat16
I32 = mybir.dt.int32
AF = mybir.ActivationFunctionType
ALU = mybir.AluOpType
AX = mybir.AxisListType

D = 1024
S = 2048
NTP = 16
NT = 17
NSAMP = 16
NTOK = S + NSAMP
EPS = 1e-6
SCALE = 0.125
MASKV = -30000.0
NDS = 40
PAST = 8192
NPHYS = 2560

STAGE = 99


def tile_info(t):
    if t < NTP:
        return 128, t * 128
    return NSAMP, S


class Sch:
    def __init__(self, nc):
        self.nc = nc
        self.E = {'pe': nc.tensor, 'act': nc.scalar, 'dve': nc.vector, 'pool': nc.gpsimd, 'sp': nc.sync}
        self.sems = {}
        self.ccnt = {}
        for e in ('pe', 'act', 'dve', 'pool'):
            self.sems['c_' + e] = nc.alloc_semaphore('c_' + e)
            self.ccnt[e] = 0
        self.dcnt = [0] * NDS
        for i in range(NDS):
            self.sems['d%d' % i] = nc.alloc_semaphore('d%d' % i)
        self.di = 0
        self.lastw = {}
        self.readers = {}
        self.waited = {e: {} for e in self.E}
        self.n_inst = 0
        self.alias = {}

    def _need(self, eng, reads, writes):
        toks = []
        for r in reads:
            t = self.lastw.get(r)
            if t is not None and not (t[2] == 'pe' and eng == 'pe'):
                toks.append(t)
        for w in writes:
            t = self.lastw.get(w)
            if t is not None and t[2] != eng:
                toks.append(t)
            rd = self.readers.get(w)
            if rd:
                for sem, (val, src) in rd.items():
                    if src != eng:
                        toks.append((sem, val, src))
        return toks

    def _wait(self, eng, toks):
        wd = self.waited[eng]
        best = {}
        for sem, val, src in toks:
            if wd.get(sem, 0) >= val:
                continue
            if best.get(sem, 0) < val:
                best[sem] = val
        for sem, val in best.items():
            self.E[eng].wait_ge(self.sems[sem], val)
            wd[sem] = val
            self.n_inst += 1

    def _record(self, tok, reads, writes):
        for r in reads:
            d = self.readers.setdefault(r, {})
            d[tok[0]] = (tok[1], tok[2])
        for w in writes:
            self.lastw[w] = tok
            self.readers[w] = {}

    def op(self, eng, fn, reads=(), writes=()):
        reads = [self.alias.get(r, r) for r in reads]
        writes = [self.alias.get(w, w) for w in writes]
        self._wait(eng, self._need(eng, reads, writes))
        inst = fn(self.E[eng])
        self.ccnt[eng] += 1
        inst.then_inc(self.sems['c_' + eng], 1)
        self._record(('c_' + eng, self.ccnt[eng], eng), reads, writes)
        self.n_inst += 1

    def dma(self, q, out, in_, reads=(), writes=(), **kw):
        reads = [self.alias.get(r, r) for r in reads]
        writes = [self.alias.get(w, w) for w in writes]
        i = self.di
        self.di = (self.di + 1) % NDS
        name = 'd%d' % i
        toks = self._need(q, reads, writes)
        if self.dcnt[i] > 0:
            toks.append((name, self.dcnt[i], 'dma'))
        self._wait(q, toks)
        inst = self.E[q].dma_start(out=out, in_=in_, **kw)
        self.dcnt[i] += 16
        inst.then_inc(self.sems[name], 16)
        self._record((name, self.dcnt[i], 'dma'), reads, writes)
        self.n_inst += 1

    def idma(self, out, in_, idx_ap, reads=(), writes=()):
        q = 'pool'
        i = self.di
        self.di = (self.di + 1) % NDS
        name = 'd%d' % i
        toks = self._need(q, reads, writes)
        if self.dcnt[i] > 0:
            toks.append((name, self.dcnt[i], 'dma'))
        self._wait(q, toks)
        inst = self.E[q].indirect_dma_start(out=out, out_offset=None, in_=in_,
                                            in_offset=bass.IndirectOffsetOnAxis(ap=idx_ap, axis=0))
        self.dcnt[i] += 16
        inst.then_inc(self.sems[name], 16)
        self._record((name, self.dcnt[i], 'dma'), reads, writes)
        self.n_inst += 1

    def raw_tok_wait(self, eng, res_list):
        toks = []
        for r in res_list:
            t = self.lastw.get(r)
            if t is not None:
                toks.append(t)
        self._wait(eng, toks)

    def barrier(self):
        toks = []
        for e in ('pe', 'act', 'dve', 'pool'):
            if self.ccnt[e] > 0:
                toks.append(('c_' + e, self.ccnt[e], 'x'))
        for i in range(NDS):
            if self.dcnt[i] > 0:
                toks.append(('d%d' % i, self.dcnt[i], 'dma'))
        for e in self.E:
            self._wait(e, toks)


class Ring:
    def __init__(self, items):
        self.items = items
        self.i = 0

    def next(self):
        it = self.items[self.i]
        self.i = (self.i + 1) % len(self.items)
        return it


def _bucket_table(nmax):
    import jax
    import jax.numpy as jnp
    cpu = jax.devices('cpu')[0]
    with jax.default_device(cpu):
        n = jnp.arange(nmax, dtype=jnp.int32)
        exact = 16
        nf = jnp.maximum(n, 1).astype(jnp.float32)
        big = exact + (jnp.log(nf / exact) / math.log(4096 / exact) * (32 - exact)).astype(jnp.int32)
        b = jnp.where(n < exact, n, jnp.minimum(big, 31))
        return np.asarray(b).astype(np.int64)


def host_constants():
    c = {}
    bk = _bucket_table(8448)

    def onehot(ns, valid):
        L = len(ns)
        oh = np.zeros((33, L), np.float32)
        nn = np.clip(ns, 0, len(bk) - 1)
        idx = np.where(valid, bk[nn], 32)
        oh[idx, np.arange(L)] = 1.0
        return oh

    n = np.arange(8448) - 127
    c['oh_slc'] = onehot(n, n >= 0)
    n = np.arange(768) - 127
    c['oh_win'] = onehot(n, (n >= 0) & (n <= 512))
    n = np.arange(4096) - 2063
    c['oh_cmp'] = onehot(n, n >= 0)
    for d, win, L in ((1, 128, 384), (4, 512, 768), (16, 2048, 2304)):
        n = np.arange(L) - 127
        c['oh_d%d' % d] = onehot(n, (n >= 0) & (n <= win) & (n % d == 0))
    c['identb'] = np.eye(128, dtype=np.float32).astype(ml_dtypes.bfloat16)
    c['identf'] = np.eye(128, dtype=np.float32)
    c['antib'] = np.ascontiguousarray(np.eye(128, dtype=np.float32)[::-1]).astype(ml_dtypes.bfloat16)
    t = np.arange(S)
    cur = t // 64
    j = np.arange(32)[None, :]
    valid = (j <= cur[:, None])
    forced = (j == 0) | (j == cur[:, None]) | (j == cur[:, None] - 1)
    A = valid.astype(np.float32)
    Bc = np.where(valid, np.where(forced, 1000.0, 0.0), -1.0).astype(np.float32)
    c['selA'] = np.ascontiguousarray(A.reshape(16, 128, 32).transpose(1, 0, 2))
    c['selB'] = np.ascontiguousarray(Bc.reshape(16, 128, 32).transpose(1, 0, 2))
    cs = np.arange(128) * 16
    ss = np.arange(32) * 64
    ov = np.clip(np.minimum(cs[:, None] + 32, ss[None, :] + 64) - np.maximum(cs[:, None], ss[None, :]), 0, None) / 32.0
    ov[127] = 0
    c['ov_p'] = ov.astype(np.float32).astype(ml_dtypes.bfloat16)
    es = np.zeros((32, 16, 128), np.float32)
    for a in range(16):
        es[2 * a, a, :64] = 1
        es[2 * a + 1, a, 64:] = 1
    c['esel'] = es.astype(ml_dtypes.bfloat16)
    cs = np.arange(512) * 16
    ss = np.arange(128) * 64
    ovs = np.clip(np.minimum(cs[:, None] + 32, ss[None, :] + 64) - np.maximum(cs[:, None], ss[None, :]), 0, None) / 32.0
    ovs[511] = 0
    c['ov_s'] = np.ascontiguousarray(ovs.reshape(4, 128, 128).transpose(1, 0, 2)).astype(np.float32).astype(ml_dtypes.bfloat16)
    sb_ = np.zeros((4, 136), np.float32)
    sb_[:, [0, 127, 128]] = 1000.0
    sb_[:, 129:] = -1.0
    c['selB_s'] = sb_
    e2 = np.zeros((2, 128), np.float32)
    e2[0, :64] = 1
    e2[1, 64:] = 1
    c['e2'] = e2.astype(ml_dtypes.bfloat16)
    c['pidx'] = np.arange(128, dtype=np.float32).reshape(128, 1)
    return c


class Prog:
    def __init__(self):
        self.nc = bass.Bass("TRN2", target_bir_lowering=False)
        self.sch = Sch(self.nc)
        self.din_names = []

    def din(self, name, shape, dt=F32):
        self.din_names.append(name)
        return self.nc.dram_tensor(name, list(shape), dt, kind="ExternalInput").ap()

    def dout(self, name, shape, dt=F32):
        return self.nc.dram_tensor(name, list(shape), dt, kind="ExternalOutput").ap()

    def dint(self, name, shape, dt=F32):
        return self.nc.dram_tensor(name, list(shape), dt, kind="Internal").ap()

    def sb(self, name, shape, dt=F32):
        return self.nc.alloc_sbuf_tensor(name, list(shape), dt).ap()


def AP(ap, off, dims):
    return bass.AP(ap.tensor, ap.offset + off, [list(d) for d in dims])


def build():
    pg = Prog()
    nc = pg.nc
    sch = pg.sch
    hc = host_constants()

    x_p = pg.din('x_p', [S, D])
    x_s = pg.din('x_s', [NSAMP, D])
    p_p = pg.din('p_p', [4, S, 256])
    p_s = pg.din('p_s', [4, NSAMP, 256])
    cache_cmp = pg.din('cache_cmp', [2, NPHYS * 128, 512])
    cache_slc = pg.din('cache_slc', [2, NPHYS * 128, 512])
    st_a = pg.din('st_a', [2, 4, 512, 512])
    st_b = pg.din('st_b', [4, 2048, 512])
    ptab = pg.din('ptab', [4, 64], I32)
    rel_bias = pg.din('rel_bias', [32, 16])
    norm_g = pg.din('norm_g', [4, D])
    a_w_in = pg.din('a_w_in', [2, D, 5680])
    a_gate_b = pg.din('a_gate_b', [2, 48])
    a_qk_norm = pg.din('a_qk_norm', [2, 4, 64])
    a_phi_w1 = pg.din('a_phi_w1', [2, 2, 2, 1024, 256])
    a_phi_b1 = pg.din('a_phi_b1', [2, 2, 256])
    a_phi_w2 = pg.din('a_phi_w2', [2, 2, 256, 64])
    a_phi_b2 = pg.din('a_phi_b2', [2, 2, 64])
    a_w_out = pg.din('a_w_out', [2, D, D])
    kv_norm_g = pg.din('kv_norm_g', [D])
    b_w_kv = pg.din('b_w_kv', [D, 512])
    b_k_norm = pg.din('b_k_norm', [64])
    b_w_in = pg.din('b_w_in', [2, D, 4096])
    b_q_norm = pg.din('b_q_norm', [2, 3, 64])
    b_w_out = pg.din('b_w_out', [2, D, D])
    ple_w = pg.din('ple_w', [4, 256, D])
    ple_gate_w = pg.din('ple_gate_w', [4, D, D])
    cdram = {}
    for k, v in hc.items():
        dt = BF16 if v.dtype == ml_dtypes.bfloat16 else F32
        cdram[k] = pg.din('c_' + k, v.shape, dt)

    y_p = pg.dout('y_p', [S, D])
    y_s = pg.dout('y_s', [NSAMP, D])
    o_pc = pg.dout('o_pc', [2, S, 512])
    o_ps = pg.dout('o_ps', [2, S, 512])
    o_pw = pg.dout('o_pw', [2, 512, 512])
    o_pb = pg.dout('o_pb', [S, 512])
    o_sc = pg.dout('o_sc', [2, NSAMP, 512])
    o_ss = pg.dout('o_ss', [2, NSAMP, 512])
    o_sw = pg.dout('o_sw', [2, 4, 512, 512])
    o_sb = pg.dout('o_sb', [4, 2048, 512])

    gz_d = pg.dint('gz_d', [NTOK, 3072])
    fd = {}
    for k in ('slc', 'win', 'cmp', 'd1', 'd4', 'd16'):
        fd[k] = pg.dint('fd_' + k, [16, hc['oh_' + k].shape[1]], BF16)

    def psum(name, dt=F32, n=512):
        return nc.alloc_psum_tensor(name, [128, n], dt).ap()
    G = Ring([(psum('G0'), 'G0'), (psum('G1'), 'G1')])
    SS = Ring([(psum('S0'), 'S0'), (psum('S1'), 'S1')])
    OO = Ring([(psum('O0'), 'O0'), (psum('O1'), 'O1')])
    SS4 = Ring(SS.items + G.items)
    TB = (psum('TB', BF16, 1024), 'TB')
    TF = (psum('TF'), 'TF')

    identb = pg.sb('identb', [128, 128], BF16)
    antib = pg.sb('antib', [128, 128], BF16)
    tab33 = pg.sb('tab33', [33, 16], F32)
    xT = pg.sb('xT', [128, 8, NTOK], BF16)
    qT = pg.sb('qT', [128, 8, NTOK], BF16)
    ksT = pg.sb('ksT', [128, 2, S], BF16)
    kwT = pg.sb('kwT', [128, 2, S], BF16)
    vs_aug = pg.sb('vs_aug', [128, 16, 4, 65], BF16)
    vw_aug = pg.sb('vw_aug', [128, 16, 4, 65], BF16)
    gates = pg.sb('gates', [128, NT, 48], F32)
    wbf_i = [0]
    hin_i = [0]
    xnb = pg.sb('xnb', [128, D], BF16)
    class Prox:
        def __init__(self, name, base):
            self.name = name
            self.base = base
            self.members = [(base, name)]
            self.i = 0

        def cur(self):
            return self.members[self.i % len(self.members)]

        def __getitem__(self, k):
            return self.cur()[0][k]

        @property
        def tensor(self):
            return self.cur()[0].tensor

        @property
        def offset(self):
            return self.cur()[0].offset

        def rot(self):
            self.i += 1
            sch.alias[self.name] = self.cur()[1]

        def set_members(self, extra):
            self.members = [(self.base, self.name)] + list(extra)
            self.i = 0
            sch.alias[self.name] = self.name

    sq = Prox('sq', pg.sb('sq', [128, 512], F32))
    tmpf = Prox('tmpf', pg.sb('tmpf', [128, 512], F32))
    qnb = Prox('qnb', pg.sb('qnb', [128, 512], BF16))
    small = Prox('small', pg.sb('small', [128, 64], F32))
    PROXIES = [sq, tmpf, qnb, small]

    def rot_scratch():
        for p_ in PROXIES:
            p_.rot()
    rowb = [pg.sb('rowb%d' % i, [128, 512], F32) for i in range(2)]
    rowb_i = [0]
    gq = pg.sb('gq', [128, 4, 64], F32)
    gbq = pg.sb('gbq', [128, 3, 64], F32)
    gateb = pg.sb('gateb', [128, 48], F32)
    kccT = pg.sb('kccT', [128, 2, 128], BF16)
    vcc_aug = pg.sb('vcc_aug', [128, 4, 97], BF16)
    w2s = pg.sb('w2s', [128, 2, 64], BF16)
    b1s = pg.sb('b1s', [128, 2], F32)
    b2s = pg.sb('b2s', [128, 64], F32)
    hidT = pg.sb('hidT', [128, 2, 128], BF16)
    ovp = pg.sb('ovp', [128, 32], BF16)
    Pt_i = [0]
    yacc = pg.sb('yacc', [128, 256], F32)
    ybf = pg.sb('ybf', [128, 256], BF16)
    osb = pg.sb('osb', [128, 4, 100], F32)
    rden = pg.sb('rden', [128, 4], F32)
    sc1 = pg.sb('sc1', [128, 32], F32)
    sc2 = pg.sb('sc2', [128, 32], F32)
    mx8 = pg.sb('mx8', [128, 16], F32)
    mnegb = pg.sb('mnegb', [128, 32], BF16)
    mnegT = pg.sb('mnegT', [32, 128], BF16)
    h2_i = [0]

    from contextlib import ExitStack
    LT = {}
    _uid = [0]

    class Scope:
        def __init__(self, spec, name=None):
            self.spec = spec
            self.es = ExitStack()
            self.name = name

        def __enter__(self):
            if self.name:
                self.es.enter_context(nc.named_scope(self.name))
            for name, shape, dt in self.spec:
                _uid[0] += 1
                h = self.es.enter_context(nc.sbuf_tensor('%s_u%d' % (name, _uid[0]), list(shape), dt))
                LT[name] = h.ap() if hasattr(h, 'ap') and callable(getattr(h, 'ap')) else h
            names = [n for n, _, _ in self.spec]
            for p_ in PROXIES:
                ex = [(LT[n], n) for n in names if n.startswith(p_.name + '_x')]
                if ex:
                    p_.set_members(ex)
            if 'tbring' in names:
                cur_TB[0] = Ring([TB, (TF[0].bitcast(BF16), 'TF')])
                cur_G[0] = Ring(G.items + SS.items + OO.items)
            return self

        def __exit__(self, *a):
            sch.barrier()
            names = [n for n, _, _ in self.spec]
            for p_ in PROXIES:
                if any(n.startswith(p_.name + '_x') for n in names):
                    p_.set_members([])
            if 'tbring' in names:
                cur_TB[0] = Ring([TB])
                cur_G[0] = G
            self.es.close()
            return False

    cur_TB = [None]
    cur_G = [G]
    SC_NORM = [('gt', [128, D], F32), ('junk', [128, D], F32), ('hin0', [128, D], F32), ('hin1', [128, D], F32)]
    SC_PROJ = [('wst', [128, 8, 512], F32), ('wbf', [128, 2, 8, 512], BF16),
               ('kcT', [128, 2, S], BF16), ('vcT', [128, 2, S], BF16),
               ('sq_x1', [128, 512], F32), ('sq_x2', [128, 512], F32), ('tmpf_x1', [128, 512], F32), ('tmpf_x2', [128, 512], F32),
               ('qnb_x1', [128, 512], BF16), ('qnb_x2', [128, 512], BF16), ('small_x1', [128, 64], F32), ('small_x2', [128, 64], F32), ('tbring', [128, 1], F32)]
    SC_ATT = [('G1_0', [128, 4, 2048], BF16), ('G1_1', [128, 4, 2048], BF16), ('G2_0', [128, 4, 640], BF16),
              ('G2_1', [128, 4, 640], BF16), ('G3_0', [128, 4, 128], BF16), ('G3_1', [128, 4, 128], BF16),
              ('gzt0', [128, 3, 256], F32), ('gzt1', [128, 3, 256], F32), ('selA', [128, 16, 32], F32), ('selB', [128, 16, 32], F32),
              ('esel', [32, 16, 128], BF16), ('Pt0', [128, 512], BF16), ('Pt1', [128, 512], BF16), ('Pt2', [128, 512], BF16), ('Pt3', [128, 512], BF16)]
    SC_OUT = [('wst', [128, 8, 512], F32), ('wo_bf', [128, 8, D], BF16), ('wg_bf', [128, 8, D], BF16), ('wp_bf', [128, 2, D], BF16),
              ('h1', [128, D], F32), ('h1b', [128, D], BF16), ('h1T', [128, 8, 128], BF16), ('sg', [128, D], F32),
              ('h2_0', [128, D], F32), ('hin0', [128, D], F32),
              ('pin', [128, 256], F32), ('pinb', [128, 256], BF16), ('pT', [128, 2, 128], BF16)]

    cur_TB[0] = Ring([TB])
    sch.dma('sp', identb[:], cdram['identb'][:], writes=['identb'])
    sch.dma('sp', antib[:], cdram['antib'][:], writes=['identb'])
    sch.dma('sp', ovp[:], cdram['ov_p'][:], writes=['ovp'])
    sch.op('dve', lambda e: e.memset(tab33[:], MASKV), writes=['tab33'])
    sch.dma('sp', tab33[0:32, :], rel_bias[:], writes=['tab33'])
    with Scope([('ohs', [33, 512], F32), ('fstage', [16, 512], BF16)]):
        ohs, fstage = LT['ohs'], LT['fstage']
        for k in ('slc', 'win', 'cmp', 'd1', 'd4', 'd16'):
            L = hc['oh_' + k].shape[1]
            for c0 in range(0, L, 512):
                cw = min(512, L - c0)
                sch.dma('sp', ohs[:, :cw], cdram['oh_' + k][:, c0:c0 + cw], writes=['ohs'])
                g, gr = G.next()
                sch.op('pe', lambda e: e.matmul(g[0:16, :cw], lhsT=tab33[:, :], rhs=ohs[:, :cw], start=True, stop=True),
                       reads=['tab33', 'ohs'], writes=[gr])
                sch.op('act', lambda e: e.copy(out=fstage[:, :cw], in_=g[0:16, :cw]), reads=[gr], writes=['fstage'])
                sch.dma('sp', fd[k][:, c0:c0 + cw], fstage[:, :cw], reads=['fstage'], writes=['fd_' + k])

    def bc_row(ap1, P=128):
        n = ap1.shape[-1]
        return AP(ap1, 0, [[0, P], [1, n]])

    def load_w(W2, c0, cw, nk=8):
        wst = LT['wst']
        wbf = [LT['wbf'][:, 0], LT['wbf'][:, 1]]
        N = W2.shape[1]
        i = wbf_i[0]
        wbf_i[0] ^= 1
        nst = wst.shape[1]
        for k0 in range(0, nk, nst):
            kn = min(nst, nk - k0)
            src = AP(W2, c0 + k0 * 128 * N, [[N, 128], [128 * N, kn], [1, cw]])
            sch.dma('sp', wst[:, :kn, :cw], src, writes=['wst'])
            sch.op('pool', lambda e: e.tensor_copy(out=wbf[i][:, k0:k0 + kn, :cw], in_=wst[:, :kn, :cw]),
                   reads=['wst'], writes=['wbf%d' % i])
        return wbf[i], 'wbf%d' % i

    def gemm(lhsT_of, lres, W2, chunks, tiles, consumer, nk=8):
        seq = [(ci, t) for ci in range(len(chunks)) for t in tiles]
        wstate = {}

        def issue(idx):
            ci, t = seq[idx]
            c0, cw = chunks[ci]
            if ci not in wstate:
                wstate[ci] = load_w(W2, c0, cw, nk)
            wb, wres = wstate[ci]
            P, tok0 = tile_info(t)
            g, gr = cur_G[0].next()
            for kc in range(nk):
                sch.op('pe', lambda e: e.matmul(g[:P, :cw], lhsT=lhsT_of(kc, t), rhs=wb[:, kc, :cw],
                                                start=(kc == 0), stop=(kc == nk - 1)),
                       reads=[lres, wres], writes=[gr])
            return (ci, t, P, tok0, g, gr)
        cur = issue(0)
        for idx in range(len(seq)):
            nxt = issue(idx + 1) if idx + 1 < len(seq) else None
            rot_scratch()
            consumer(*cur)
            cur = nxt

    def rstd_from_ss(P, n, ss_ap, out_ap, inv):
        sch.op('dve', lambda e: e.tensor_scalar(out=ss_ap, in0=ss_ap, scalar1=inv, scalar2=EPS,
                                                op0=ALU.mult, op1=ALU.add), reads=['small'], writes=['small'])
        sch.op('act', lambda e: e.sqrt(out=ss_ap, in_=ss_ap), reads=['small'], writes=['small'])
        sch.op('dve', lambda e: e.reciprocal(out=out_ap, in_=ss_ap), reads=['small'], writes=['small'])

    def headnorm(P, src3, nh, gain3, out3, out_res, src_res, extra_reads=()):
        sq3 = sq[:P, :nh * 64].rearrange("p (h d) -> p h d", d=64)
        sch.op('act', lambda e: e.activation(out=sq3, in_=src3, func=AF.Square), reads=[src_res], writes=['sq'])
        sch.op('dve', lambda e: e.tensor_reduce(out=small[:P, 0:nh], in_=sq3, axis=AX.X, op=ALU.add),
               reads=['sq'], writes=['small'])
        rstd_from_ss(P, nh, small[:P, 0:nh], small[:P, 16:16 + nh], 1.0 / 64)
        t3 = tmpf[:P, :nh * 64].rearrange("p (h d) -> p h d", d=64)
        sch.op('dve', lambda e: e.tensor_tensor(out=t3, in0=src3, in1=small[:P, 16:16 + nh].to_broadcast((P, nh, 64)) if False else AP(small, 16, [[64, P], [1, nh], [0, 64]]),
                                                op=ALU.mult), reads=[src_res, 'small'], writes=['tmpf'])
        sch.op('dve', lambda e: e.tensor_tensor(out=out3, in0=t3, in1=gain3, op=ALU.mult),
               reads=['tmpf'] + list(extra_reads), writes=[out_res])

    def transposes_to(P, src2, src_res, nblk, dst3, dst_res):
        tb, tbr = cur_TB[0].next()
        for b in range(nblk):
            sch.op('pe', lambda e: e.transpose(out=tb[:, b * 128:b * 128 + P], in_=src2[:, b * 128:(b + 1) * 128],
                                               identity=identb[:P, :P]),
                   reads=[src_res, 'identb'], writes=[tbr])
        tb3 = AP(tb, 0, [[1024, 128], [128, nblk], [1, P]])
        sch.op('act', lambda e: e.copy(out=dst3, in_=tb3), reads=[tbr], writes=[dst_res])

    def phase_norm(src_of, gain_ap):
        gt = LT['gt']
        junk = LT['junk']
        hin = [LT['hin0'], LT['hin1']]
        sch.dma('sp', gt[:], bc_row(gain_ap), writes=['gt'])
        for t in range(NT):
            P, tok0 = tile_info(t)
            i = hin_i[0]
            hin_i[0] ^= 1
            hr = 'hin%d' % i
            sch.dma('sp', hin[i][:P], src_of(t), reads=['hd%d' % t], writes=[hr])
            sch.op('act', lambda e: e.activation(out=junk[:P], in_=hin[i][:P], func=AF.Square,
                                                 accum_out=small[:P, 0:1]), reads=[hr], writes=['junk', 'small'])
            rstd_from_ss(P, 1, small[:P, 0:1], small[:P, 1:2], 1.0 / D)
            sch.op('dve', lambda e: e.scalar_tensor_tensor(out=xnb[:P], in0=hin[i][:P], scalar=small[:P, 1:2],
                                                           in1=gt[:P], op0=ALU.mult, op1=ALU.mult),
                   reads=[hr, 'small', 'gt'], writes=['xnb'])
            transposes_to(P, xnb[:P], 'xnb', 8, xT[:, :, tok0:tok0 + P], 'xT')

    def h_src(layer):
        def f(t):
            P, tok0 = tile_info(t)
            if layer == 0:
                return x_p[tok0:tok0 + P, :] if t < NTP else x_s[:, :]
            return y_p[tok0:tok0 + P, :] if t < NTP else y_s[:, :]
        return f

    def a_project(li):
        W = a_w_in[li]
        kcT = LT['kcT']
        vcT = LT['vcT']
        sch.dma('sp', gq[:].rearrange("p a d -> p (a d)"), bc_row(a_qk_norm[li].rearrange("a d -> (a d)")), writes=['gq'])
        sch.op('dve', lambda e: e.tensor_scalar(out=gq[:, 0, :], in0=gq[:, 0, :], scalar1=SCALE, scalar2=None,
                                                op0=ALU.mult), reads=['gq'], writes=['gq'])
        sch.dma('sp', gateb[:], bc_row(a_gate_b[li]), writes=['gateb'])
        chunks = [(0, 512), (512, 512), (1024, 512), (1536, 512), (2048, 512), (2560, 48)] + \
                 [(2608 + 512 * z, 512) for z in range(6)]

        def consumer(ci, t, P, tok0, g, gr):
            if ci < 2:
                src3 = g[:P, :512].rearrange("p (h d) -> p h d", d=64)
                gain3 = AP(gq, 0, [[256, P], [0, 8], [1, 64]])
                out4 = AP(qnb, 0, [[512, P], [64, 2], [128, 4], [1, 64]])
                sq3 = sq[:P, :512].rearrange("p (h d) -> p h d", d=64)
                sch.op('act', lambda e: e.activation(out=sq3, in_=src3, func=AF.Square), reads=[gr], writes=['sq'])
                sch.op('dve', lambda e: e.tensor_reduce(out=small[:P, 0:8], in_=sq3, axis=AX.X, op=ALU.add),
                       reads=['sq'], writes=['small'])
                rstd_from_ss(P, 8, small[:P, 0:8], small[:P, 16:24], 1.0 / 64)
                t3 = tmpf[:P, :512].rearrange("p (h d) -> p h d", d=64)
                sch.op('dve', lambda e: e.tensor_tensor(out=t3, in0=src3, in1=AP(small, 16, [[64, P], [1, 8], [0, 64]]),
                                                        op=ALU.mult), reads=[gr, 'small'], writes=['tmpf'])
                t4 = tmpf[:P, :512].rearrange("p (a r d) -> p a r d", a=2, r=4)
                g4 = AP(gq, 0, [[256, P], [0, 2], [0, 4], [1, 64]])
                sch.op('dve', lambda e: e.tensor_tensor(out=out4, in0=t4, in1=g4, op=ALU.mult),
                       reads=['tmpf', 'gq'], writes=['qnb'])
                transposes_to(P, qnb[:P], 'qnb', 4, qT[:, 4 * ci:4 * ci + 4, tok0:tok0 + P], 'qT')
            elif ci < 5:
                kind = ci - 2
                ri = rowb_i[0]
                rowb_i[0] ^= 1
                rb = rowb[ri]
                rr = 'rowb%d' % ri
                if kind == 0:
                    sch.op('act', lambda e: e.copy(out=rb[:P, :], in_=g[:P, :512]), reads=[gr], writes=[rr])
                else:
                    src3 = g[:P, 0:256].rearrange("p (h d) -> p h d", d=64)
                    gain3 = AP(gq, (1 + kind) * 64, [[256, P], [0, 4], [1, 64]])
                    out3 = rb[:P, 0:256].rearrange("p (h d) -> p h d", d=64)
                    headnorm(P, src3, 4, gain3, out3, rr, gr, extra_reads=['gq'])
                    sch.op('act', lambda e: e.copy(out=rb[:P, 256:512], in_=g[:P, 256:512]), reads=[gr], writes=[rr])
                if t < NTP:
                    if kind == 0:
                        sch.dma('pool', o_pc[li, tok0:tok0 + P, :], rb[:P, :], reads=[rr], writes=['o_pc'])
                    elif kind == 1:
                        sch.dma('pool', o_ps[li, tok0:tok0 + P, :], rb[:P, :], reads=[rr], writes=['o_ps'])
                    elif t >= 12:
                        sch.dma('pool', o_pw[li, tok0 - 1536:tok0 - 1536 + P, :], rb[:P, :], reads=[rr], writes=['o_pw'])
                else:
                    if kind == 0:
                        sch.dma('pool', o_sc[li, :, :], rb[:P, :], reads=[rr], writes=['o_sc'])
                    elif kind == 1:
                        sch.dma('pool', o_ss[li, :, :], rb[:P, :], reads=[rr], writes=['o_ss'])
                    else:
                        for b in range(4):
                            sch.dma('pool', o_sw[li, b, 508:512, :], rb[4 * b:4 * b + 4, :], reads=[rr], writes=['o_sw'])
                if t < NTP:
                    sch.op('dve', lambda e: e.tensor_copy(out=qnb[:P, :], in_=rb[:P, :]), reads=[rr], writes=['qnb'])
                    if kind == 0:
                        transposes_to(P, qnb[:P, 0:256], 'qnb', 2, kcT[:, :, tok0:tok0 + P], 'kcT')
                        transposes_to(P, qnb[:P, 256:512], 'qnb', 2, vcT[:, :, tok0:tok0 + P], 'vcT')
                    else:
                        kt, ktr, va, var = (ksT, 'ksT', vs_aug, 'vs_aug') if kind == 1 else (kwT, 'kwT', vw_aug, 'vw_aug')
                        transposes_to(P, qnb[:P, 0:256], 'qnb', 2, kt[:, :, tok0:tok0 + P], ktr)
                        sch.op('pool', lambda e: e.tensor_copy(out=va[:, t, :, 0:64],
                                                                in_=qnb[:, 256:512].rearrange("p (g d) -> p g d", d=64)),
                               reads=['qnb'], writes=[var])
            elif ci == 5:
                sch.op('dve', lambda e: e.tensor_tensor(out=gates[:P, t, :], in0=g[:P, :48], in1=gateb[:P, :], op=ALU.add),
                       reads=[gr, 'gateb'], writes=['gates'])
                sch.op('act', lambda e: e.activation(out=gates[:P, t, :], in_=gates[:P, t, :], func=AF.Sigmoid),
                       reads=['gates'], writes=['gates'])
            else:
                zc = ci - 6
                br, hh = zc // 2, zc % 2
                ri = rowb_i[0]
                rowb_i[0] ^= 1
                rb = rowb[ri]
                rr = 'rowb%d' % ri
                sch.op('act', lambda e: e.activation(out=tmpf[:P, :], in_=g[:P, :512], func=AF.Silu), reads=[gr], writes=['tmpf'])
                gb = AP(gates, t * 48 + br * 16 + hh * 8, [[NT * 48, P], [1, 8], [0, 64]])
                sch.op('dve', lambda e: e.tensor_tensor(out=rb[:P, :].rearrange("p (h d) -> p h d", d=64),
                                                        in0=tmpf[:P, :].rearrange("p (h d) -> p h d", d=64), in1=gb, op=ALU.mult),
                       reads=['tmpf', 'gates'], writes=[rr])
                sch.dma('pool', gz_d[tok0:tok0 + P, zc * 512:(zc + 1) * 512], rb[:P, :], reads=[rr], writes=['gz_d'])

        gemm(lambda kc, t: xT[:, kc, tile_info(t)[1]:tile_info(t)[1] + tile_info(t)[0]], 'xT', W, chunks, list(range(NT)), consumer)

    def compress_prompt(li):
        wst = LT['wst']
        w1s = LT['wbf'].rearrange("p a k (b h) -> p (a k b) h", h=256)
        kcT = LT['kcT']
        vcT = LT['vcT']
        sch.op('dve', lambda e: e.memset(kccT[:], 0.0), writes=['kccT'])
        sch.op('dve', lambda e: e.memset(vcc_aug[:], 0.0), writes=['vcc_aug'])
        for kv in range(2):
            srcT, sres = (kcT, 'kcT') if kv == 0 else (vcT, 'vcT')
            for half in range(2):
                for ab in range(2):
                    src = AP(a_phi_w1[li, kv, ab], 0, [[256, 64], [64 * 256, 16], [1, 256]])
                    sch.dma('sp', wst[half * 64:half * 64 + 64, 0:8, :].rearrange("p a (b h) -> p (a b) h", h=256), src, writes=['wst'])
                    sch.op('pool', lambda e: e.tensor_copy(out=w1s[half * 64:half * 64 + 64, ab * 16:(ab + 1) * 16, :],
                                                           in_=wst[half * 64:half * 64 + 64, 0:8, :].rearrange("p a (b h) -> p (a b) h", h=256)),
                           reads=['wst'], writes=['wbf0', 'wbf1'])
            sch.dma('sp', wst[:, 0, 0:128].rearrange("p (a d) -> p a d", d=64),
                    AP(a_phi_w2[li, kv], 0, [[64, 128], [128 * 64, 2], [1, 64]]), writes=['wst'])
            sch.op('pool', lambda e: e.tensor_copy(out=w2s[:], in_=wst[:, 0, 0:128].rearrange("p (a d) -> p a d", d=64)),
                   reads=['wst'], writes=['w2s'])
            for hh_ in range(2):
                sch.dma('sp', b1s[:, hh_:hh_ + 1], AP(a_phi_b1[li, kv], hh_ * 128, [[1, 128], [1, 1]]), writes=['b1s'])
            sch.dma('sp', b2s[:], bc_row(a_phi_b2[li, kv]), writes=['b2s'])
            for g in range(4):
                half, ch = g % 2, g // 2
                hs = slice(half * 64, half * 64 + 64)
                for hh in range(2):
                    ps, psr = SS.next()
                    n = 0
                    for ab in range(2):
                        for s in range(16):
                            rhs = AP(srcT, half * 64 * (2 * S) + ch * S + 16 * ab + s, [[2 * S, 64], [16, 127]])
                            sch.op('pe', lambda e: e.matmul(ps[:, 0:127], lhsT=w1s[hs, ab * 16 + s, hh * 128:(hh + 1) * 128],
                                                            rhs=rhs, start=(n == 0), stop=(n == 31)),
                                   reads=['wbf0', 'wbf1', sres], writes=[psr])
                            n += 1
                    sch.op('act', lambda e: e.activation(out=hidT[:, hh, 0:127], in_=ps[:, 0:127], func=AF.Silu,
                                                         bias=b1s[:, hh:hh + 1]), reads=[psr, 'b1s'], writes=['hidT'])
                po, por = OO.next()
                for hh in range(2):
                    sch.op('pe', lambda e: e.matmul(po[0:127, 0:64], lhsT=hidT[:, hh, 0:127], rhs=w2s[:, hh, :],
                                                    start=(hh == 0), stop=(hh == 1)), reads=['hidT', 'w2s'], writes=[por])
                sch.op('dve', lambda e: e.tensor_tensor(out=tmpf[0:127, 0:64], in0=po[0:127, 0:64], in1=b2s[0:127, :], op=ALU.add),
                       reads=[por, 'b2s'], writes=['tmpf'])
                if kv == 0:
                    src3 = tmpf[0:127, 0:64].rearrange("p (h d) -> p h d", d=64)
                    gain3 = AP(gq, 64, [[256, 127], [0, 1], [1, 64]])
                    sq3 = sq[:127, :64].rearrange("p (h d) -> p h d", d=64)
                    sch.op('act', lambda e: e.activation(out=sq3, in_=src3, func=AF.Square), reads=['tmpf'], writes=['sq'])
                    sch.op('dve', lambda e: e.tensor_reduce(out=small[:127, 0:1], in_=sq3, axis=AX.X, op=ALU.add),
                           reads=['sq'], writes=['small'])
                    rstd_from_ss(127, 1, small[:127, 0:1], small[:127, 16:17], 1.0 / 64)
                    if half == 0:
                        sch.op('dve', lambda e: e.memset(qnb[:, 0:128], 0.0), writes=['qnb'])
                    sch.op('dve', lambda e: e.scalar_tensor_tensor(out=qnb[0:127, half * 64:half * 64 + 64], in0=tmpf[0:127, 0:64],
                                                                   scalar=small[0:127, 16:17], in1=AP(gq, 64, [[256, 127], [1, 64]]),
                                                                   op0=ALU.mult, op1=ALU.mult),
                           reads=['tmpf', 'small', 'gq'], writes=['qnb'])
                    if half == 1:
                        transposes_to(128, qnb[:, 0:128], 'qnb', 1, kccT[:, ch:ch + 1, :], 'kccT')
                else:
                    sch.op('act', lambda e: e.copy(out=vcc_aug[0:127, g, 0:64], in_=tmpf[0:127, 0:64]), reads=['tmpf'], writes=['vcc_aug'])
        for g in range(4):
            sch.op('dve', lambda e: e.memset(vcc_aug[0:127, g, 64:65], 1.0), writes=['vcc_aug'])
            sch.op('pool', lambda e: e.tensor_copy(out=vcc_aug[:, g, 65:97], in_=ovp[:, :]), reads=['ovp'], writes=['vcc_aug'])

    pend = [None]
    cur_SS = [SS]
    late = []

    def run_late():
        while late:
            late.pop(0)()

    def flush_pv():
        p = pend[0]
        pend[0] = None
        if p is not None:
            p[0]()
            if p[1] is not None:
                p[1]()

    def flush_all():
        flush_pv()
        run_late()

    def att_tile(nk, P, kT_ap, kres, qT_rhs, G_rhs, gres, extra, v_rhs, vres, o, ores, first, last, vw, score=None, after=None):
        N = 4 * P
        Pt = [LT['Pt0'], LT['Pt1'], LT['Pt2'], LT['Pt3']]
        s, sr = cur_SS[0].next()
        nmm = 2 + len(extra)
        sch.op('pe', lambda e: e.matmul(s[:nk, :N], lhsT=kT_ap, rhs=qT_rhs, start=True, stop=False),
               reads=[kres, 'qT'], writes=[sr])
        sch.op('pe', lambda e: e.matmul(s[:nk, :N], lhsT=antib[0:nk, 128 - nk:128], rhs=G_rhs, start=False, stop=(nmm == 2)),
               reads=['identb', gres], writes=[sr])
        for xi, xt in enumerate(extra):
            if len(xt) == 3:
                xl, xr, xres = xt
                sout = s[:nk, :N]
            else:
                hf_, xl, xr, xres = xt
                sout = s[64 * hf_:64 * hf_ + 64, :N]
            sch.op('pe', lambda e: e.matmul(sout, lhsT=xl, rhs=xr, start=False, stop=(xi == len(extra) - 1)),
                   reads=xres, writes=[sr])
        pi = Pt_i[0]
        Pt_i[0] = (pi + 1) % 4
        pt = Pt[pi]
        pr = 'Pt%d' % pi
        sch.op('act', lambda e: e.activation(out=pt[:nk, :N], in_=s[:nk, :N], func=AF.Exp), reads=[sr], writes=[pr])

        def pv():
            for r in range(4):
                sch.op('pe', lambda e: e.matmul(o[:P, r * 128:r * 128 + vw], lhsT=pt[:nk, r * P:(r + 1) * P], rhs=v_rhs,
                                                start=(first and r == 0), stop=(last and r == 3)), reads=[pr, vres], writes=[ores])
            if score is not None:
                srhs, sres2, o2, o2res = score
                for r in range(4):
                    sch.op('pe', lambda e: e.matmul(o2[:P, r * 128:(r + 1) * 128], lhsT=pt[:nk, r * P:(r + 1) * P], rhs=srhs,
                                                    start=(first and r == 0), stop=(last and r == 3)), reads=[pr, sres2], writes=[o2res])
        flush_pv()
        pend[0] = (pv, after)

    def toeplitz_load(dst, dres, fdk, base_off, pstep, nh_off, L, width):
        src = AP(fd[fdk], nh_off * L + base_off, [[pstep, 128], [L, 4], [1, width]])
        sch.dma('sp', dst, src, reads=['fd_' + fdk], writes=[dres])

    def a_attention_prompt(li):
        cur_SS[0] = SS4
        G1 = [LT['G1_0'], LT['G1_1']]
        G2 = [LT['G2_0'], LT['G2_1']]
        G3 = [LT['G3_0'], LT['G3_1']]
        gzt = [LT['gzt0'], LT['gzt1']]
        selA, selB, esel = LT['selA'], LT['selB'], LT['esel']
        sch.dma('sp', selA[:], cdram['selA'][:], writes=['selA'])
        sch.dma('sp', selB[:], cdram['selB'][:], writes=['selB'])
        sch.dma('sp', esel[:], cdram['esel'][:], writes=['esel'])
        Lslc = hc['oh_slc'].shape[1]
        for g in range(4):
            half, ch = g % 2, g // 2
            hs = slice(half * 64, half * 64 + 64)
            gi = g % 2
            toeplitz_load(G1[gi][:], 'G1_%d' % gi, 'slc', 0, 1, 4 * g, Lslc, 2048)
            toeplitz_load(G2[gi][:], 'G2_%d' % gi, 'win', 0, 1, 4 * g, 768, 640)
            for b in range(NTP):
                P, tok0 = 128, b * 128
                qrhs = qT[hs, 4 * ch:4 * ch + 4, tok0:tok0 + P]
                bi = b % 2
                toeplitz_load(G3[bi][:], 'G3_%d' % bi, 'cmp', tok0, 16, 4 * g, 4096, 128)
                sch.dma('sp', gzt[bi][:], AP(gz_d, tok0 * 3072 + g * 256, [[3072, 128], [1024, 3], [1, 256]]),
                        reads=['gz_d'], writes=['gzt%d' % bi])
                o, ores = OO.next()

                def after_cmp(o=o, ores=ores, bi=bi, b=b):
                    o3 = AP(o, 0, [[512, 128], [128, 4], [1, 97]])
                    sch.op('dve', lambda e: e.tensor_scalar(out=rden[:, :], in0=AP(o, 64, [[512, 128], [128, 4]]), scalar1=1e-30,
                                                            scalar2=None, op0=ALU.max), reads=[ores], writes=['rden'])
                    sch.op('dve', lambda e: e.reciprocal(out=rden[:, :], in_=rden[:, :]), reads=['rden'], writes=['rden'])
                    sch.op('dve', lambda e: e.tensor_tensor(out=osb[:, :, 0:97], in0=o3, in1=AP(rden, 0, [[4, 128], [1, 4], [0, 97]]),
                                                            op=ALU.mult), reads=[ores, 'rden'], writes=['osb'])
                    sch.op('dve', lambda e: e.tensor_tensor(out=yacc[:, :].rearrange("p (r d) -> p r d", d=64), in0=osb[:, :, 0:64],
                                                            in1=gzt[bi][:, 0, :].rearrange("p (r d) -> p r d", d=64), op=ALU.mult),
                           reads=['osb', 'gzt%d' % bi], writes=['yacc'])
                    sch.op('dve', lambda e: e.tensor_reduce(out=sc1[:, :], in_=AP(osb, 65, [[400, 128], [1, 32], [100, 4]]),
                                                            axis=AX.X, op=ALU.add), reads=['osb'], writes=['sc1'])
                    sch.op('dve', lambda e: e.tensor_tensor(out=sc1[:, :], in0=sc1[:, :], in1=selA[:, b, :], op=ALU.mult),
                           reads=['sc1', 'selA'], writes=['sc1'])
                    sch.op('dve', lambda e: e.tensor_tensor(out=sc1[:, :], in0=sc1[:, :], in1=selB[:, b, :], op=ALU.add),
                           reads=['sc1', 'selB'], writes=['sc1'])
                    sch.op('dve', lambda e: e.max(out=mx8[:, 0:8], in_=sc1[:, :]), reads=['sc1'], writes=['mx8'])
                    sch.op('dve', lambda e: e.match_replace(out=sc2[:, :], in_to_replace=mx8[:, 0:8], in_values=sc1[:, :], imm_value=-2.0),
                           reads=['sc1', 'mx8'], writes=['sc2'])
                    sch.op('dve', lambda e: e.max(out=mx8[:, 8:16], in_=sc2[:, :]), reads=['sc2'], writes=['mx8'])
                    sch.op('dve', lambda e: e.tensor_scalar(out=sc2[:, :], in0=sc1[:, :], scalar1=mx8[:, 15:16], scalar2=None,
                                                            op0=ALU.is_ge), reads=['sc1', 'mx8'], writes=['sc2'])
                    sch.op('dve', lambda e: e.tensor_scalar(out=mnegb[:, :], in0=sc2[:, :], scalar1=-1.0, scalar2=-MASKV,
                                                            op0=ALU.add, op1=ALU.mult), reads=['sc2'], writes=['mnegb'])

                    def late_cmp():
                        tb, tbr = TB
                        sch.op('pe', lambda e: e.transpose(out=tb[0:32, 0:128], in_=mnegb[:, :], identity=identb[:, :]),
                               reads=['mnegb', 'identb'], writes=[tbr])
                        sch.op('act', lambda e: e.copy(out=mnegT[:, :], in_=tb[0:32, 0:128]), reads=[tbr], writes=['mnegT'])
                    late.append(late_cmp)
                att_tile(128, P, kccT[hs, ch, :], 'kccT', qrhs, G3[bi][:, :, :], 'G3_%d' % bi, [],
                         vcc_aug[:, g, :], 'vcc_aug', o, ores, True, True, 97, after=after_cmp)
                o, ores = OO.next()
                a0 = max(0, b - 4)

                def after_win(o=o, ores=ores, bi=bi):
                    branch_out(o, ores, gzt[bi][:, 2, :], 'gzt%d' % bi, False)
                for a in range(a0, b + 1):
                    att_tile(128, P, kwT[hs, ch, a * 128:(a + 1) * 128], 'kwT', qrhs,
                             G2[gi][:, :, 128 * (b - a):128 * (b - a) + 128], 'G2_%d' % gi, [],
                             vw_aug[:, a, g, :], 'vw_aug', o, ores, a == a0, a == b, 65, after=(after_win if a == b else None))
                o, ores = OO.next()
                mrhs = AP(mnegT, 0, [[128, 32], [0, 4], [1, 128]])

                def after_slc(o=o, ores=ores, bi=bi, g=g, tok0=tok0, P=P):
                    branch_out(o, ores, gzt[bi][:, 1, :], 'gzt%d' % bi, False)
                    sch.op('act', lambda e: e.copy(out=ybf[:, :], in_=yacc[:, :]), reads=['yacc'], writes=['ybf'])

                    def late_slc():
                        transposes_to(128, ybf[:, :], 'ybf', 2, xT[:, 2 * g:2 * g + 2, tok0:tok0 + P], 'xT')
                    late.append(late_slc)
                flush_pv() if False else None
                for a in range(b + 1):
                    if a == 0:
                        flush_pv()
                        run_late()
                    att_tile(128, P, ksT[hs, ch, a * 128:(a + 1) * 128], 'ksT', qrhs,
                             G1[gi][:, :, 128 * (b - a):128 * (b - a) + 128], 'G1_%d' % gi,
                             [(esel[:, a, :], mrhs, ['esel', 'mnegT'])],
                             vs_aug[:, a, g, :], 'vs_aug', o, ores, a == 0, a == b, 65, after=(after_slc if a == b else None))
        flush_all()
        cur_SS[0] = SS

    def branch_out(o, ores, gz2, gzres, first, P=128):
        o3 = AP(o, 0, [[512, P], [128, 4], [1, 64]])
        sch.op('dve', lambda e: e.tensor_scalar(out=rden[:P, :], in0=AP(o, 64, [[512, P], [128, 4]]), scalar1=1e-30,
                                                scalar2=None, op0=ALU.max), reads=[ores], writes=['rden'])
        sch.op('dve', lambda e: e.reciprocal(out=rden[:P, :], in_=rden[:P, :]), reads=['rden'], writes=['rden'])
        sch.op('dve', lambda e: e.tensor_tensor(out=osb[:P, :, 0:64], in0=o3, in1=AP(rden, 0, [[4, P], [1, 4], [0, 64]]),
                                                op=ALU.mult), reads=[ores, 'rden'], writes=['osb'])
        sch.op('dve', lambda e: e.tensor_tensor(out=osb[:P, :, 0:64], in0=osb[:P, :, 0:64],
                                                in1=gz2.rearrange("p (r d) -> p r d", d=64), op=ALU.mult),
               reads=['osb', gzres], writes=['osb'])
        y3 = yacc[:P, :].rearrange("p (r d) -> p r d", d=64)
        if first:
            sch.op('dve', lambda e: e.tensor_copy(out=y3, in_=osb[:P, :, 0:64]), reads=['osb'], writes=['yacc'])
        else:
            sch.op('dve', lambda e: e.tensor_tensor(out=y3, in0=y3, in1=osb[:P, :, 0:64], op=ALU.add),
                   reads=['osb', 'yacc'], writes=['yacc'])

    def load_w_full(dst, dres, W2, nk):
        wst = LT['wst']
        N = W2.shape[1]
        for c0 in range(0, N, 512):
            src = AP(W2, c0, [[N, 128], [128 * N, nk], [1, 512]])
            sch.dma('sp', wst[:, :nk, :], src, writes=['wst'])
            sch.op('pool', lambda e: e.tensor_copy(out=dst[:, :, c0:c0 + 512], in_=wst[:, :nk, :]), reads=['wst'], writes=[dres])

    def out_phase(layer, wout2, yT=None):
        yT = xT if yT is None else yT
        wo_bf, wg_bf, wp_bf, h1, h1b, h1T, sg = (LT[k] for k in ('wo_bf', 'wg_bf', 'wp_bf', 'h1', 'h1b', 'h1T', 'sg'))
        h2 = [LT['h2_0'], LT['h2_0']]
        hin = [LT['hin0'], LT['hin0']]
        pin, pinb, pT = LT['pin'], LT['pinb'], LT['pT']
        load_w_full(wo_bf, 'wo_bf', wout2, 8)
        load_w_full(wg_bf, 'wg_bf', ple_gate_w[layer], 8)
        load_w_full(wp_bf, 'wp_bf', ple_w[layer], 2)
        hs = h_src(layer)
        for t in range(NT):
            P, tok0 = tile_info(t)
            i = hin_i[0]
            hin_i[0] ^= 1
            hr = 'hin0'
            sch.dma('sp', hin[i][:P], hs(t), reads=['hd%d' % t], writes=[hr])
            psrc = p_p[layer, tok0:tok0 + P, :] if t < NTP else p_s[layer, :, :]
            sch.dma('sp', pin[:P], psrc, writes=['pin'])
            for c in range(2):
                g, gr = G.next()
                for kc in range(8):
                    sch.op('pe', lambda e: e.matmul(g[:P, :], lhsT=yT[:, kc, tok0:tok0 + P], rhs=wo_bf[:, kc, c * 512:(c + 1) * 512],
                                                    start=(kc == 0), stop=(kc == 7)), reads=['xT', 'qT', 'wo_bf'], writes=[gr])
                sch.op('dve', lambda e: e.tensor_tensor(out=h1[:P, c * 512:(c + 1) * 512], in0=g[:P, :], in1=hin[i][:P, c * 512:(c + 1) * 512],
                                                        op=ALU.add), reads=[gr, hr], writes=['h1'])
            sch.op('act', lambda e: e.copy(out=h1b[:P, :], in_=h1[:P, :]), reads=['h1'], writes=['h1b'])
            transposes_to(P, h1b[:P], 'h1b', 8, h1T[:, :, :P], 'h1T')
            sch.op('dve', lambda e: e.tensor_copy(out=pinb[:P, :], in_=pin[:P, :]), reads=['pin'], writes=['pinb'])
            transposes_to(P, pinb[:P], 'pinb', 2, pT[:, :, :P], 'pT')
            j = h2_i[0]
            h2_i[0] ^= 1
            h2r = 'h2_0'
            for c in range(2):
                g, gr = G.next()
                for kc in range(8):
                    sch.op('pe', lambda e: e.matmul(g[:P, :], lhsT=h1T[:, kc, :P], rhs=wg_bf[:, kc, c * 512:(c + 1) * 512],
                                                    start=(kc == 0), stop=(kc == 7)), reads=['h1T', 'wg_bf'], writes=[gr])
                sch.op('act', lambda e: e.activation(out=sg[:P, c * 512:(c + 1) * 512], in_=g[:P, :], func=AF.Sigmoid),
                       reads=[gr], writes=['sg'])
                g, gr = G.next()
                for kc in range(2):
                    sch.op('pe', lambda e: e.matmul(g[:P, :], lhsT=pT[:, kc, :P], rhs=wp_bf[:, kc, c * 512:(c + 1) * 512],
                                                    start=(kc == 0), stop=(kc == 1)), reads=['pT', 'wp_bf'], writes=[gr])
                sch.op('dve', lambda e: e.tensor_tensor(out=sg[:P, c * 512:(c + 1) * 512], in0=g[:P, :], in1=sg[:P, c * 512:(c + 1) * 512],
                                                        op=ALU.mult), reads=[gr, 'sg'], writes=['sg'])
                sch.op('dve', lambda e: e.tensor_tensor(out=h2[j][:P, c * 512:(c + 1) * 512], in0=sg[:P, c * 512:(c + 1) * 512],
                                                        in1=h1[:P, c * 512:(c + 1) * 512], op=ALU.add), reads=['sg', 'h1'], writes=[h2r])
            dst = y_p[tok0:tok0 + P, :] if t < NTP else y_s[:, :]
            sch.dma('pool', dst, h2[j][:P, :], reads=[h2r], writes=['hd%d' % t])


    def shared_kv_phase():
        sch.dma('sp', gq[:, 1, :], bc_row(b_k_norm), writes=['gq'])

        def consumer(ci, t, P, tok0, g, gr):
            ri = rowb_i[0]
            rowb_i[0] ^= 1
            rb = rowb[ri]
            rr = 'rowb%d' % ri
            src3 = g[:P, 0:256].rearrange("p (h d) -> p h d", d=64)
            gain3 = AP(gq, 64, [[256, P], [0, 4], [1, 64]])
            out3 = rb[:P, 0:256].rearrange("p (h d) -> p h d", d=64)
            headnorm(P, src3, 4, gain3, out3, rr, gr, extra_reads=['gq'])
            sch.op('act', lambda e: e.copy(out=rb[:P, 256:512], in_=g[:P, 256:512]), reads=[gr], writes=[rr])
            if t < NTP:
                sch.dma('pool', o_pb[tok0:tok0 + P, :], rb[:P, :], reads=[rr], writes=['o_pb'])
                sch.op('dve', lambda e: e.tensor_copy(out=qnb[:P, :], in_=rb[:P, :]), reads=[rr], writes=['qnb'])
                transposes_to(P, qnb[:P, 0:256], 'qnb', 2, ksT[:, :, tok0:tok0 + P], 'ksT')
                sch.op('pool', lambda e: e.tensor_copy(out=vs_aug[:, t, :, 0:64],
                                                        in_=qnb[:, 256:512].rearrange("p (g d) -> p g d", d=64)),
                       reads=['qnb'], writes=['vs_aug'])
            else:
                for b in range(4):
                    sch.dma('pool', o_sb[b, 2044:2048, :], rb[4 * b:4 * b + 4, :], reads=[rr], writes=['o_sb'])
        gemm(lambda kc, t: xT[:, kc, tile_info(t)[1]:tile_info(t)[1] + tile_info(t)[0]], 'xT', b_w_kv, [(0, 512)],
             list(range(NT)), consumer)

    def q_consume(P, g, gr, gain_off, dst3, dres):
        src3 = g[:P, :512].rearrange("p (h d) -> p h d", d=64)
        out4 = AP(qnb, 0, [[512, P], [64, 2], [128, 4], [1, 64]])
        sq3 = sq[:P, :512].rearrange("p (h d) -> p h d", d=64)
        sch.op('act', lambda e: e.activation(out=sq3, in_=src3, func=AF.Square), reads=[gr], writes=['sq'])
        sch.op('dve', lambda e: e.tensor_reduce(out=small[:P, 0:8], in_=sq3, axis=AX.X, op=ALU.add),
               reads=['sq'], writes=['small'])
        rstd_from_ss(P, 8, small[:P, 0:8], small[:P, 16:24], 1.0 / 64)
        t3 = tmpf[:P, :512].rearrange("p (h d) -> p h d", d=64)
        sch.op('dve', lambda e: e.tensor_tensor(out=t3, in0=src3, in1=AP(small, 16, [[64, P], [1, 8], [0, 64]]),
                                                op=ALU.mult), reads=[gr, 'small'], writes=['tmpf'])
        t4 = tmpf[:P, :512].rearrange("p (a r d) -> p a r d", a=2, r=4)
        g4 = AP(gbq, gain_off, [[192, P], [0, 2], [0, 4], [1, 64]])
        sch.op('dve', lambda e: e.tensor_tensor(out=out4, in0=t4, in1=g4, op=ALU.mult),
               reads=['tmpf', 'gbq'], writes=['qnb'])
        transposes_to(P, qnb[:P], 'qnb', 4, dst3, dres)

    def b_z_phase(j):
        W = b_w_in[j]

        def consumer(ci, t, P, tok0, g, gr):
            ri = rowb_i[0]
            rowb_i[0] ^= 1
            rb = rowb[ri]
            rr = 'rowb%d' % ri
            sch.op('act', lambda e: e.activation(out=rb[:P, :], in_=g[:P, :512], func=AF.Silu), reads=[gr], writes=[rr])
            sch.dma('pool', gz_d[tok0:tok0 + P, ci * 512:(ci + 1) * 512], rb[:P, :], reads=[rr], writes=['gz_d'])
        gemm(lambda kc, t: xT[:, kc, tile_info(t)[1]:tile_info(t)[1] + tile_info(t)[0]], 'xT', W,
             [(3072, 512), (3584, 512)], list(range(NT)), consumer)

    def b_q_phase(j, gp):
        W = b_w_in[j]
        q3T = LT['q3T']
        sch.dma('sp', gbq[:].rearrange("p a d -> p (a d)"), bc_row(b_q_norm[j].rearrange("a d -> (a d)")), writes=['gbq'])
        sch.op('dve', lambda e: e.tensor_scalar(out=gbq[:], in0=gbq[:], scalar1=SCALE, scalar2=None, op0=ALU.mult),
               reads=['gbq'], writes=['gbq'])

        def consumer(ci, t, P, tok0, g, gr):
            q_consume(P, g, gr, ci * 64, q3T[:, 4 * ci:4 * ci + 4, tok0:tok0 + P], 'qT')
        gemm(lambda kc, t: xT[:, kc, tile_info(t)[1]:tile_info(t)[1] + tile_info(t)[0]], 'xT', W,
             [(grp * 1024 + gp * 512, 512) for grp in range(3)], list(range(NT)), consumer)

    def b_attention_prompt(j, gp):
        cur_SS[0] = SS4
        q3T = LT['q3T']
        Gd = {1: LT['Gd1'], 4: LT['Gd4'], 16: LT['Gd16']}
        gzt = [LT['gzt0'], LT['gzt1']]
        Ld = {1: 384, 4: 768, 16: 2304}
        Wd = {1: 256, 4: 640, 16: 2048}
        for g in (2 * gp, 2 * gp + 1):
            half, ch = g % 2, g // 2
            hs = slice(half * 64, half * 64 + 64)
            for d in (1, 4, 16):
                toeplitz_load(Gd[d][:], 'Gd%d' % d, 'd%d' % d, 0, 1, 4 * g, Ld[d], Wd[d])
            for b in range(NTP):
                P, tok0 = 128, b * 128
                bi = b % 2
                sch.dma('sp', gzt[bi][:, 0, :], gz_d[tok0:tok0 + P, g * 256:(g + 1) * 256], reads=['gz_d'], writes=['gzt%d' % bi])
                o, ores = OO.next()
                tiles = []
                for grp, d, na in ((0, 1, 2), (1, 4, 5), (2, 16, 99)):
                    for a in range(max(0, b - na + 1), b + 1):
                        tiles.append((grp, d, a))
                def after_b(o=o, ores=ores, bi=bi, g=g, tok0=tok0, P=P):
                    branch_out(o, ores, gzt[bi][:, 0, :], 'gzt%d' % bi, True)
                    sch.op('act', lambda e: e.copy(out=ybf[:, :], in_=yacc[:, :]), reads=['yacc'], writes=['ybf'])

                    def late_b():
                        transposes_to(128, ybf[:, :], 'ybf', 2, qT[:, 2 * g:2 * g + 2, tok0:tok0 + P], 'qT')
                    late.append(late_b)
                for ti, (grp, d, a) in enumerate(tiles):
                    if ti == min(3, len(tiles) - 1):
                        run_late()
                    qrhs = q3T[hs, 4 * grp:4 * grp + 4, tok0:tok0 + P]
                    att_tile(128, P, ksT[hs, ch, a * 128:(a + 1) * 128], 'ksT', qrhs,
                             Gd[d][:, :, 128 * (b - a):128 * (b - a) + 128], 'Gd%d' % d, [],
                             vs_aug[:, a, g, :], 'vs_aug', o, ores, ti == 0, ti == len(tiles) - 1, 65,
                             after=(after_b if ti == len(tiles) - 1 else None))
        flush_all()
        cur_SS[0] = SS

    SC_PROJB = [('wst', [128, 4, 512], F32), ('wbf', [128, 2, 8, 512], BF16), ('sq_x1', [128, 512], F32), ('tmpf_x1', [128, 512], F32),
                ('qnb_x1', [128, 512], BF16), ('qnb_x2', [128, 512], BF16), ('small_x1', [128, 64], F32), ('small_x2', [128, 64], F32), ('tbring', [128, 1], F32)]
    SC_Q3 = [('q3T', [128, 12, NTOK], BF16)]
    SC_ATTB = [('Gd1', [128, 4, 256], BF16), ('Gd4', [128, 4, 640], BF16), ('Gd16', [128, 4, 2048], BF16),
               ('gzt0', [128, 1, 256], F32), ('gzt1', [128, 1, 256], F32),
               ('Pt0', [128, 512], BF16), ('Pt1', [128, 512], BF16), ('Pt2', [128, 512], BF16), ('Pt3', [128, 512], BF16)]


    GsT = pg.sb('GsT', [128, 58, 16, 4], BF16)
    msk_d = pg.dint('msk_d', [4, 2048], BF16)
    cache_cmp_flat = AP(cache_cmp, 0, [[512, 2 * NPHYS * 128], [1, 512]])
    cache_slc_flat = AP(cache_slc, 0, [[512, 2 * NPHYS * 128], [1, 512]])
    TI_SLC, TI_WIN, TI_CMP, TI_B = 0, 25, 30, 34

    def load_gs(ti, fdk, L, base, pstep, nk):
        src = AP(fd[fdk], base, [[pstep, nk], [L, 16], [1, 4]])
        sch.dma('sp', GsT[:nk, ti, :, :], src, reads=['fd_' + fdk], writes=['GsT'])

    def sample_tables():
        Ls = hc['oh_slc'].shape[1]
        for a in range(40, 64):
            load_gs(TI_SLC + a - 40, 'slc', Ls, 8192 - 128 * a, 1, 128)
        load_gs(TI_SLC + 24, 'slc', Ls, 124, 1, 4)
        for a in range(4):
            load_gs(TI_WIN + a, 'win', 768, 512 - 128 * a, 1, 128)
        load_gs(TI_WIN + 4, 'win', 768, 124, 1, 4)
        for ct in range(4):
            load_gs(TI_CMP + ct, 'slc', Ls, 6256 - 2048 * ct, 16, 128)
        ti = TI_B
        for d, L, a0 in ((1, 384, 15), (4, 768, 12), (16, 2304, 0)):
            for a in range(a0, 16):
                load_gs(ti, 'd%d' % d, L, 2048 - 128 * a, 1, 128)
                ti += 1
            load_gs(ti, 'd%d' % d, L, 124, 1, 4)
            ti += 1

    SC_PREP = [('rows0', [128, 512], F32), ('rows1', [128, 512], F32), ('kb0', [128, 256], BF16), ('kb1', [128, 256], BF16),
               ('kTt0', [128, 2, 128], BF16), ('kTt1', [128, 2, 128], BF16), ('kTt2', [128, 2, 128], BF16),
               ('vat0', [128, 4, 65], BF16), ('vat1', [128, 4, 65], BF16), ('vat2', [128, 4, 65], BF16),
               ('Pt0', [128, 512], BF16), ('Pt1', [128, 512], BF16), ('Pt2', [128, 512], BF16), ('Pt3', [128, 512], BF16),
               ('gzs', [4, 1024], F32), ('ys', [4, 4, 256], F32)]
    SC_SAMPA = SC_PREP + [('idx', [128, 256], I32), ('ptab_i', [128, 256], I32), ('pidx', [128, 1], F32),
                          ('w1s', [128, 32, 256], BF16), ('w2s2', [128, 2, 2, 64], BF16), ('b1s2', [128, 4], F32),
                          ('b2s2', [128, 2, 64], F32), ('wst4', [128, 4, 256], F32), ('chT', [128, 4, 2176], BF16),
                          ('rowsb', [128, 512], BF16), ('kccT_s', [128, 2, 512], BF16), ('vcc_s', [128, 4, 4, 65], BF16),
                          ('ovs', [128, 4, 128], BF16), ('selBs', [4, 136], F32), ('sc1s', [4, 136], F32), ('sc2s', [4, 136], F32),
                          ('scs', [4, 128], F32), ('mx8s', [4, 16], F32), ('mneg_s', [4, 4, 128], BF16), ('Xm', [1, 2048], BF16),
                          ('e2', [1, 64], BF16)]
    prep_i = [0, 0]

    def prep_tile(load_fn, nk):
        ri = prep_i[0]
        prep_i[0] ^= 1
        ki = prep_i[1]
        prep_i[1] = (ki + 1) % 3
        rows, kb, kTt, vat = LT['rows%d' % ri], LT['kb%d' % ri], LT['kTt%d' % ki], LT['vat%d' % ki]
        rr, kbr, ktr, var = 'rows%d' % ri, 'kb%d' % ri, 'kTt%d' % ki, 'vat%d' % ki
        load_fn(rows[:nk, :], rr)
        sch.op('dve', lambda e: e.tensor_copy(out=kb[:nk, :], in_=rows[:nk, 0:256]), reads=[rr], writes=[kbr])
        sch.op('act', lambda e: e.copy(out=vat[:nk, :, 0:64], in_=rows[:nk, 256:512].rearrange("p (g d) -> p g d", d=64)),
               reads=[rr], writes=[var])
        transposes_to(nk, kb[:nk, :], kbr, 2, kTt[:, :, :nk], ktr)
        return kTt, ktr, vat, var

    def dram_loader(src_ap, res=()):
        def f(rows_ap, rr):
            sch.dma('sp', rows_ap, src_ap, reads=list(res), writes=[rr])
        return f

    def page_loader(flat, col):
        def f(rows_ap, rr):
            sch.idma(rows_ap, flat, LT['idx'][:, col:col + 1], reads=['idx'], writes=[rr])
        return f

    def branch_out_s(o, ores, g, first):
        P = 4
        gzs, ys = LT['gzs'], LT['ys']
        o3 = AP(o, 0, [[512, P], [128, 4], [1, 64]])
        sch.op('dve', lambda e: e.tensor_scalar(out=rden[:P, :], in0=AP(o, 64, [[512, P], [128, 4]]), scalar1=1e-30,
                                                scalar2=None, op0=ALU.max), reads=[ores], writes=['rden'])
        sch.op('dve', lambda e: e.reciprocal(out=rden[:P, :], in_=rden[:P, :]), reads=['rden'], writes=['rden'])
        sch.op('dve', lambda e: e.tensor_tensor(out=osb[:P, :, 0:64], in0=o3, in1=AP(rden, 0, [[4, P], [1, 4], [0, 64]]),
                                                op=ALU.mult), reads=[ores, 'rden'], writes=['osb'])
        gz3 = gzs[:P, g * 256:(g + 1) * 256].rearrange("p (r d) -> p r d", d=64)
        sch.op('dve', lambda e: e.tensor_tensor(out=osb[:P, :, 0:64], in0=osb[:P, :, 0:64], in1=gz3, op=ALU.mult),
               reads=['osb', 'gzs'], writes=['osb'])
        y3 = ys[:P, g, :].rearrange("p (r d) -> p r d", d=64)
        if first:
            sch.op('dve', lambda e: e.tensor_copy(out=y3, in_=osb[:P, :, 0:64]), reads=['osb'], writes=['ys'])
        else:
            sch.op('dve', lambda e: e.tensor_tensor(out=y3, in0=y3, in1=osb[:P, :, 0:64], op=ALU.add),
                   reads=['osb', 'ys'], writes=['ys'])

    def samp_finish(bs, ydst, yres, groups=(0, 1, 2, 3)):
        ys = LT['ys']
        qs = S + 4 * bs
        for g in groups:
            sch.op('act', lambda e: e.copy(out=ybf[:4, :], in_=ys[:4, g, :]), reads=['ys'], writes=['ybf'])
            transposes_to(4, ybf[:4, :], 'ybf', 2, ydst[:, 2 * g:2 * g + 2, qs:qs + 4], yres)

    def load_gz(bs, br):
        sch.dma('sp', LT['gzs'][:4, :], gz_d[S + 4 * bs:S + 4 * bs + 4, br * 1024:(br + 1) * 1024], reads=['gz_d'], writes=['gzs'])

    def samp_multi(bs, tiles, qsrc, qblk_of, groups, masked):
        acc = {0: G.items[0], 1: G.items[1], 2: OO.items[0], 3: OO.items[1]}
        qs = S + 4 * bs
        for ti, (load_fn, nk, tab, ea, grp) in enumerate(tiles):
            kTt, ktr, vat, var = prep_tile(load_fn, nk)
            for g in groups:
                half, ch = g % 2, g // 2
                hs = slice(half * 64, half * 64 + 64)
                qb = qblk_of(g, grp)
                qrhs = qsrc[hs, qb:qb + 4, qs:qs + 4]
                extra = []
                if masked and ea is not None:
                    extra = [(hf_, LT['e2'][0:1, 0:64], AP(LT['Xm'], g * 128 + 2 * ea + hf_, [[2048, 1], [0, 4], [512, 4]]), ['e2', 'Xm'])
                             for hf_ in range(2)]
                o, ores = acc[g]
                att_tile(nk, 4, kTt[hs, ch, :nk], ktr, qrhs, GsT[:nk, tab, 4 * g:4 * g + 4, :], 'GsT', extra,
                         vat[:nk, g, :], var, o, ores, ti == 0, ti == len(tiles) - 1, 65)
        flush_pv()
        return acc

    def init_vat():
        for i_ in range(3):
            sch.op('dve', lambda e: e.memset(LT['vat%d' % i_][:, :, 64:65], 1.0), writes=['vat%d' % i_])

    def a_sample(li):
        init_vat()
        idx, ptab_i, pidx = LT['idx'], LT['ptab_i'], LT['pidx']
        w1s, w2s2, b1s2, b2s2, wst4 = LT['w1s'], LT['w2s2'], LT['b1s2'], LT['b2s2'], LT['wst4']
        chT, rowsb, kccT_s, vcc_s, ovs = LT['chT'], LT['rowsb'], LT['kccT_s'], LT['vcc_s'], LT['ovs']
        selBs, sc1s, sc2s, scs, mx8s, mneg_s, Xm, e2 = (LT[k] for k in ('selBs', 'sc1s', 'sc2s', 'scs', 'mx8s', 'mneg_s', 'Xm', 'e2'))
        sch.dma('sp', ptab_i[:, :], AP(ptab, 0, [[0, 128], [1, 256]]), writes=['ptab_i'])
        sch.dma('sp', pidx[:, :], cdram['pidx'][:, :], writes=['pidx'])
        sch.dma('sp', ovs[:], cdram['ov_s'][:], writes=['ovs'])
        sch.dma('sp', selBs[:], cdram['selB_s'][:], writes=['selBs'])
        sch.op('dve', lambda e: e.memset(e2[:], 1.0), writes=['e2'])
        sch.op('dve', lambda e: e.tensor_scalar(out=idx[:, :], in0=ptab_i[:, :], scalar1=128.0, scalar2=pidx[:, 0:1],
                                                op0=ALU.mult, op1=ALU.add), reads=['ptab_i', 'pidx'], writes=['idx'])
        if li == 1:
            sch.op('dve', lambda e: e.tensor_scalar(out=idx[:, :], in0=idx[:, :], scalar1=float(NPHYS * 128), scalar2=None,
                                                    op0=ALU.add), reads=['idx'], writes=['idx'])
        for kv in range(2):
            sch.dma('sp', wst4[:, 0, 0:128].rearrange("p (a d) -> p a d", d=64),
                    AP(a_phi_w2[li, kv], 0, [[64, 128], [128 * 64, 2], [1, 64]]), writes=['wst4'])
            sch.op('pool', lambda e: e.tensor_copy(out=w2s2[:, kv, :, :], in_=wst4[:, 0, 0:128].rearrange("p (a d) -> p a d", d=64)),
                   reads=['wst4'], writes=['w2s2'])
            for hh_ in range(2):
                sch.dma('sp', b1s2[:, 2 * kv + hh_:2 * kv + hh_ + 1], AP(a_phi_b1[li, kv], hh_ * 128, [[1, 128], [1, 1]]), writes=['b1s2'])
            sch.dma('sp', b2s2[:, kv, :], bc_row(a_phi_b2[li, kv]), writes=['b2s2'])

        ksv = ksT[:, :, :].rearrange("p a (b h) -> p (a b) h", h=256)
        kwv = kwT[:, :, :].rearrange("p a (b h) -> p (a b) h", h=256)

        def w1_of(kv, ab):
            if kv == 0:
                return w1s[:, ab * 16:(ab + 1) * 16, :], 'w1s'
            return (ksv, 'ksT') if ab == 0 else (kwv, 'kwT')

        for kv in range(2):
            for ab in range(2):
                wv, wres = w1_of(kv, ab)
                for sq4 in range(4):
                    for half in range(2):
                        src = AP(a_phi_w1[li, kv, ab], sq4 * 4 * 64 * 256, [[256, 64], [64 * 256, 4], [1, 256]])
                        sch.dma('sp', wst4[half * 64:half * 64 + 64, :, :], src, writes=['wst4'])
                    sch.op('pool', lambda e: e.tensor_copy(out=wv[:, sq4 * 4:sq4 * 4 + 4, :], in_=wst4[:, :, :]),
                           reads=['wst4'], writes=[wres])

        for bs in range(4):
            qs = S + 4 * bs
            sch.op('dve', lambda e: e.memset(kccT_s[:], 0.0), writes=['kccT_s'])
            sch.op('dve', lambda e: e.memset(vcc_s[:], 0.0), writes=['vcc_s'])
            for ct in range(4):
                npg = 17 if ct < 3 else 16
                ncol = 128 if ct < 3 else 127
                for pi in range(npg):
                    ri = prep_i[0]
                    prep_i[0] ^= 1
                    rows, rr = LT['rows%d' % ri], 'rows%d' % ri
                    sch.idma(rows[:, :], cache_cmp_flat, idx[:, bs * 64 + 16 * ct + pi:bs * 64 + 16 * ct + pi + 1],
                             reads=['idx'], writes=[rr])
                    if pi % 2 == 0:
                        sch.op('dve', lambda e: e.tensor_copy(out=rowsb[:, :], in_=rows[:, :]), reads=[rr], writes=['rowsb'])
                    else:
                        sch.op('act', lambda e: e.copy(out=rowsb[:, :], in_=rows[:, :]), reads=[rr], writes=['rowsb'])
                    transposes_to(128, rowsb[:, :], 'rowsb', 4, chT[:, :, pi * 128:(pi + 1) * 128], 'chT')
                for kv in range(2):
                    for g in range(4):
                        half, ch = g % 2, g // 2
                        hs = slice(half * 64, half * 64 + 64)
                        blk = kv * 2 + ch
                        for hh in range(2):
                            ps, psr = SS.next()
                            n = 0
                            for ab in range(2):
                                for s_ in range(16):
                                    rhs = AP(chT, half * 64 * (4 * 2176) + blk * 2176 + 16 * ab + s_, [[4 * 2176, 64], [16, ncol]])
                                    wv_, wres_ = w1_of(kv, ab)
                                    sch.op('pe', lambda e: e.matmul(ps[:, 0:ncol], lhsT=wv_[hs, s_, hh * 128:(hh + 1) * 128],
                                                                    rhs=rhs, start=(n == 0), stop=(n == 31)),
                                           reads=[wres_, 'chT'], writes=[psr])
                                    n += 1
                            sch.op('act', lambda e: e.activation(out=hidT[:, hh, 0:ncol], in_=ps[:, 0:ncol], func=AF.Silu,
                                                                 bias=b1s2[:, 2 * kv + hh:2 * kv + hh + 1]), reads=[psr, 'b1s2'], writes=['hidT'])
                        po, por = OO.next()
                        for hh in range(2):
                            sch.op('pe', lambda e: e.matmul(po[0:ncol, 0:64], lhsT=hidT[:, hh, 0:ncol], rhs=w2s2[:, kv, hh, :],
                                                            start=(hh == 0), stop=(hh == 1)), reads=['hidT', 'w2s2'], writes=[por])
                        sch.op('dve', lambda e: e.tensor_tensor(out=tmpf[0:ncol, 0:64], in0=po[0:ncol, 0:64], in1=b2s2[0:ncol, kv, :], op=ALU.add),
                               reads=[por, 'b2s2'], writes=['tmpf'])
                        if kv == 0:
                            src3 = tmpf[0:ncol, 0:64].rearrange("p (h d) -> p h d", d=64)
                            sq3 = sq[:ncol, :64].rearrange("p (h d) -> p h d", d=64)
                            sch.op('act', lambda e: e.activation(out=sq3, in_=src3, func=AF.Square), reads=['tmpf'], writes=['sq'])
                            sch.op('dve', lambda e: e.tensor_reduce(out=small[:ncol, 0:1], in_=sq3, axis=AX.X, op=ALU.add),
                                   reads=['sq'], writes=['small'])
                            rstd_from_ss(ncol, 1, small[:ncol, 0:1], small[:ncol, 16:17], 1.0 / 64)
                            if half == 0:
                                sch.op('dve', lambda e: e.memset(qnb[:, 0:128], 0.0), writes=['qnb'])
                            sch.op('dve', lambda e: e.scalar_tensor_tensor(out=qnb[0:ncol, half * 64:half * 64 + 64], in0=tmpf[0:ncol, 0:64],
                                                                           scalar=small[0:ncol, 16:17], in1=AP(gq, 64, [[256, ncol], [1, 64]]),
                                                                           op0=ALU.mult, op1=ALU.mult),
                                   reads=['tmpf', 'small', 'gq'], writes=['qnb'])
                            if half == 1:
                                transposes_to(128, qnb[:, 0:128], 'qnb', 1, kccT_s[:, ch:ch + 1, ct * 128:(ct + 1) * 128], 'kccT_s')
                        else:
                            sch.op('act', lambda e: e.copy(out=vcc_s[0:ncol, ct, g, 0:64], in_=tmpf[0:ncol, 0:64]), reads=['tmpf'], writes=['vcc_s'])
                            sch.op('dve', lambda e: e.memset(vcc_s[0:ncol, ct, g, 64:65], 1.0), writes=['vcc_s'])
            load_gz(bs, 0)
            for g in range(4):
                half, ch = g % 2, g // 2
                hs = slice(half * 64, half * 64 + 64)
                qrhs = qT[hs, 4 * ch:4 * ch + 4, qs:qs + 4]
                o, ores = OO.next()
                o2, o2res = TF
                for ct in range(4):
                    att_tile(128, 4, kccT_s[hs, ch, ct * 128:(ct + 1) * 128], 'kccT_s', qrhs, GsT[:, TI_CMP + ct, 4 * g:4 * g + 4, :], 'GsT',
                             [], vcc_s[:, ct, g, :], 'vcc_s', o, ores, ct == 0, ct == 3, 65,
                             score=(ovs[:, ct, :], 'ovs', o2, o2res))
                flush_pv()
                branch_out_s(o, ores, g, True)
                sch.op('dve', lambda e: e.tensor_scalar(out=scs[:, :], in0=o2[:4, 0:128], scalar1=rden[:4, 0:1], scalar2=None, op0=ALU.mult),
                       reads=[o2res, 'rden'], writes=['scs'])
                for r in range(1, 4):
                    sch.op('dve', lambda e: e.scalar_tensor_tensor(out=scs[:, :], in0=o2[:4, r * 128:(r + 1) * 128], scalar=rden[:4, r:r + 1],
                                                                   in1=scs[:, :], op0=ALU.mult, op1=ALU.add),
                           reads=[o2res, 'rden', 'scs'], writes=['scs'])
                sch.op('dve', lambda e: e.tensor_copy(out=sc1s[:, :], in_=selBs[:, :]), reads=['selBs'], writes=['sc1s'])
                sch.op('dve', lambda e: e.tensor_tensor(out=sc1s[:, 0:128], in0=scs[:, :], in1=selBs[:, 0:128], op=ALU.add),
                       reads=['scs', 'selBs'], writes=['sc1s'])
                sch.op('dve', lambda e: e.max(out=mx8s[:, 0:8], in_=sc1s[:, :]), reads=['sc1s'], writes=['mx8s'])
                sch.op('dve', lambda e: e.match_replace(out=sc2s[:, :], in_to_replace=mx8s[:, 0:8], in_values=sc1s[:, :], imm_value=-2.0),
                       reads=['sc1s', 'mx8s'], writes=['sc2s'])
                sch.op('dve', lambda e: e.max(out=mx8s[:, 8:16], in_=sc2s[:, :]), reads=['sc2s'], writes=['mx8s'])
                sch.op('dve', lambda e: e.tensor_scalar(out=sc2s[:, :], in0=sc1s[:, :], scalar1=mx8s[:, 15:16], scalar2=None,
                                                        op0=ALU.is_ge), reads=['sc1s', 'mx8s'], writes=['sc2s'])
                sch.op('dve', lambda e: e.tensor_scalar(out=mneg_s[:, g, :], in0=sc2s[:, 0:128], scalar1=-1.0, scalar2=-MASKV,
                                                        op0=ALU.add, op1=ALU.mult), reads=['sc2s'], writes=['mneg_s'])
            sch.dma('sp', AP(msk_d, bs * 2048, [[512, 4], [128, 4], [1, 128]]), mneg_s[:, :, :], reads=['mneg_s'], writes=['msk_d'])
            sch.dma('sp', Xm[0:1, :], AP(msk_d, bs * 2048, [[0, 1], [1, 2048]]), reads=['msk_d'], writes=['Xm'])
            load_gz(bs, 1)
            tiles = []
            for a in range(64):
                tab = TI_SLC + max(a, 40) - 40
                tiles.append((page_loader(cache_slc_flat, bs * 64 + a), 128, tab, a, 0))
            tiles.append((dram_loader(o_ss[li, 4 * bs:4 * bs + 4, :], ('o_ss',)), 4, TI_SLC + 24, None, 0))
            acc = samp_multi(bs, tiles, qT, lambda g, grp: 4 * (g // 2), (0, 1, 2, 3), True)
            for g in range(4):
                branch_out_s(acc[g][0], acc[g][1], g, False)
            load_gz(bs, 2)
            tiles = []
            for a in range(4):
                tiles.append((dram_loader(st_a[li, bs, 128 * a:128 * a + 128, :]), 128, TI_WIN + a, None, 0))
            tiles.append((dram_loader(o_sw[li, bs, 508:512, :], ('o_sw',)), 4, TI_WIN + 4, None, 0))
            acc = samp_multi(bs, tiles, qT, lambda g, grp: 4 * (g // 2), (0, 1, 2, 3), False)
            for g in range(4):
                branch_out_s(acc[g][0], acc[g][1], g, False)
            samp_finish(bs, xT, 'xT')

    def b_sample(j, gp):
        init_vat()
        q3T = LT['q3T']
        groups = (2 * gp, 2 * gp + 1)
        for bs in range(4):
            sch.dma('sp', LT['gzs'][:4, :], gz_d[S + 4 * bs:S + 4 * bs + 4, 0:1024], reads=['gz_d'], writes=['gzs'])
            tiles = []
            ti = TI_B
            for grp, (d, a0) in enumerate(((1, 15), (4, 12), (16, 0))):
                for a in range(a0, 16):
                    tiles.append((dram_loader(st_b[bs, 128 * a:128 * a + 128, :]), 128, ti, None, grp))
                    ti += 1
                tiles.append((dram_loader(o_sb[bs, 2044:2048, :], ('o_sb',)), 4, ti, None, grp))
                ti += 1
            acc = samp_multi(bs, tiles, q3T, lambda g, grp: 4 * grp, groups, False)
            for g in groups:
                branch_out_s(acc[g][0], acc[g][1], g, True)
            samp_finish(bs, qT, 'qT', groups)

    def state_copies():
        for li in range(2):
            for b in range(4):
                sch.dma('pool', o_sw[li, b, 0:508, :], st_a[li, b, 4:512, :], writes=['o_sw'])
        for b in range(4):
            sch.dma('pool', o_sb[b, 0:2044, :], st_b[b, 4:2048, :], writes=['o_sb'])

    sch.op('dve', lambda e: e.memset(vs_aug[:, :, :, 64:65], 1.0), writes=['vs_aug'])
    sch.op('dve', lambda e: e.memset(vw_aug[:, :, :, 64:65], 1.0), writes=['vw_aug'])

    state_copies()
    sample_tables()
    for layer in range(4):
        if layer >= STAGE:
            break
        with Scope(SC_NORM, 'norm'):
            phase_norm(h_src(layer), norm_g[layer])
        if layer < 2:
            with Scope(SC_PROJ, 'projA'):
                a_project(layer)
                compress_prompt(layer)
            with Scope(SC_ATT, 'attA'):
                a_attention_prompt(layer)
            with Scope(SC_SAMPA, 'sampA'):
                a_sample(layer)
            with Scope(SC_OUT, 'out'):
                out_phase(layer, a_w_out[layer])
            if layer == 1 and STAGE > 2:
                with Scope(SC_NORM, 'norm'):
                    phase_norm(h_src(2), kv_norm_g)
                with Scope(SC_PROJB, 'projB'):
                    shared_kv_phase()
        else:
            j = layer - 2
            with Scope(SC_PROJB, 'projB'):
                b_z_phase(j)
            with Scope(SC_Q3):
                for gp in range(2):
                    with Scope(SC_PROJB, 'projB'):
                        b_q_phase(j, gp)
                    if gp == 1:
                        pass
                    with Scope(SC_ATTB, 'attB'):
                        b_attention_prompt(j, gp)
                    with Scope(SC_PREP, 'sampB'):
                        b_sample(j, gp)
            with Scope(SC_OUT, 'out'):
                out_phase(layer, b_w_out[j], qT)

    sch.barrier()
    return pg, hc


_CACHE = {}


def kernel(**inputs):
    if 'prog' not in _CACHE:
        _CACHE['prog'] = build()
    pg, hc = _CACHE['prog']
    f = lambda a: np.ascontiguousarray(np.asarray(a))
    x_prompt = f(inputs['x_prompt'])
    x_sample = f(inputs['x_sample'])
    cache_cmp = f(inputs['cache_a_cmp']).reshape(2, NPHYS * 128, 512)
    cache_slc = f(inputs['cache_a_slc']).reshape(2, NPHYS * 128, 512)
    st_a = f(inputs['state_a_win'])
    st_b = f(inputs['state_b_win'])
    ptab = f(inputs['page_table']).astype(np.int32)
    p_prompt = f(inputs['p_prompt'])
    p_sample = f(inputs['p_sample'])
    shared = {
        'cache_cmp': cache_cmp, 'cache_slc': cache_slc,
        'rel_bias': f(inputs['rel_bias']), 'norm_g': f(inputs['norm_g']), 'a_w_in': f(inputs['a_w_in']),
        'a_gate_b': f(inputs['a_gate_b']).reshape(2, 48), 'a_qk_norm': f(inputs['a_qk_norm']),
        'a_phi_w1': f(inputs['a_phi_w1']), 'a_phi_b1': f(inputs['a_phi_b1']), 'a_phi_w2': f(inputs['a_phi_w2']),
        'a_phi_b2': f(inputs['a_phi_b2']), 'a_w_out': f(inputs['a_w_out']), 'kv_norm_g': f(inputs['kv_norm_g']),
        'b_w_kv': f(inputs['b_w_kv']), 'b_k_norm': f(inputs['b_k_norm']), 'b_w_in': f(inputs['b_w_in']),
        'b_q_norm': f(inputs['b_q_norm']), 'b_w_out': f(inputs['b_w_out']), 'ple_w': f(inputs['ple_w']),
        'ple_gate_w': f(inputs['ple_gate_w']),
    }
    for k, v in hc.items():
        shared['c_' + k] = v
    in_maps = []
    for c in range(8):
        m = dict(shared)
        m['x_p'] = x_prompt[c]
        m['x_s'] = x_sample[4 * c:4 * c + 4].reshape(16, D)
        m['p_p'] = p_prompt[:, c]
        m['p_s'] = p_sample[:, 4 * c:4 * c + 4].reshape(4, 16, 256)
        m['st_a'] = st_a[:, 4 * c:4 * c + 4].reshape(2, 4, 512, 512)
        m['st_b'] = st_b[4 * c:4 * c + 4].reshape(4, 2048, 512)
        m['ptab'] = ptab[4 * c:4 * c + 4]
        in_maps.append({k: m[k] for k in pg.din_names})
    res = run_bass_kernel_spmd(pg.nc, in_maps, core_ids=list(range(8)))
    R = res.results
    cat = lambda k, ax=0: np.stack([np.asarray(r[k]) for r in R], axis=ax)
    y_prompt = cat('y_p').reshape(8, S, D)
    y_sample = cat('y_s').reshape(32, 4, D)
    pr_c = cat('o_pc', 1).reshape(2, 8, S, 2, 4, 64)
    pr_s = cat('o_ps', 1).reshape(2, 8, S, 2, 4, 64)
    pr_w = cat('o_pw', 1).reshape(2, 8, 512, 2, 4, 64)
    pr_b = cat('o_pb').reshape(8, S, 2, 4, 64)
    sm_c = cat('o_sc', 1).reshape(2, 32, 4, 2, 4, 64)
    sm_s = cat('o_ss', 1).reshape(2, 32, 4, 2, 4, 64)
    sm_w = cat('o_sw', 1).reshape(2, 32, 512, 2, 4, 64)
    sm_b = cat('o_sb').reshape(32, 2048, 2, 4, 64)
    return (y_prompt, y_sample, pr_c, pr_s, pr_w, pr_b, sm_c, sm_s, sm_w, sm_b)
```

```python
import math
import numpy as np
import ml_dtypes
import concourse.bass as bass
import concourse.mybir as mybir
from concourse.bass_utils import run_bass_kernel_spmd

F32 = mybir.dt.float32
BF16 = mybir.dt.bfloat16
I32 = mybir.dt.int32
AF = mybir.ActivationFunctionType
ALU = mybir.AluOpType
AX = mybir.AxisListType

D = 1024
S = 2048
NTP = 16
NT = 17
NSAMP = 16
NTOK = S + NSAMP
EPS = 1e-6
SCALE = 0.125
MASKV = -30000.0
NDS = 40
PAST = 8192
NPHYS = 2560

STAGE = 99


def tile_info(t):
    if t < NTP:
        return 128, t * 128
    return NSAMP, S


class Sch:
    def __init__(self, nc):
        self.nc = nc
        self.E = {'pe': nc.tensor, 'act': nc.scalar, 'dve': nc.vector, 'pool': nc.gpsimd, 'sp': nc.sync}
        self.sems = {}
        self.ccnt = {}
        for e in ('pe', 'act', 'dve', 'pool'):
            self.sems['c_' + e] = nc.alloc_semaphore('c_' + e)
            self.ccnt[e] = 0
        self.dcnt = [0] * NDS
        for i in range(NDS):
            self.sems['d%d' % i] = nc.alloc_semaphore('d%d' % i)
        self.di = 0
        self.lastw = {}
        self.readers = {}
        self.waited = {e: {} for e in self.E}
        self.n_inst = 0
        self.alias = {}

    def _need(self, eng, reads, writes):
        toks = []
        for r in reads:
            t = self.lastw.get(r)
            if t is not None and not (t[2] == 'pe' and eng == 'pe'):
                toks.append(t)
        for w in writes:
            t = self.lastw.get(w)
            if t is not None and t[2] != eng:
                toks.append(t)
            rd = self.readers.get(w)
            if rd:
                for sem, (val, src) in rd.items():
                    if src != eng:
                        toks.append((sem, val, src))
        return toks

    def _wait(self, eng, toks):
        wd = self.waited[eng]
        best = {}
        for sem, val, src in toks:
            if wd.get(sem, 0) >= val:
                continue
            if best.get(sem, 0) < val:
                best[sem] = val
        for sem, val in best.items():
            self.E[eng].wait_ge(self.sems[sem], val)
            wd[sem] = val
            self.n_inst += 1

    def _record(self, tok, reads, writes):
        for r in reads:
            d = self.readers.setdefault(r, {})
            d[tok[0]] = (tok[1], tok[2])
        for w in writes:
            self.lastw[w] = tok
            self.readers[w] = {}

    def op(self, eng, fn, reads=(), writes=()):
        reads = [self.alias.get(r, r) for r in reads]
        writes = [self.alias.get(w, w) for w in writes]
        self._wait(eng, self._need(eng, reads, writes))
        inst = fn(self.E[eng])
        self.ccnt[eng] += 1
        inst.then_inc(self.sems['c_' + eng], 1)
        self._record(('c_' + eng, self.ccnt[eng], eng), reads, writes)
        self.n_inst += 1

    def dma(self, q, out, in_, reads=(), writes=(), **kw):
        reads = [self.alias.get(r, r) for r in reads]
        writes = [self.alias.get(w, w) for w in writes]
        i = self.di
        self.di = (self.di + 1) % NDS
        name = 'd%d' % i
        toks = self._need(q, reads, writes)
        if self.dcnt[i] > 0:
            toks.append((name, self.dcnt[i], 'dma'))
        self._wait(q, toks)
        inst = self.E[q].dma_start(out=out, in_=in_, **kw)
        self.dcnt[i] += 16
        inst.then_inc(self.sems[name], 16)
        self._record((name, self.dcnt[i], 'dma'), reads, writes)
        self.n_inst += 1

    def idma(self, out, in_, idx_ap, reads=(), writes=()):
        q = 'pool'
        i = self.di
        self.di = (self.di + 1) % NDS
        name = 'd%d' % i
        toks = self._need(q, reads, writes)
        if self.dcnt[i] > 0:
            toks.append((name, self.dcnt[i], 'dma'))
        self._wait(q, toks)
        inst = self.E[q].indirect_dma_start(out=out, out_offset=None, in_=in_,
                                            in_offset=bass.IndirectOffsetOnAxis(ap=idx_ap, axis=0))
        self.dcnt[i] += 16
        inst.then_inc(self.sems[name], 16)
        self._record((name, self.dcnt[i], 'dma'), reads, writes)
        self.n_inst += 1

    def raw_tok_wait(self, eng, res_list):
        toks = []
        for r in res_list:
            t = self.lastw.get(r)
            if t is not None:
                toks.append(t)
        self._wait(eng, toks)

    def barrier(self):
        toks = []
        for e in ('pe', 'act', 'dve', 'pool'):
            if self.ccnt[e] > 0:
                toks.append(('c_' + e, self.ccnt[e], 'x'))
        for i in range(NDS):
            if self.dcnt[i] > 0:
                toks.append(('d%d' % i, self.dcnt[i], 'dma'))
        for e in self.E:
            self._wait(e, toks)


class Ring:
    def __init__(self, items):
        self.items = items
        self.i = 0

    def next(self):
        it = self.items[self.i]
        self.i = (self.i + 1) % len(self.items)
        return it


def _bucket_table(nmax):
    import jax
    import jax.numpy as jnp
    cpu = jax.devices('cpu')[0]
    with jax.default_device(cpu):
        n = jnp.arange(nmax, dtype=jnp.int32)
        exact = 16
        nf = jnp.maximum(n, 1).astype(jnp.float32)
        big = exact + (jnp.log(nf / exact) / math.log(4096 / exact) * (32 - exact)).astype(jnp.int32)
        b = jnp.where(n < exact, n, jnp.minimum(big, 31))
        return np.asarray(b).astype(np.int64)


def host_constants():
    c = {}
    bk = _bucket_table(8448)

    def onehot(ns, valid):
        L = len(ns)
        oh = np.zeros((33, L), np.float32)
        nn = np.clip(ns, 0, len(bk) - 1)
        idx = np.where(valid, bk[nn], 32)
        oh[idx, np.arange(L)] = 1.0
        return oh

    n = np.arange(8448) - 127
    c['oh_slc'] = onehot(n, n >= 0)
    n = np.arange(768) - 127
    c['oh_win'] = onehot(n, (n >= 0) & (n <= 512))
    n = np.arange(4096) - 2063
    c['oh_cmp'] = onehot(n, n >= 0)
    for d, win, L in ((1, 128, 384), (4, 512, 768), (16, 2048, 2304)):
        n = np.arange(L) - 127
        c['oh_d%d' % d] = onehot(n, (n >= 0) & (n <= win) & (n % d == 0))
    c['identb'] = np.eye(128, dtype=np.float32).astype(ml_dtypes.bfloat16)
    c['identf'] = np.eye(128, dtype=np.float32)
    c['antib'] = np.ascontiguousarray(np.eye(128, dtype=np.float32)[::-1]).astype(ml_dtypes.bfloat16)
    t = np.arange(S)
    cur = t // 64
    j = np.arange(32)[None, :]
    valid = (j <= cur[:, None])
    forced = (j == 0) | (j == cur[:, None]) | (j == cur[:, None] - 1)
    A = valid.astype(np.float32)
    Bc = np.where(valid, np.where(forced, 1000.0, 0.0), -1.0).astype(np.float32)
    c['selA'] = np.ascontiguousarray(A.reshape(16, 128, 32).transpose(1, 0, 2))
    c['selB'] = np.ascontiguousarray(Bc.reshape(16, 128, 32).transpose(1, 0, 2))
    cs = np.arange(128) * 16
    ss = np.arange(32) * 64
    ov = np.clip(np.minimum(cs[:, None] + 32, ss[None, :] + 64) - np.maximum(cs[:, None], ss[None, :]), 0, None) / 32.0
    ov[127] = 0
    c['ov_p'] = ov.astype(np.float32).astype(ml_dtypes.bfloat16)
    es = np.zeros((32, 16, 128), np.float32)
    for a in range(16):
        es[2 * a, a, :64] = 1
        es[2 * a + 1, a, 64:] = 1
    c['esel'] = es.astype(ml_dtypes.bfloat16)
    cs = np.arange(512) * 16
    ss = np.arange(128) * 64
    ovs = np.clip(np.minimum(cs[:, None] + 32, ss[None, :] + 64) - np.maximum(cs[:, None], ss[None, :]), 0, None) / 32.0
    ovs[511] = 0
    c['ov_s'] = np.ascontiguousarray(ovs.reshape(4, 128, 128).transpose(1, 0, 2)).astype(np.float32).astype(ml_dtypes.bfloat16)
    sb_ = np.zeros((4, 136), np.float32)
    sb_[:, [0, 127, 128]] = 1000.0
    sb_[:, 129:] = -1.0
    c['selB_s'] = sb_
    e2 = np.zeros((2, 128), np.float32)
    e2[0, :64] = 1
    e2[1, 64:] = 1
    c['e2'] = e2.astype(ml_dtypes.bfloat16)
    c['pidx'] = np.arange(128, dtype=np.float32).reshape(128, 1)
    return c


class Prog:
    def __init__(self):
        self.nc = bass.Bass("TRN2", target_bir_lowering=False)
        self.sch = Sch(self.nc)
        self.din_names = []

    def din(self, name, shape, dt=F32):
        self.din_names.append(name)
        return self.nc.dram_tensor(name, list(shape), dt, kind="ExternalInput").ap()

    def dout(self, name, shape, dt=F32):
        return self.nc.dram_tensor(name, list(shape), dt, kind="ExternalOutput").ap()

    def dint(self, name, shape, dt=F32):
        return self.nc.dram_tensor(name, list(shape), dt, kind="Internal").ap()

    def sb(self, name, shape, dt=F32):
        return self.nc.alloc_sbuf_tensor(name, list(shape), dt).ap()


def AP(ap, off, dims):
    return bass.AP(ap.tensor, ap.offset + off, [list(d) for d in dims])


def build():
    pg = Prog()
    nc = pg.nc
    sch = pg.sch
    hc = host_constants()

    x_p = pg.din('x_p', [S, D])
    x_s = pg.din('x_s', [NSAMP, D])
    p_p = pg.din('p_p', [4, S, 256])
    p_s = pg.din('p_s', [4, NSAMP, 256])
    cache_cmp = pg.din('cache_cmp', [2, NPHYS * 128, 512])
    cache_slc = pg.din('cache_slc', [2, NPHYS * 128, 512])
    st_a = pg.din('st_a', [2, 4, 512, 512])
    st_b = pg.din('st_b', [4, 2048, 512])
    ptab = pg.din('ptab', [4, 64], I32)
    rel_bias = pg.din('rel_bias', [32, 16])
    norm_g = pg.din('norm_g', [4, D])
    a_w_in = pg.din('a_w_in', [2, D, 5680])
    a_gate_b = pg.din('a_gate_b', [2, 48])
    a_qk_norm = pg.din('a_qk_norm', [2, 4, 64])
    a_phi_w1 = pg.din('a_phi_w1', [2, 2, 2, 1024, 256])
    a_phi_b1 = pg.din('a_phi_b1', [2, 2, 256])
    a_phi_w2 = pg.din('a_phi_w2', [2, 2, 256, 64])
    a_phi_b2 = pg.din('a_phi_b2', [2, 2, 64])
    a_w_out = pg.din('a_w_out', [2, D, D])
    kv_norm_g = pg.din('kv_norm_g', [D])
    b_w_kv = pg.din('b_w_kv', [D, 512])
    b_k_norm = pg.din('b_k_norm', [64])
    b_w_in = pg.din('b_w_in', [2, D, 4096])
    b_q_norm = pg.din('b_q_norm', [2, 3, 64])
    b_w_out = pg.din('b_w_out', [2, D, D])
    ple_w = pg.din('ple_w', [4, 256, D])
    ple_gate_w = pg.din('ple_gate_w', [4, D, D])
    cdram = {}
    for k, v in hc.items():
        dt = BF16 if v.dtype == ml_dtypes.bfloat16 else F32
        cdram[k] = pg.din('c_' + k, v.shape, dt)

    y_p = pg.dout('y_p', [S, D])
    y_s = pg.dout('y_s', [NSAMP, D])
    o_pc = pg.dout('o_pc', [2, S, 512])
    o_ps = pg.dout('o_ps', [2, S, 512])
    o_pw = pg.dout('o_pw', [2, 512, 512])
    o_pb = pg.dout('o_pb', [S, 512])
    o_sc = pg.dout('o_sc', [2, NSAMP, 512])
    o_ss = pg.dout('o_ss', [2, NSAMP, 512])
    o_sw = pg.dout('o_sw', [2, 4, 512, 512])
    o_sb = pg.dout('o_sb', [4, 2048, 512])

    gz_d = pg.dint('gz_d', [NTOK, 3072])
    fd = {}
    for k in ('slc', 'win', 'cmp', 'd1', 'd4', 'd16'):
        fd[k] = pg.dint('fd_' + k, [16, hc['oh_' + k].shape[1]], BF16)

    def psum(name, dt=F32, n=512):
        return nc.alloc_psum_tensor(name, [128, n], dt).ap()
    G = Ring([(psum('G0'), 'G0'), (psum('G1'), 'G1')])
    SS = Ring([(psum('S0'), 'S0'), (psum('S1'), 'S1')])
    OO = Ring([(psum('O0'), 'O0'), (psum('O1'), 'O1')])
    SS4 = Ring(SS.items + G.items)
    TB = (psum('TB', BF16, 1024), 'TB')
    TF = (psum('TF'), 'TF')

    identb = pg.sb('identb', [128, 128], BF16)
    antib = pg.sb('antib', [128, 128], BF16)
    tab33 = pg.sb('tab33', [33, 16], F32)
    xT = pg.sb('xT', [128, 8, NTOK], BF16)
    qT = pg.sb('qT', [128, 8, NTOK], BF16)
    ksT = pg.sb('ksT', [128, 2, S], BF16)
    kwT = pg.sb('kwT', [128, 2, S], BF16)
    vs_aug = pg.sb('vs_aug', [128, 16, 4, 65], BF16)
    vw_aug = pg.sb('vw_aug', [128, 16, 4, 65], BF16)
    gates = pg.sb('gates', [128, NT, 48], F32)
    wbf_i = [0]
    hin_i = [0]
    xnb = pg.sb('xnb', [128, D], BF16)
    class Prox:
        def __init__(self, name, base):
            self.name = name
            self.base = base
            self.members = [(base, name)]
            self.i = 0

        def cur(self):
            return self.members[self.i % len(self.members)]

        def __getitem__(self, k):
            return self.cur()[0][k]

        @property
        def tensor(self):
            return self.cur()[0].tensor

        @property
        def offset(self):
            return self.cur()[0].offset

        def rot(self):
            self.i += 1
            sch.alias[self.name] = self.cur()[1]

        def set_members(self, extra):
            self.members = [(self.base, self.name)] + list(extra)
            self.i = 0
            sch.alias[self.name] = self.name

    sq = Prox('sq', pg.sb('sq', [128, 512], F32))
    tmpf = Prox('tmpf', pg.sb('tmpf', [128, 512], F32))
    qnb = Prox('qnb', pg.sb('qnb', [128, 512], BF16))
    small = Prox('small', pg.sb('small', [128, 64], F32))
    PROXIES = [sq, tmpf, qnb, small]

    def rot_scratch():
        for p_ in PROXIES:
            p_.rot()
    rowb = [pg.sb('rowb%d' % i, [128, 512], F32) for i in range(2)]
    rowb_i = [0]
    gq = pg.sb('gq', [128, 4, 64], F32)
    gbq = pg.sb('gbq', [128, 3, 64], F32)
    gateb = pg.sb('gateb', [128, 48], F32)
    kccT = pg.sb('kccT', [128, 2, 128], BF16)
    vcc_aug = pg.sb('vcc_aug', [128, 4, 97], BF16)
    w2s = pg.sb('w2s', [128, 2, 64], BF16)
    b1s = pg.sb('b1s', [128, 2], F32)
    b2s = pg.sb('b2s', [128, 64], F32)
    hidT = pg.sb('hidT', [128, 2, 128], BF16)
    ovp = pg.sb('ovp', [128, 32], BF16)
    Pt_i = [0]
    yacc = pg.sb('yacc', [128, 256], F32)
    ybf = pg.sb('ybf', [128, 256], BF16)
    osb = pg.sb('osb', [128, 4, 100], F32)
    rden = pg.sb('rden', [128, 4], F32)
    sc1 = pg.sb('sc1', [128, 32], F32)
    sc2 = pg.sb('sc2', [128, 32], F32)
    mx8 = pg.sb('mx8', [128, 16], F32)
    mnegb = pg.sb('mnegb', [128, 32], BF16)
    mnegT = pg.sb('mnegT', [32, 4, 128], BF16)
    h2_i = [0]

    from contextlib import ExitStack
    LT = {}
    _uid = [0]

    class Scope:
        def __init__(self, spec, name=None):
            self.spec = spec
            self.es = ExitStack()
            self.name = name

        def __enter__(self):
            if self.name:
                self.es.enter_context(nc.named_scope(self.name))
            for name, shape, dt in self.spec:
                _uid[0] += 1
                h = self.es.enter_context(nc.sbuf_tensor('%s_u%d' % (name, _uid[0]), list(shape), dt))
                LT[name] = h.ap() if hasattr(h, 'ap') and callable(getattr(h, 'ap')) else h
            names = [n for n, _, _ in self.spec]
            for p_ in PROXIES:
                ex = [(LT[n], n) for n in names if n.startswith(p_.name + '_x')]
                if ex:
                    p_.set_members(ex)
            if 'tbring' in names:
                cur_TB[0] = Ring([TB, (TF[0].bitcast(BF16), 'TF')])
                cur_G[0] = Ring(G.items + SS.items + OO.items)
            return self

        def __exit__(self, *a):
            sch.barrier()
            names = [n for n, _, _ in self.spec]
            for p_ in PROXIES:
                if any(n.startswith(p_.name + '_x') for n in names):
                    p_.set_members([])
            if 'tbring' in names:
                cur_TB[0] = Ring([TB])
                cur_G[0] = G
            self.es.close()
            return False

    cur_TB = [None]
    cur_G = [G]
    SC_NORM = [('gt', [128, D], F32), ('junk', [128, D], F32), ('hin0', [128, D], F32), ('hin1', [128, D], F32)]
    SC_PROJ = [('wst', [128, 8, 512], F32), ('wbf', [128, 2, 8, 512], BF16),
               ('kcT', [128, 2, S], BF16), ('vcT', [128, 2, S], BF16),
               ('sq_x1', [128, 512], F32), ('sq_x2', [128, 512], F32), ('tmpf_x1', [128, 512], F32), ('tmpf_x2', [128, 512], F32),
               ('qnb_x1', [128, 512], BF16), ('qnb_x2', [128, 512], BF16), ('small_x1', [128, 64], F32), ('small_x2', [128, 64], F32), ('tbring', [128, 1], F32)]
    SC_ATT = [('G1_0', [128, 16, 4, 128], BF16), ('G1_1', [128, 16, 4, 128], BF16), ('G2_0', [128, 5, 4, 128], BF16),
              ('G2_1', [128, 5, 4, 128], BF16), ('G3_0', [128, 4, 128], BF16), ('G3_1', [128, 4, 128], BF16),
              ('gzt0', [128, 3, 256], F32), ('gzt1', [128, 3, 256], F32), ('selA', [128, 16, 32], F32), ('selB', [128, 16, 32], F32),
              ('esel', [32, 16, 128], BF16), ('Pt0', [128, 512], BF16), ('Pt1', [128, 512], BF16), ('Pt2', [128, 512], BF16), ('Pt3', [128, 512], BF16)]
    SC_OUT = [('wst', [128, 8, 512], F32), ('wo_bf', [128, 8, D], BF16), ('wg_bf', [128, 8, D], BF16), ('wp_bf', [128, 2, D], BF16),
              ('h1', [128, D], F32), ('h1b', [128, D], BF16), ('h1T', [128, 8, 128], BF16), ('sg', [128, D], F32),
              ('h2_0', [128, D], F32), ('hin0', [128, D], F32),
              ('pin', [128, 256], F32), ('pinb', [128, 256], BF16), ('pT', [128, 2, 128], BF16)]

    cur_TB[0] = Ring([TB])
    sch.dma('sp', identb[:], cdram['identb'][:], writes=['identb'])
    sch.dma('sp', antib[:], cdram['antib'][:], writes=['identb'])
    sch.dma('sp', ovp[:], cdram['ov_p'][:], writes=['ovp'])
    sch.op('dve', lambda e: e.memset(tab33[:], MASKV), writes=['tab33'])
    sch.dma('sp', tab33[0:32, :], rel_bias[:], writes=['tab33'])
    with Scope([('ohs', [33, 512], F32), ('fstage', [16, 512], BF16)]):
        ohs, fstage = LT['ohs'], LT['fstage']
        for k in ('slc', 'win', 'cmp', 'd1', 'd4', 'd16'):
            L = hc['oh_' + k].shape[1]
            for c0 in range(0, L, 512):
                cw = min(512, L - c0)
                sch.dma('sp', ohs[:, :cw], cdram['oh_' + k][:, c0:c0 + cw], writes=['ohs'])
                g, gr = G.next()
                sch.op('pe', lambda e: e.matmul(g[0:16, :cw], lhsT=tab33[:, :], rhs=ohs[:, :cw], start=True, stop=True),
                       reads=['tab33', 'ohs'], writes=[gr])
                sch.op('act', lambda e: e.copy(out=fstage[:, :cw], in_=g[0:16, :cw]), reads=[gr], writes=['fstage'])
                sch.dma('sp', fd[k][:, c0:c0 + cw], fstage[:, :cw], reads=['fstage'], writes=['fd_' + k])

    def qv(base, nblk, t):
        ps_ = nblk * NTOK
        if t < NTP:
            return AP(base, t * nblk * 128, [[ps_, 128], [128, nblk], [1, 128]])
        return AP(base, NTP * nblk * 128, [[ps_, 128], [NSAMP, nblk], [1, NSAMP]])

    def bc_row(ap1, P=128):
        n = ap1.shape[-1]
        return AP(ap1, 0, [[0, P], [1, n]])

    def load_w(W2, c0, cw, nk=8):
        wst = LT['wst']
        wbf = [LT['wbf'][:, 0], LT['wbf'][:, 1]]
        N = W2.shape[1]
        i = wbf_i[0]
        wbf_i[0] ^= 1
        nst = wst.shape[1]
        for k0 in range(0, nk, nst):
            kn = min(nst, nk - k0)
            src = AP(W2, c0 + k0 * 128 * N, [[N, 128], [128 * N, kn], [1, cw]])
            sch.dma('sp', wst[:, :kn, :cw], src, writes=['wst'])
            sch.op('pool', lambda e: e.tensor_copy(out=wbf[i][:, k0:k0 + kn, :cw], in_=wst[:, :kn, :cw]),
                   reads=['wst'], writes=['wbf%d' % i])
        return wbf[i], 'wbf%d' % i

    def gemm(lhsT_of, lres, W2, chunks, tiles, consumer, nk=8):
        seq = [(ci, t) for ci in range(len(chunks)) for t in tiles]
        wstate = {}

        def issue(idx):
            ci, t = seq[idx]
            c0, cw = chunks[ci]
            if ci not in wstate:
                wstate[ci] = load_w(W2, c0, cw, nk)
            wb, wres = wstate[ci]
            P, tok0 = tile_info(t)
            g, gr = cur_G[0].next()
            for kc in range(nk):
                sch.op('pe', lambda e: e.matmul(g[:P, :cw], lhsT=lhsT_of(kc, t), rhs=wb[:, kc, :cw],
                                                start=(kc == 0), stop=(kc == nk - 1)),
                       reads=[lres, wres], writes=[gr])
            return (ci, t, P, tok0, g, gr)
        cur = issue(0)
        for idx in range(len(seq)):
            nxt = issue(idx + 1) if idx + 1 < len(seq) else None
            rot_scratch()
            consumer(*cur)
            cur = nxt

    def rstd_from_ss(P, n, ss_ap, out_ap, inv):
        sch.op('dve', lambda e: e.tensor_scalar(out=ss_ap, in0=ss_ap, scalar1=inv, scalar2=EPS,
                                                op0=ALU.mult, op1=ALU.add), reads=['small'], writes=['small'])
        sch.op('act', lambda e: e.sqrt(out=ss_ap, in_=ss_ap), reads=['small'], writes=['small'])
        sch.op('dve', lambda e: e.reciprocal(out=out_ap, in_=ss_ap), reads=['small'], writes=['small'])

    def headnorm(P, src3, nh, gain3, out3, out_res, src_res, extra_reads=()):
        sq3 = sq[:P, :nh * 64].rearrange("p (h d) -> p h d", d=64)
        sch.op('act', lambda e: e.activation(out=sq3, in_=src3, func=AF.Square), reads=[src_res], writes=['sq'])
        sch.op('dve', lambda e: e.tensor_reduce(out=small[:P, 0:nh], in_=sq3, axis=AX.X, op=ALU.add),
               reads=['sq'], writes=['small'])
        rstd_from_ss(P, nh, small[:P, 0:nh], small[:P, 16:16 + nh], 1.0 / 64)
        t3 = tmpf[:P, :nh * 64].rearrange("p (h d) -> p h d", d=64)
        sch.op('dve', lambda e: e.tensor_tensor(out=t3, in0=src3, in1=small[:P, 16:16 + nh].to_broadcast((P, nh, 64)) if False else AP(small, 16, [[64, P], [1, nh], [0, 64]]),
                                                op=ALU.mult), reads=[src_res, 'small'], writes=['tmpf'])
        sch.op('dve', lambda e: e.tensor_tensor(out=out3, in0=t3, in1=gain3, op=ALU.mult),
               reads=['tmpf'] + list(extra_reads), writes=[out_res])

    def transposes_to(P, src2, src_res, nblk, dst3, dst_res, src_dims=None):
        tb, tbr = cur_TB[0].next()
        for b in range(nblk):
            sch.op('pe', lambda e: e.transpose(out=tb[:, b * 128:b * 128 + P], in_=src2[:, b * 128:(b + 1) * 128],
                                               identity=identb[:P, :P]),
                   reads=[src_res, 'identb'], writes=[tbr])
        tb3 = AP(tb, 0, [[1024, 128], [128, nblk], [1, P]]) if src_dims is None else AP(tb, 0, [[1024, 128]] + src_dims)
        sch.op('act', lambda e: e.copy(out=dst3, in_=tb3), reads=[tbr], writes=[dst_res])

    def phase_norm(src_of, gain_ap):
        gt = LT['gt']
        junk = LT['junk']
        hin = [LT['hin0'], LT['hin1']]
        sch.dma('sp', gt[:], bc_row(gain_ap), writes=['gt'])
        for t in range(NT):
            P, tok0 = tile_info(t)
            i = hin_i[0]
            hin_i[0] ^= 1
            hr = 'hin%d' % i
            sch.dma('sp', hin[i][:P], src_of(t), reads=['hd%d' % t], writes=[hr])
            sch.op('act', lambda e: e.activation(out=junk[:P], in_=hin[i][:P], func=AF.Square,
                                                 accum_out=small[:P, 0:1]), reads=[hr], writes=['junk', 'small'])
            rstd_from_ss(P, 1, small[:P, 0:1], small[:P, 1:2], 1.0 / D)
            sch.op('dve', lambda e: e.scalar_tensor_tensor(out=xnb[:P], in0=hin[i][:P], scalar=small[:P, 1:2],
                                                           in1=gt[:P], op0=ALU.mult, op1=ALU.mult),
                   reads=[hr, 'small', 'gt'], writes=['xnb'])
            transposes_to(P, xnb[:P], 'xnb', 8, xT[:, :, tok0:tok0 + P], 'xT')

    def h_src(layer):
        def f(t):
            P, tok0 = tile_info(t)
            if layer == 0:
                return x_p[tok0:tok0 + P, :] if t < NTP else x_s[:, :]
            return y_p[tok0:tok0 + P, :] if t < NTP else y_s[:, :]
        return f

    def a_project(li):
        W = a_w_in[li]
        kcT = LT['kcT']
        vcT = LT['vcT']
        sch.dma('sp', gq[:].rearrange("p a d -> p (a d)"), bc_row(a_qk_norm[li].rearrange("a d -> (a d)")), writes=['gq'])
        sch.op('dve', lambda e: e.tensor_scalar(out=gq[:, 0, :], in0=gq[:, 0, :], scalar1=SCALE, scalar2=None,
                                                op0=ALU.mult), reads=['gq'], writes=['gq'])
        sch.dma('sp', gateb[:], bc_row(a_gate_b[li]), writes=['gateb'])
        chunks = [(0, 512), (512, 512), (1024, 512), (1536, 512), (2048, 512), (2560, 48)] + \
                 [(2608 + 512 * z, 512) for z in range(6)]

        def consumer(ci, t, P, tok0, g, gr):
            if ci < 2:
                src3 = g[:P, :512].rearrange("p (h d) -> p h d", d=64)
                gain3 = AP(gq, 0, [[256, P], [0, 8], [1, 64]])
                out4 = AP(qnb, 0, [[512, P], [64, 2], [128, 4], [1, 64]])
                sq3 = sq[:P, :512].rearrange("p (h d) -> p h d", d=64)
                sch.op('act', lambda e: e.activation(out=sq3, in_=src3, func=AF.Square), reads=[gr], writes=['sq'])
                sch.op('dve', lambda e: e.tensor_reduce(out=small[:P, 0:8], in_=sq3, axis=AX.X, op=ALU.add),
                       reads=['sq'], writes=['small'])
                rstd_from_ss(P, 8, small[:P, 0:8], small[:P, 16:24], 1.0 / 64)
                t3 = tmpf[:P, :512].rearrange("p (h d) -> p h d", d=64)
                sch.op('dve', lambda e: e.tensor_tensor(out=t3, in0=src3, in1=AP(small, 16, [[64, P], [1, 8], [0, 64]]),
                                                        op=ALU.mult), reads=[gr, 'small'], writes=['tmpf'])
                t4 = tmpf[:P, :512].rearrange("p (a r d) -> p a r d", a=2, r=4)
                g4 = AP(gq, 0, [[256, P], [0, 2], [0, 4], [1, 64]])
                sch.op('dve', lambda e: e.tensor_tensor(out=out4, in0=t4, in1=g4, op=ALU.mult),
                       reads=['tmpf', 'gq'], writes=['qnb'])
                transposes_to(P, qnb[:P], 'qnb', 4, qv(qT, 8, t)[:, 4 * ci:4 * ci + 4, :], 'qT')
            elif ci < 5:
                kind = ci - 2
                ri = rowb_i[0]
                rowb_i[0] ^= 1
                rb = rowb[ri]
                rr = 'rowb%d' % ri
                if kind == 0:
                    sch.op('act', lambda e: e.copy(out=rb[:P, :], in_=g[:P, :512]), reads=[gr], writes=[rr])
                else:
                    src3 = g[:P, 0:256].rearrange("p (h d) -> p h d", d=64)
                    gain3 = AP(gq, (1 + kind) * 64, [[256, P], [0, 4], [1, 64]])
                    out3 = rb[:P, 0:256].rearrange("p (h d) -> p h d", d=64)
                    headnorm(P, src3, 4, gain3, out3, rr, gr, extra_reads=['gq'])
                    sch.op('act', lambda e: e.copy(out=rb[:P, 256:512], in_=g[:P, 256:512]), reads=[gr], writes=[rr])
                if t < NTP:
                    if kind == 0:
                        sch.dma('pool', o_pc[li, tok0:tok0 + P, :], rb[:P, :], reads=[rr], writes=['o_pc'])
                    elif kind == 1:
                        sch.dma('pool', o_ps[li, tok0:tok0 + P, :], rb[:P, :], reads=[rr], writes=['o_ps'])
                    elif t >= 12:
                        sch.dma('pool', o_pw[li, tok0 - 1536:tok0 - 1536 + P, :], rb[:P, :], reads=[rr], writes=['o_pw'])
                else:
                    if kind == 0:
                        sch.dma('pool', o_sc[li, :, :], rb[:P, :], reads=[rr], writes=['o_sc'])
                    elif kind == 1:
                        sch.dma('pool', o_ss[li, :, :], rb[:P, :], reads=[rr], writes=['o_ss'])
                    else:
                        for b in range(4):
                            sch.dma('pool', o_sw[li, b, 508:512, :], rb[4 * b:4 * b + 4, :], reads=[rr], writes=['o_sw'])
                if t < NTP:
                    sch.op('dve', lambda e: e.tensor_copy(out=qnb[:P, :], in_=rb[:P, :]), reads=[rr], writes=['qnb'])
                    if kind == 0:
                        transposes_to(P, qnb[:P, 0:256], 'qnb', 2, kcT[:, :, tok0:tok0 + P], 'kcT')
                        transposes_to(P, qnb[:P, 256:512], 'qnb', 2, vcT[:, :, tok0:tok0 + P], 'vcT')
                    else:
                        kt, ktr, va, var = (ksT, 'ksT', vs_aug, 'vs_aug') if kind == 1 else (kwT, 'kwT', vw_aug, 'vw_aug')
                        transposes_to(P, qnb[:P, 0:256], 'qnb', 2, kt[:, :, tok0:tok0 + P], ktr)
                        sch.op('pool', lambda e: e.tensor_copy(out=va[:, t, :, 0:64],
                                                                in_=qnb[:, 256:512].rearrange("p (g d) -> p g d", d=64)),
                               reads=['qnb'], writes=[var])
            elif ci == 5:
                sch.op('dve', lambda e: e.tensor_tensor(out=gates[:P, t, :], in0=g[:P, :48], in1=gateb[:P, :], op=ALU.add),
                       reads=[gr, 'gateb'], writes=['gates'])
                sch.op('act', lambda e: e.activation(out=gates[:P, t, :], in_=gates[:P, t, :], func=AF.Sigmoid),
                       reads=['gates'], writes=['gates'])
            else:
                zc = ci - 6
                br, hh = zc // 2, zc % 2
                ri = rowb_i[0]
                rowb_i[0] ^= 1
                rb = rowb[ri]
                rr = 'rowb%d' % ri
                sch.op('act', lambda e: e.activation(out=tmpf[:P, :], in_=g[:P, :512], func=AF.Silu), reads=[gr], writes=['tmpf'])
                gb = AP(gates, t * 48 + br * 16 + hh * 8, [[NT * 48, P], [1, 8], [0, 64]])
                sch.op('dve', lambda e: e.tensor_tensor(out=rb[:P, :].rearrange("p (h d) -> p h d", d=64),
                                                        in0=tmpf[:P, :].rearrange("p (h d) -> p h d", d=64), in1=gb, op=ALU.mult),
                       reads=['tmpf', 'gates'], writes=[rr])
                sch.dma('pool', gz_d[tok0:tok0 + P, zc * 512:(zc + 1) * 512], rb[:P, :], reads=[rr], writes=['gz_d'])

        gemm(lambda kc, t: xT[:, kc, tile_info(t)[1]:tile_info(t)[1] + tile_info(t)[0]], 'xT', W, chunks, list(range(NT)), consumer)

    def compress_prompt(li):
        wst = LT['wst']
        w1s = LT['wbf'].rearrange("p a k (b h) -> p (a k b) h", h=256)
        kcT = LT['kcT']
        vcT = LT['vcT']
        sch.op('dve', lambda e: e.memset(kccT[:], 0.0), writes=['kccT'])
        sch.op('dve', lambda e: e.memset(vcc_aug[:], 0.0), writes=['vcc_aug'])
        for kv in range(2):
            srcT, sres = (kcT, 'kcT') if kv == 0 else (vcT, 'vcT')
            for half in range(2):
                for ab in range(2):
                    src = AP(a_phi_w1[li, kv, ab], 0, [[256, 64], [64 * 256, 16], [1, 256]])
                    sch.dma('sp', wst[half * 64:half * 64 + 64, 0:8, :].rearrange("p a (b h) -> p (a b) h", h=256), src, writes=['wst'])
                    sch.op('pool', lambda e: e.tensor_copy(out=w1s[half * 64:half * 64 + 64, ab * 16:(ab + 1) * 16, :],
                                                           in_=wst[half * 64:half * 64 + 64, 0:8, :].rearrange("p a (b h) -> p (a b) h", h=256)),
                           reads=['wst'], writes=['wbf0', 'wbf1'])
            sch.dma('sp', wst[:, 0, 0:128].rearrange("p (a d) -> p a d", d=64),
                    AP(a_phi_w2[li, kv], 0, [[64, 128], [128 * 64, 2], [1, 64]]), writes=['wst'])
            sch.op('pool', lambda e: e.tensor_copy(out=w2s[:], in_=wst[:, 0, 0:128].rearrange("p (a d) -> p a d", d=64)),
                   reads=['wst'], writes=['w2s'])
            for hh_ in range(2):
                sch.dma('sp', b1s[:, hh_:hh_ + 1], AP(a_phi_b1[li, kv], hh_ * 128, [[1, 128], [1, 1]]), writes=['b1s'])
            sch.dma('sp', b2s[:], bc_row(a_phi_b2[li, kv]), writes=['b2s'])
            for g in range(4):
                half, ch = g % 2, g // 2
                hs = slice(half * 64, half * 64 + 64)
                for hh in range(2):
                    ps, psr = SS.next()
                    n = 0
                    for ab in range(2):
                        for s in range(16):
                            rhs = AP(srcT, half * 64 * (2 * S) + ch * S + 16 * ab + s, [[2 * S, 64], [16, 127]])
                            sch.op('pe', lambda e: e.matmul(ps[:, 0:127], lhsT=w1s[hs, ab * 16 + s, hh * 128:(hh + 1) * 128],
                                                            rhs=rhs, start=(n == 0), stop=(n == 31)),
                                   reads=['wbf0', 'wbf1', sres], writes=[psr])
                            n += 1
                    sch.op('act', lambda e: e.activation(out=hidT[:, hh, 0:127], in_=ps[:, 0:127], func=AF.Silu,
                                                         bias=b1s[:, hh:hh + 1]), reads=[psr, 'b1s'], writes=['hidT'])
                po, por = OO.next()
                for hh in range(2):
                    sch.op('pe', lambda e: e.matmul(po[0:127, 0:64], lhsT=hidT[:, hh, 0:127], rhs=w2s[:, hh, :],
                                                    start=(hh == 0), stop=(hh == 1)), reads=['hidT', 'w2s'], writes=[por])
                sch.op('dve', lambda e: e.tensor_tensor(out=tmpf[0:127, 0:64], in0=po[0:127, 0:64], in1=b2s[0:127, :], op=ALU.add),
                       reads=[por, 'b2s'], writes=['tmpf'])
                if kv == 0:
                    src3 = tmpf[0:127, 0:64].rearrange("p (h d) -> p h d", d=64)
                    gain3 = AP(gq, 64, [[256, 127], [0, 1], [1, 64]])
                    sq3 = sq[:127, :64].rearrange("p (h d) -> p h d", d=64)
                    sch.op('act', lambda e: e.activation(out=sq3, in_=src3, func=AF.Square), reads=['tmpf'], writes=['sq'])
                    sch.op('dve', lambda e: e.tensor_reduce(out=small[:127, 0:1], in_=sq3, axis=AX.X, op=ALU.add),
                           reads=['sq'], writes=['small'])
                    rstd_from_ss(127, 1, small[:127, 0:1], small[:127, 16:17], 1.0 / 64)
                    if half == 0:
                        sch.op('dve', lambda e: e.memset(qnb[:, 0:128], 0.0), writes=['qnb'])
                    sch.op('dve', lambda e: e.scalar_tensor_tensor(out=qnb[0:127, half * 64:half * 64 + 64], in0=tmpf[0:127, 0:64],
                                                                   scalar=small[0:127, 16:17], in1=AP(gq, 64, [[256, 127], [1, 64]]),
                                                                   op0=ALU.mult, op1=ALU.mult),
                           reads=['tmpf', 'small', 'gq'], writes=['qnb'])
                    if half == 1:
                        transposes_to(128, qnb[:, 0:128], 'qnb', 1, kccT[:, ch:ch + 1, :], 'kccT')
                else:
                    sch.op('act', lambda e: e.copy(out=vcc_aug[0:127, g, 0:64], in_=tmpf[0:127, 0:64]), reads=['tmpf'], writes=['vcc_aug'])
        for g in range(4):
            sch.op('dve', lambda e: e.memset(vcc_aug[0:127, g, 64:65], 1.0), writes=['vcc_aug'])
            sch.op('pool', lambda e: e.tensor_copy(out=vcc_aug[:, g, 65:97], in_=ovp[:, :]), reads=['ovp'], writes=['vcc_aug'])

    pend = [None]
    cur_SS = [SS]
    late = []

    def run_late():
        while late:
            late.pop(0)()

    def flush_pv():
        p = pend[0]
        pend[0] = None
        if p is not None:
            p[0]()
            if p[1] is not None:
                p[1]()

    def flush_all():
        flush_pv()
        run_late()

    def att_tile(nk, P, kT_ap, kres, qT_rhs, G_rhs, gres, extra, v_rhs, vres, o, ores, first, last, vw, score=None, after=None):
        N = 4 * P
        Pt = [LT['Pt0'], LT['Pt1'], LT['Pt2'], LT['Pt3']]
        s, sr = cur_SS[0].next()
        nmm = 2 + len(extra)
        sch.op('pe', lambda e: e.matmul(s[:nk, :N], lhsT=kT_ap, rhs=qT_rhs, start=True, stop=False),
               reads=[kres, 'qT'], writes=[sr])
        sch.op('pe', lambda e: e.matmul(s[:nk, :N], lhsT=antib[0:nk, 128 - nk:128], rhs=G_rhs, start=False, stop=(nmm == 2)),
               reads=['identb', gres], writes=[sr])
        for xi, xt in enumerate(extra):
            if len(xt) == 3:
                xl, xr, xres = xt
                sout = s[:nk, :N]
            else:
                hf_, xl, xr, xres = xt
                sout = s[64 * hf_:64 * hf_ + 64, :N]
            sch.op('pe', lambda e: e.matmul(sout, lhsT=xl, rhs=xr, start=False, stop=(xi == len(extra) - 1)),
                   reads=xres, writes=[sr])
        pi = Pt_i[0]
        Pt_i[0] = (pi + 1) % 4
        pt = Pt[pi]
        pr = 'Pt%d' % pi
        sch.op('act', lambda e: e.activation(out=pt[:nk, :N], in_=s[:nk, :N], func=AF.Exp), reads=[sr], writes=[pr])

        def pv():
            for r in range(4):
                sch.op('pe', lambda e: e.matmul(o[:P, r * 128:r * 128 + vw], lhsT=pt[:nk, r * P:(r + 1) * P], rhs=v_rhs,
                                                start=(first and r == 0), stop=(last and r == 3)), reads=[pr, vres], writes=[ores])
            if score is not None:
                srhs, sres2, o2, o2res = score
                for r in range(4):
                    sch.op('pe', lambda e: e.matmul(o2[:P, r * 128:(r + 1) * 128], lhsT=pt[:nk, r * P:(r + 1) * P], rhs=srhs,
                                                    start=(first and r == 0), stop=(last and r == 3)), reads=[pr, sres2], writes=[o2res])
        flush_pv()
        pend[0] = (pv, after)

    def toeplitz_load(dst, dres, fdk, base_off, pstep, nh_off, L, width):
        src = AP(fd[fdk], nh_off * L + base_off, [[pstep, 128], [L, 4], [1, width]])
        sch.dma('sp', dst, src, reads=['fd_' + fdk], writes=[dres])

    def a_attention_prompt(li):
        cur_SS[0] = SS4
        G1 = [LT['G1_0'], LT['G1_1']]
        G2 = [LT['G2_0'], LT['G2_1']]
        G3 = [LT['G3_0'], LT['G3_1']]
        gzt = [LT['gzt0'], LT['gzt1']]
        selA, selB, esel = LT['selA'], LT['selB'], LT['esel']
        sch.dma('sp', selA[:], cdram['selA'][:], writes=['selA'])
        sch.dma('sp', selB[:], cdram['selB'][:], writes=['selB'])
        sch.dma('sp', esel[:], cdram['esel'][:], writes=['esel'])
        Lslc = hc['oh_slc'].shape[1]
        for g in range(4):
            half, ch = g % 2, g // 2
            hs = slice(half * 64, half * 64 + 64)
            gi = g % 2
            for dl in range(16):
                toeplitz_load(G1[gi][:, dl], 'G1_%d' % gi, 'slc', 128 * dl, 1, 4 * g, Lslc, 128)
            for dl in range(5):
                toeplitz_load(G2[gi][:, dl], 'G2_%d' % gi, 'win', 128 * dl, 1, 4 * g, 768, 128)
            for b in range(NTP):
                P, tok0 = 128, b * 128
                qrhs = qv(qT, 8, b)[hs, 4 * ch:4 * ch + 4, :]
                bi = b % 2
                toeplitz_load(G3[bi][:], 'G3_%d' % bi, 'cmp', tok0, 16, 4 * g, 4096, 128)
                sch.dma('sp', gzt[bi][:], AP(gz_d, tok0 * 3072 + g * 256, [[3072, 128], [1024, 3], [1, 256]]),
                        reads=['gz_d'], writes=['gzt%d' % bi])
                o, ores = OO.next()

                def after_cmp(o=o, ores=ores, bi=bi, b=b):
                    o3 = AP(o, 0, [[512, 128], [128, 4], [1, 97]])
                    sch.op('dve', lambda e: e.tensor_scalar(out=rden[:, :], in0=AP(o, 64, [[512, 128], [128, 4]]), scalar1=1e-30,
                                                            scalar2=None, op0=ALU.max), reads=[ores], writes=['rden'])
                    sch.op('dve', lambda e: e.reciprocal(out=rden[:, :], in_=rden[:, :]), reads=['rden'], writes=['rden'])
                    sch.op('dve', lambda e: e.tensor_tensor(out=osb[:, :, 0:97], in0=o3, in1=AP(rden, 0, [[4, 128], [1, 4], [0, 97]]),
                                                            op=ALU.mult), reads=[ores, 'rden'], writes=['osb'])
                    sch.op('dve', lambda e: e.tensor_tensor(out=yacc[:, :].rearrange("p (r d) -> p r d", d=64), in0=osb[:, :, 0:64],
                                                            in1=gzt[bi][:, 0, :].rearrange("p (r d) -> p r d", d=64), op=ALU.mult),
                           reads=['osb', 'gzt%d' % bi], writes=['yacc'])
                    sch.op('dve', lambda e: e.tensor_reduce(out=sc1[:, :], in_=AP(osb, 65, [[400, 128], [1, 32], [100, 4]]),
                                                            axis=AX.X, op=ALU.add), reads=['osb'], writes=['sc1'])
                    sch.op('dve', lambda e: e.tensor_tensor(out=sc1[:, :], in0=sc1[:, :], in1=selA[:, b, :], op=ALU.mult),
                           reads=['sc1', 'selA'], writes=['sc1'])
                    sch.op('dve', lambda e: e.tensor_tensor(out=sc1[:, :], in0=sc1[:, :], in1=selB[:, b, :], op=ALU.add),
                           reads=['sc1', 'selB'], writes=['sc1'])
                    sch.op('dve', lambda e: e.max(out=mx8[:, 0:8], in_=sc1[:, :]), reads=['sc1'], writes=['mx8'])
                    sch.op('dve', lambda e: e.match_replace(out=sc2[:, :], in_to_replace=mx8[:, 0:8], in_values=sc1[:, :], imm_value=-2.0),
                           reads=['sc1', 'mx8'], writes=['sc2'])
                    sch.op('dve', lambda e: e.max(out=mx8[:, 8:16], in_=sc2[:, :]), reads=['sc2'], writes=['mx8'])
                    sch.op('dve', lambda e: e.tensor_scalar(out=sc2[:, :], in0=sc1[:, :], scalar1=mx8[:, 15:16], scalar2=None,
                                                            op0=ALU.is_ge), reads=['sc1', 'mx8'], writes=['sc2'])
                    sch.op('dve', lambda e: e.tensor_scalar(out=mnegb[:, :], in0=sc2[:, :], scalar1=-1.0, scalar2=-MASKV,
                                                            op0=ALU.add, op1=ALU.mult), reads=['sc2'], writes=['mnegb'])

                    def late_cmp():
                        tb, tbr = TB
                        sch.op('pe', lambda e: e.transpose(out=tb[0:32, 0:128], in_=mnegb[:, :], identity=identb[:, :]),
                               reads=['mnegb', 'identb'], writes=[tbr])
                        sch.op('act', lambda e: e.copy(out=mnegT[:, :, :], in_=AP(tb, 0, [[1024, 32], [0, 4], [1, 128]])), reads=[tbr], writes=['mnegT'])
                    late.append(late_cmp)
                att_tile(128, P, kccT[hs, ch, :], 'kccT', qrhs, G3[bi][:, :, :], 'G3_%d' % bi, [],
                         vcc_aug[:, g, :], 'vcc_aug', o, ores, True, True, 97, after=after_cmp)
                o, ores = OO.next()
                a0 = max(0, b - 4)

                def after_win(o=o, ores=ores, bi=bi):
                    branch_out(o, ores, gzt[bi][:, 2, :], 'gzt%d' % bi, False)
                for a in range(a0, b + 1):
                    att_tile(128, P, kwT[hs, ch, a * 128:(a + 1) * 128], 'kwT', qrhs,
                             G2[gi][:, b - a], 'G2_%d' % gi, [],
                             vw_aug[:, a, g, :], 'vw_aug', o, ores, a == a0, a == b, 65, after=(after_win if a == b else None))
                o, ores = OO.next()
                mrhs = mnegT[:, :, :]

                def after_slc(o=o, ores=ores, bi=bi, g=g, tok0=tok0, P=P):
                    branch_out(o, ores, gzt[bi][:, 1, :], 'gzt%d' % bi, False)
                    sch.op('act', lambda e: e.copy(out=ybf[:, :], in_=yacc[:, :]), reads=['yacc'], writes=['ybf'])

                    def late_slc():
                        transposes_to(128, ybf[:, :], 'ybf', 2, xT[:, 2 * g:2 * g + 2, tok0:tok0 + P], 'xT')
                    late.append(late_slc)
                flush_pv() if False else None
                for a in range(b + 1):
                    if a == 0:
                        flush_pv()
                        run_late()
                    att_tile(128, P, ksT[hs, ch, a * 128:(a + 1) * 128], 'ksT', qrhs,
                             G1[gi][:, b - a], 'G1_%d' % gi,
                             [(esel[:, a, :], mrhs, ['esel', 'mnegT'])],
                             vs_aug[:, a, g, :], 'vs_aug', o, ores, a == 0, a == b, 65, after=(after_slc if a == b else None))
        flush_all()
        cur_SS[0] = SS

    def branch_out(o, ores, gz2, gzres, first, P=128):
        o3 = AP(o, 0, [[512, P], [128, 4], [1, 64]])
        sch.op('dve', lambda e: e.tensor_scalar(out=rden[:P, :], in0=AP(o, 64, [[512, P], [128, 4]]), scalar1=1e-30,
                                                scalar2=None, op0=ALU.max), reads=[ores], writes=['rden'])
        sch.op('dve', lambda e: e.reciprocal(out=rden[:P, :], in_=rden[:P, :]), reads=['rden'], writes=['rden'])
        sch.op('dve', lambda e: e.tensor_tensor(out=osb[:P, :, 0:64], in0=o3, in1=AP(rden, 0, [[4, P], [1, 4], [0, 64]]),
                                                op=ALU.mult), reads=[ores, 'rden'], writes=['osb'])
        sch.op('dve', lambda e: e.tensor_tensor(out=osb[:P, :, 0:64], in0=osb[:P, :, 0:64],
                                                in1=gz2.rearrange("p (r d) -> p r d", d=64), op=ALU.mult),
               reads=['osb', gzres], writes=['osb'])
        y3 = yacc[:P, :].rearrange("p (r d) -> p r d", d=64)
        if first:
            sch.op('dve', lambda e: e.tensor_copy(out=y3, in_=osb[:P, :, 0:64]), reads=['osb'], writes=['yacc'])
        else:
            sch.op('dve', lambda e: e.tensor_tensor(out=y3, in0=y3, in1=osb[:P, :, 0:64], op=ALU.add),
                   reads=['osb', 'yacc'], writes=['yacc'])

    def load_w_full(dst, dres, W2, nk):
        wst = LT['wst']
        N = W2.shape[1]
        for c0 in range(0, N, 512):
            src = AP(W2, c0, [[N, 128], [128 * N, nk], [1, 512]])
            sch.dma('sp', wst[:, :nk, :], src, writes=['wst'])
            sch.op('pool', lambda e: e.tensor_copy(out=dst[:, :, c0:c0 + 512], in_=wst[:, :nk, :]), reads=['wst'], writes=[dres])

    def out_phase(layer, wout2, yT=None):
        yT = xT if yT is None else yT
        wo_bf, wg_bf, wp_bf, h1, h1b, h1T, sg = (LT[k] for k in ('wo_bf', 'wg_bf', 'wp_bf', 'h1', 'h1b', 'h1T', 'sg'))
        h2 = [LT['h2_0'], LT['h2_0']]
        hin = [LT['hin0'], LT['hin0']]
        pin, pinb, pT = LT['pin'], LT['pinb'], LT['pT']
        load_w_full(wo_bf, 'wo_bf', wout2, 8)
        load_w_full(wg_bf, 'wg_bf', ple_gate_w[layer], 8)
        load_w_full(wp_bf, 'wp_bf', ple_w[layer], 2)
        hs = h_src(layer)
        for t in range(NT):
            P, tok0 = tile_info(t)
            i = hin_i[0]
            hin_i[0] ^= 1
            hr = 'hin0'
            sch.dma('sp', hin[i][:P], hs(t), reads=['hd%d' % t], writes=[hr])
            psrc = p_p[layer, tok0:tok0 + P, :] if t < NTP else p_s[layer, :, :]
            sch.dma('sp', pin[:P], psrc, writes=['pin'])
            for c in range(2):
                g, gr = G.next()
                for kc in range(8):
                    sch.op('pe', lambda e: e.matmul(g[:P, :], lhsT=yT[:, kc, tok0:tok0 + P], rhs=wo_bf[:, kc, c * 512:(c + 1) * 512],
                                                    start=(kc == 0), stop=(kc == 7)), reads=['xT', 'qT', 'wo_bf'], writes=[gr])
                sch.op('dve', lambda e: e.tensor_tensor(out=h1[:P, c * 512:(c + 1) * 512], in0=g[:P, :], in1=hin[i][:P, c * 512:(c + 1) * 512],
                                                        op=ALU.add), reads=[gr, hr], writes=['h1'])
            sch.op('act', lambda e: e.copy(out=h1b[:P, :], in_=h1[:P, :]), reads=['h1'], writes=['h1b'])
            transposes_to(P, h1b[:P], 'h1b', 8, h1T[:, :, :P], 'h1T')
            sch.op('dve', lambda e: e.tensor_copy(out=pinb[:P, :], in_=pin[:P, :]), reads=['pin'], writes=['pinb'])
            transposes_to(P, pinb[:P], 'pinb', 2, pT[:, :, :P], 'pT')
            j = h2_i[0]
            h2_i[0] ^= 1
            h2r = 'h2_0'
            for c in range(2):
                g, gr = G.next()
                for kc in range(8):
                    sch.op('pe', lambda e: e.matmul(g[:P, :], lhsT=h1T[:, kc, :P], rhs=wg_bf[:, kc, c * 512:(c + 1) * 512],
                                                    start=(kc == 0), stop=(kc == 7)), reads=['h1T', 'wg_bf'], writes=[gr])
                sch.op('act', lambda e: e.activation(out=sg[:P, c * 512:(c + 1) * 512], in_=g[:P, :], func=AF.Sigmoid),
                       reads=[gr], writes=['sg'])
                g, gr = G.next()
                for kc in range(2):
                    sch.op('pe', lambda e: e.matmul(g[:P, :], lhsT=pT[:, kc, :P], rhs=wp_bf[:, kc, c * 512:(c + 1) * 512],
                                                    start=(kc == 0), stop=(kc == 1)), reads=['pT', 'wp_bf'], writes=[gr])
                sch.op('dve', lambda e: e.tensor_tensor(out=sg[:P, c * 512:(c + 1) * 512], in0=g[:P, :], in1=sg[:P, c * 512:(c + 1) * 512],
                                                        op=ALU.mult), reads=[gr, 'sg'], writes=['sg'])
                sch.op('dve', lambda e: e.tensor_tensor(out=h2[j][:P, c * 512:(c + 1) * 512], in0=sg[:P, c * 512:(c + 1) * 512],
                                                        in1=h1[:P, c * 512:(c + 1) * 512], op=ALU.add), reads=['sg', 'h1'], writes=[h2r])
            dst = y_p[tok0:tok0 + P, :] if t < NTP else y_s[:, :]
            sch.dma('pool', dst, h2[j][:P, :], reads=[h2r], writes=['hd%d' % t])


    def shared_kv_phase():
        sch.dma('sp', gq[:, 1, :], bc_row(b_k_norm), writes=['gq'])

        def consumer(ci, t, P, tok0, g, gr):
            ri = rowb_i[0]
            rowb_i[0] ^= 1
            rb = rowb[ri]
            rr = 'rowb%d' % ri
            src3 = g[:P, 0:256].rearrange("p (h d) -> p h d", d=64)
            gain3 = AP(gq, 64, [[256, P], [0, 4], [1, 64]])
            out3 = rb[:P, 0:256].rearrange("p (h d) -> p h d", d=64)
            headnorm(P, src3, 4, gain3, out3, rr, gr, extra_reads=['gq'])
            sch.op('act', lambda e: e.copy(out=rb[:P, 256:512], in_=g[:P, 256:512]), reads=[gr], writes=[rr])
            if t < NTP:
                sch.dma('pool', o_pb[tok0:tok0 + P, :], rb[:P, :], reads=[rr], writes=['o_pb'])
                sch.op('dve', lambda e: e.tensor_copy(out=qnb[:P, :], in_=rb[:P, :]), reads=[rr], writes=['qnb'])
                transposes_to(P, qnb[:P, 0:256], 'qnb', 2, ksT[:, :, tok0:tok0 + P], 'ksT')
                sch.op('pool', lambda e: e.tensor_copy(out=vs_aug[:, t, :, 0:64],
                                                        in_=qnb[:, 256:512].rearrange("p (g d) -> p g d", d=64)),
                       reads=['qnb'], writes=['vs_aug'])
            else:
                for b in range(4):
                    sch.dma('pool', o_sb[b, 2044:2048, :], rb[4 * b:4 * b + 4, :], reads=[rr], writes=['o_sb'])
        gemm(lambda kc, t: xT[:, kc, tile_info(t)[1]:tile_info(t)[1] + tile_info(t)[0]], 'xT', b_w_kv, [(0, 512)],
             list(range(NT)), consumer)

    def q_consume(P, g, gr, gain_off, dst3, dres):
        src3 = g[:P, :512].rearrange("p (h d) -> p h d", d=64)
        out4 = AP(qnb, 0, [[512, P], [64, 2], [128, 4], [1, 64]])
        sq3 = sq[:P, :512].rearrange("p (h d) -> p h d", d=64)
        sch.op('act', lambda e: e.activation(out=sq3, in_=src3, func=AF.Square), reads=[gr], writes=['sq'])
        sch.op('dve', lambda e: e.tensor_reduce(out=small[:P, 0:8], in_=sq3, axis=AX.X, op=ALU.add),
               reads=['sq'], writes=['small'])
        rstd_from_ss(P, 8, small[:P, 0:8], small[:P, 16:24], 1.0 / 64)
        t3 = tmpf[:P, :512].rearrange("p (h d) -> p h d", d=64)
        sch.op('dve', lambda e: e.tensor_tensor(out=t3, in0=src3, in1=AP(small, 16, [[64, P], [1, 8], [0, 64]]),
                                                op=ALU.mult), reads=[gr, 'small'], writes=['tmpf'])
        t4 = tmpf[:P, :512].rearrange("p (a r d) -> p a r d", a=2, r=4)
        g4 = AP(gbq, gain_off, [[192, P], [0, 2], [0, 4], [1, 64]])
        sch.op('dve', lambda e: e.tensor_tensor(out=out4, in0=t4, in1=g4, op=ALU.mult),
               reads=['tmpf', 'gbq'], writes=['qnb'])
        transposes_to(P, qnb[:P], 'qnb', 4, dst3, dres)

    def b_z_phase(j):
        W = b_w_in[j]

        def consumer(ci, t, P, tok0, g, gr):
            ri = rowb_i[0]
            rowb_i[0] ^= 1
            rb = rowb[ri]
            rr = 'rowb%d' % ri
            sch.op('act', lambda e: e.activation(out=rb[:P, :], in_=g[:P, :512], func=AF.Silu), reads=[gr], writes=[rr])
            sch.dma('pool', gz_d[tok0:tok0 + P, ci * 512:(ci + 1) * 512], rb[:P, :], reads=[rr], writes=['gz_d'])
        gemm(lambda kc, t: xT[:, kc, tile_info(t)[1]:tile_info(t)[1] + tile_info(t)[0]], 'xT', W,
             [(3072, 512), (3584, 512)], list(range(NT)), consumer)

    def b_q_phase(j, gp):
        W = b_w_in[j]
        q3T = LT['q3T']
        sch.dma('sp', gbq[:].rearrange("p a d -> p (a d)"), bc_row(b_q_norm[j].rearrange("a d -> (a d)")), writes=['gbq'])
        sch.op('dve', lambda e: e.tensor_scalar(out=gbq[:], in0=gbq[:], scalar1=SCALE, scalar2=None, op0=ALU.mult),
               reads=['gbq'], writes=['gbq'])

        def consumer(ci, t, P, tok0, g, gr):
            q_consume(P, g, gr, ci * 64, qv(q3T, 12, t)[:, 4 * ci:4 * ci + 4, :], 'qT')
        gemm(lambda kc, t: xT[:, kc, tile_info(t)[1]:tile_info(t)[1] + tile_info(t)[0]], 'xT', W,
             [(grp * 1024 + gp * 512, 512) for grp in range(3)], list(range(NT)), consumer)

    def b_attention_prompt(j, gp):
        cur_SS[0] = SS4
        q3T = LT['q3T']
        Gd = {1: LT['Gd1'], 4: LT['Gd4'], 16: LT['Gd16']}
        gzt = [LT['gzt0'], LT['gzt1']]
        Ld = {1: 384, 4: 768, 16: 2304}
        Wd = {1: 256, 4: 640, 16: 2048}
        for g in (2 * gp, 2 * gp + 1):
            half, ch = g % 2, g // 2
            hs = slice(half * 64, half * 64 + 64)
            for d in (1, 4, 16):
                for dl in range(Wd[d] // 128):
                    toeplitz_load(Gd[d][:, dl], 'Gd%d' % d, 'd%d' % d, 128 * dl, 1, 4 * g, Ld[d], 128)
            for b in range(NTP):
                P, tok0 = 128, b * 128
                bi = b % 2
                sch.dma('sp', gzt[bi][:, 0, :], gz_d[tok0:tok0 + P, g * 256:(g + 1) * 256], reads=['gz_d'], writes=['gzt%d' % bi])
                o, ores = OO.next()
                tiles = []
                for grp, d, na in ((0, 1, 2), (1, 4, 5), (2, 16, 99)):
                    for a in range(max(0, b - na + 1), b + 1):
                        tiles.append((grp, d, a))
                def after_b(o=o, ores=ores, bi=bi, g=g, tok0=tok0, P=P):
                    branch_out(o, ores, gzt[bi][:, 0, :], 'gzt%d' % bi, True)
                    sch.op('act', lambda e: e.copy(out=ybf[:, :], in_=yacc[:, :]), reads=['yacc'], writes=['ybf'])

                    def late_b():
                        transposes_to(128, ybf[:, :], 'ybf', 2, qT[:, 2 * g:2 * g + 2, tok0:tok0 + P], 'qT')
                    late.append(late_b)
                for ti, (grp, d, a) in enumerate(tiles):
                    if ti == min(3, len(tiles) - 1):
                        run_late()
                    qrhs = qv(q3T, 12, b)[hs, 4 * grp:4 * grp + 4, :]
                    att_tile(128, P, ksT[hs, ch, a * 128:(a + 1) * 128], 'ksT', qrhs,
                             Gd[d][:, b - a], 'Gd%d' % d, [],
                             vs_aug[:, a, g, :], 'vs_aug', o, ores, ti == 0, ti == len(tiles) - 1, 65,
                             after=(after_b if ti == len(tiles) - 1 else None))
        flush_all()
        cur_SS[0] = SS

    SC_PROJB = [('wst', [128, 4, 512], F32), ('wbf', [128, 2, 8, 512], BF16), ('sq_x1', [128, 512], F32), ('tmpf_x1', [128, 512], F32),
                ('qnb_x1', [128, 512], BF16), ('qnb_x2', [128, 512], BF16), ('small_x1', [128, 64], F32), ('small_x2', [128, 64], F32), ('tbring', [128, 1], F32)]
    SC_Q3 = [('q3T', [128, 12, NTOK], BF16)]
    SC_ATTB = [('Gd1', [128, 2, 4, 128], BF16), ('Gd4', [128, 5, 4, 128], BF16), ('Gd16', [128, 16, 4, 128], BF16),
               ('gzt0', [128, 1, 256], F32), ('gzt1', [128, 1, 256], F32),
               ('Pt0', [128, 512], BF16), ('Pt1', [128, 512], BF16), ('Pt2', [128, 512], BF16), ('Pt3', [128, 512], BF16)]


    GsT = pg.sb('GsT', [128, 58, 16, 4], BF16)
    msk_d = pg.dint('msk_d', [4, 2048], BF16)
    cache_cmp_flat = AP(cache_cmp, 0, [[512, 2 * NPHYS * 128], [1, 512]])
    cache_slc_flat = AP(cache_slc, 0, [[512, 2 * NPHYS * 128], [1, 512]])
    TI_SLC, TI_WIN, TI_CMP, TI_B = 0, 25, 30, 34

    def load_gs(ti, fdk, L, base, pstep, nk):
        src = AP(fd[fdk], base, [[pstep, nk], [L, 16], [1, 4]])
        sch.dma('sp', GsT[:nk, ti, :, :], src, reads=['fd_' + fdk], writes=['GsT'])

    def sample_tables():
        Ls = hc['oh_slc'].shape[1]
        for a in range(40, 64):
            load_gs(TI_SLC + a - 40, 'slc', Ls, 8192 - 128 * a, 1, 128)
        load_gs(TI_SLC + 24, 'slc', Ls, 124, 1, 4)
        for a in range(4):
            load_gs(TI_WIN + a, 'win', 768, 512 - 128 * a, 1, 128)
        load_gs(TI_WIN + 4, 'win', 768, 124, 1, 4)
        for ct in range(4):
            load_gs(TI_CMP + ct, 'slc', Ls, 6256 - 2048 * ct, 16, 128)
        ti = TI_B
        for d, L, a0 in ((1, 384, 15), (4, 768, 12), (16, 2304, 0)):
            for a in range(a0, 16):
                load_gs(ti, 'd%d' % d, L, 2048 - 128 * a, 1, 128)
                ti += 1
            load_gs(ti, 'd%d' % d, L, 124, 1, 4)
            ti += 1

    SC_PREP = [('rows0', [128, 512], F32), ('rows1', [128, 512], F32), ('kb0', [128, 256], BF16), ('kb1', [128, 256], BF16),
               ('kTt0', [128, 2, 128], BF16), ('kTt1', [128, 2, 128], BF16), ('kTt2', [128, 2, 128], BF16),
               ('vat0', [128, 4, 65], BF16), ('vat1', [128, 4, 65], BF16), ('vat2', [128, 4, 65], BF16),
               ('Pt0', [128, 512], BF16), ('Pt1', [128, 512], BF16), ('Pt2', [128, 512], BF16), ('Pt3', [128, 512], BF16),
               ('gzs', [4, 1024], F32), ('ys', [4, 4, 256], F32)]
    SC_SAMPA = SC_PREP + [('idx', [128, 256], I32), ('ptab_i', [128, 256], I32), ('pidx', [128, 1], F32),
                          ('w1s', [128, 32, 256], BF16), ('w2s2', [128, 2, 2, 64], BF16), ('b1s2', [128, 4], F32),
                          ('b2s2', [128, 2, 64], F32), ('wst4', [128, 4, 256], F32), ('chT', [128, 4, 2176], BF16),
                          ('rowsb', [128, 512], BF16), ('kccT_s', [128, 2, 512], BF16), ('vcc_s', [128, 4, 4, 65], BF16),
                          ('ovs', [128, 4, 128], BF16), ('selBs', [4, 136], F32), ('sc1s', [4, 136], F32), ('sc2s', [4, 136], F32),
                          ('scs', [4, 128], F32), ('mx8s', [4, 16], F32), ('mneg_s', [4, 4, 128], BF16), ('Xm', [1, 2048], BF16),
                          ('e2', [1, 64], BF16)]
    prep_i = [0, 0]

    def prep_tile(load_fn, nk):
        ri = prep_i[0]
        prep_i[0] ^= 1
        ki = prep_i[1]
        prep_i[1] = (ki + 1) % 3
        rows, kb, kTt, vat = LT['rows%d' % ri], LT['kb%d' % ri], LT['kTt%d' % ki], LT['vat%d' % ki]
        rr, kbr, ktr, var = 'rows%d' % ri, 'kb%d' % ri, 'kTt%d' % ki, 'vat%d' % ki
        load_fn(rows[:nk, :], rr)
        sch.op('dve', lambda e: e.tensor_copy(out=kb[:nk, :], in_=rows[:nk, 0:256]), reads=[rr], writes=[kbr])
        sch.op('act', lambda e: e.copy(out=vat[:nk, :, 0:64], in_=rows[:nk, 256:512].rearrange("p (g d) -> p g d", d=64)),
               reads=[rr], writes=[var])
        transposes_to(nk, kb[:nk, :], kbr, 2, kTt[:, :, :nk], ktr)
        return kTt, ktr, vat, var

    def dram_loader(src_ap, res=()):
        def f(rows_ap, rr):
            sch.dma('sp', rows_ap, src_ap, reads=list(res), writes=[rr])
        return f

    def page_loader(flat, col):
        def f(rows_ap, rr):
            sch.idma(rows_ap, flat, LT['idx'][:, col:col + 1], reads=['idx'], writes=[rr])
        return f

    def branch_out_s(o, ores, g, first):
        P = 4
        gzs, ys = LT['gzs'], LT['ys']
        o3 = AP(o, 0, [[512, P], [128, 4], [1, 64]])
        sch.op('dve', lambda e: e.tensor_scalar(out=rden[:P, :], in0=AP(o, 64, [[512, P], [128, 4]]), scalar1=1e-30,
                                                scalar2=None, op0=ALU.max), reads=[ores], writes=['rden'])
        sch.op('dve', lambda e: e.reciprocal(out=rden[:P, :], in_=rden[:P, :]), reads=['rden'], writes=['rden'])
        sch.op('dve', lambda e: e.tensor_tensor(out=osb[:P, :, 0:64], in0=o3, in1=AP(rden, 0, [[4, P], [1, 4], [0, 64]]),
                                                op=ALU.mult), reads=[ores, 'rden'], writes=['osb'])
        gz3 = gzs[:P, g * 256:(g + 1) * 256].rearrange("p (r d) -> p r d", d=64)
        sch.op('dve', lambda e: e.tensor_tensor(out=osb[:P, :, 0:64], in0=osb[:P, :, 0:64], in1=gz3, op=ALU.mult),
               reads=['osb', 'gzs'], writes=['osb'])
        y3 = ys[:P, g, :].rearrange("p (r d) -> p r d", d=64)
        if first:
            sch.op('dve', lambda e: e.tensor_copy(out=y3, in_=osb[:P, :, 0:64]), reads=['osb'], writes=['ys'])
        else:
            sch.op('dve', lambda e: e.tensor_tensor(out=y3, in0=y3, in1=osb[:P, :, 0:64], op=ALU.add),
                   reads=['osb', 'ys'], writes=['ys'])

    def samp_finish(bs, ydst, yres, groups=(0, 1, 2, 3)):
        ys = LT['ys']
        qs = S + 4 * bs
        for g in groups:
            sch.op('act', lambda e: e.copy(out=ybf[:4, :], in_=ys[:4, g, :]), reads=['ys'], writes=['ybf'])
            transposes_to(4, ybf[:4, :], 'ybf', 2, ydst[:, 2 * g:2 * g + 2, qs:qs + 4], yres)

    def load_gz(bs, br):
        sch.dma('sp', LT['gzs'][:4, :], gz_d[S + 4 * bs:S + 4 * bs + 4, br * 1024:(br + 1) * 1024], reads=['gz_d'], writes=['gzs'])

    def samp_multi(bs, tiles, qsrc, qblk_of, groups, masked):
        acc = {0: G.items[0], 1: G.items[1], 2: OO.items[0], 3: OO.items[1]}
        qs = S + 4 * bs
        for ti, (load_fn, nk, tab, ea, grp) in enumerate(tiles):
            kTt, ktr, vat, var = prep_tile(load_fn, nk)
            for g in groups:
                half, ch = g % 2, g // 2
                hs = slice(half * 64, half * 64 + 64)
                qb = qblk_of(g, grp)
                qrhs = qsrc[hs, qb:qb + 4, 4 * bs:4 * bs + 4]
                extra = []
                if masked and ea is not None:
                    extra = [(hf_, LT['e2'][0:1, 0:64], AP(LT['Xm'], g * 128 + 2 * ea + hf_, [[2048, 1], [0, 4], [512, 4]]), ['e2', 'Xm'])
                             for hf_ in range(2)]
                o, ores = acc[g]
                att_tile(nk, 4, kTt[hs, ch, :nk], ktr, qrhs, GsT[:nk, tab, 4 * g:4 * g + 4, :], 'GsT', extra,
                         vat[:nk, g, :], var, o, ores, ti == 0, ti == len(tiles) - 1, 65)
        flush_pv()
        return acc

    def init_vat():
        for i_ in range(3):
            sch.op('dve', lambda e: e.memset(LT['vat%d' % i_][:, :, 64:65], 1.0), writes=['vat%d' % i_])

    def a_sample(li):
        init_vat()
        idx, ptab_i, pidx = LT['idx'], LT['ptab_i'], LT['pidx']
        w1s, w2s2, b1s2, b2s2, wst4 = LT['w1s'], LT['w2s2'], LT['b1s2'], LT['b2s2'], LT['wst4']
        chT, rowsb, kccT_s, vcc_s, ovs = LT['chT'], LT['rowsb'], LT['kccT_s'], LT['vcc_s'], LT['ovs']
        selBs, sc1s, sc2s, scs, mx8s, mneg_s, Xm, e2 = (LT[k] for k in ('selBs', 'sc1s', 'sc2s', 'scs', 'mx8s', 'mneg_s', 'Xm', 'e2'))
        sch.dma('sp', ptab_i[:, :], AP(ptab, 0, [[0, 128], [1, 256]]), writes=['ptab_i'])
        sch.dma('sp', pidx[:, :], cdram['pidx'][:, :], writes=['pidx'])
        sch.dma('sp', ovs[:], cdram['ov_s'][:], writes=['ovs'])
        sch.dma('sp', selBs[:], cdram['selB_s'][:], writes=['selBs'])
        sch.op('dve', lambda e: e.memset(e2[:], 1.0), writes=['e2'])
        sch.op('dve', lambda e: e.tensor_scalar(out=idx[:, :], in0=ptab_i[:, :], scalar1=128.0, scalar2=pidx[:, 0:1],
                                                op0=ALU.mult, op1=ALU.add), reads=['ptab_i', 'pidx'], writes=['idx'])
        if li == 1:
            sch.op('dve', lambda e: e.tensor_scalar(out=idx[:, :], in0=idx[:, :], scalar1=float(NPHYS * 128), scalar2=None,
                                                    op0=ALU.add), reads=['idx'], writes=['idx'])
        for kv in range(2):
            sch.dma('sp', wst4[:, 0, 0:128].rearrange("p (a d) -> p a d", d=64),
                    AP(a_phi_w2[li, kv], 0, [[64, 128], [128 * 64, 2], [1, 64]]), writes=['wst4'])
            sch.op('pool', lambda e: e.tensor_copy(out=w2s2[:, kv, :, :], in_=wst4[:, 0, 0:128].rearrange("p (a d) -> p a d", d=64)),
                   reads=['wst4'], writes=['w2s2'])
            for hh_ in range(2):
                sch.dma('sp', b1s2[:, 2 * kv + hh_:2 * kv + hh_ + 1], AP(a_phi_b1[li, kv], hh_ * 128, [[1, 128], [1, 1]]), writes=['b1s2'])
            sch.dma('sp', b2s2[:, kv, :], bc_row(a_phi_b2[li, kv]), writes=['b2s2'])

        ksv = ksT[:, :, :].rearrange("p a (b h) -> p (a b) h", h=256)
        kwv = kwT[:, :, :].rearrange("p a (b h) -> p (a b) h", h=256)

        def w1_of(kv, ab):
            if kv == 0:
                return w1s[:, ab * 16:(ab + 1) * 16, :], 'w1s'
            return (ksv, 'ksT') if ab == 0 else (kwv, 'kwT')

        for kv in range(2):
            for ab in range(2):
                wv, wres = w1_of(kv, ab)
                for sq4 in range(4):
                    for half in range(2):
                        src = AP(a_phi_w1[li, kv, ab], sq4 * 4 * 64 * 256, [[256, 64], [64 * 256, 4], [1, 256]])
                        sch.dma('sp', wst4[half * 64:half * 64 + 64, :, :], src, writes=['wst4'])
                    sch.op('pool', lambda e: e.tensor_copy(out=wv[:, sq4 * 4:sq4 * 4 + 4, :], in_=wst4[:, :, :]),
                           reads=['wst4'], writes=[wres])

        for bs in range(4):
            qs = S + 4 * bs
            sch.op('dve', lambda e: e.memset(kccT_s[:], 0.0), writes=['kccT_s'])
            sch.op('dve', lambda e: e.memset(vcc_s[:], 0.0), writes=['vcc_s'])
            for ct in range(4):
                npg = 17 if ct < 3 else 16
                ncol = 128 if ct < 3 else 127
                for pi in range(npg):
                    ri = prep_i[0]
                    prep_i[0] ^= 1
                    rows, rr = LT['rows%d' % ri], 'rows%d' % ri
                    sch.idma(rows[:, :], cache_cmp_flat, idx[:, bs * 64 + 16 * ct + pi:bs * 64 + 16 * ct + pi + 1],
                             reads=['idx'], writes=[rr])
                    if pi % 2 == 0:
                        sch.op('dve', lambda e: e.tensor_copy(out=rowsb[:, :], in_=rows[:, :]), reads=[rr], writes=['rowsb'])
                    else:
                        sch.op('act', lambda e: e.copy(out=rowsb[:, :], in_=rows[:, :]), reads=[rr], writes=['rowsb'])
                    transposes_to(128, rowsb[:, :], 'rowsb', 4, AP(chT, pi * 8, [[4 * 2176, 128], [2176, 4], [1, 8], [136, 16]]), 'chT',
                                  src_dims=[[128, 4], [16, 8], [1, 16]])
                for kv in range(2):
                    for g in range(4):
                        half, ch = g % 2, g // 2
                        hs = slice(half * 64, half * 64 + 64)
                        blk = kv * 2 + ch
                        for hh in range(2):
                            ps, psr = SS.next()
                            n = 0
                            for ab in range(2):
                                for s_ in range(16):
                                    rhs = AP(chT, half * 64 * (4 * 2176) + blk * 2176 + s_ * 136 + ab, [[4 * 2176, 64], [1, ncol]])
                                    wv_, wres_ = w1_of(kv, ab)
                                    sch.op('pe', lambda e: e.matmul(ps[:, 0:ncol], lhsT=wv_[hs, s_, hh * 128:(hh + 1) * 128],
                                                                    rhs=rhs, start=(n == 0), stop=(n == 31)),
                                           reads=[wres_, 'chT'], writes=[psr])
                                    n += 1
                            sch.op('act', lambda e: e.activation(out=hidT[:, hh, 0:ncol], in_=ps[:, 0:ncol], func=AF.Silu,
                                                                 bias=b1s2[:, 2 * kv + hh:2 * kv + hh + 1]), reads=[psr, 'b1s2'], writes=['hidT'])
                        po, por = OO.next()
                        for hh in range(2):
                            sch.op('pe', lambda e: e.matmul(po[0:ncol, 0:64], lhsT=hidT[:, hh, 0:ncol], rhs=w2s2[:, kv, hh, :],
                                                            start=(hh == 0), stop=(hh == 1)), reads=['hidT', 'w2s2'], writes=[por])
                        sch.op('dve', lambda e: e.tensor_tensor(out=tmpf[0:ncol, 0:64], in0=po[0:ncol, 0:64], in1=b2s2[0:ncol, kv, :], op=ALU.add),
                               reads=[por, 'b2s2'], writes=['tmpf'])
                        if kv == 0:
                            src3 = tmpf[0:ncol, 0:64].rearrange("p (h d) -> p h d", d=64)
                            sq3 = sq[:ncol, :64].rearrange("p (h d) -> p h d", d=64)
                            sch.op('act', lambda e: e.activation(out=sq3, in_=src3, func=AF.Square), reads=['tmpf'], writes=['sq'])
                            sch.op('dve', lambda e: e.tensor_reduce(out=small[:ncol, 0:1], in_=sq3, axis=AX.X, op=ALU.add),
                                   reads=['sq'], writes=['small'])
                            rstd_from_ss(ncol, 1, small[:ncol, 0:1], small[:ncol, 16:17], 1.0 / 64)
                            if half == 0:
                                sch.op('dve', lambda e: e.memset(qnb[:, 0:128], 0.0), writes=['qnb'])
                            sch.op('dve', lambda e: e.scalar_tensor_tensor(out=qnb[0:ncol, half * 64:half * 64 + 64], in0=tmpf[0:ncol, 0:64],
                                                                           scalar=small[0:ncol, 16:17], in1=AP(gq, 64, [[256, ncol], [1, 64]]),
                                                                           op0=ALU.mult, op1=ALU.mult),
                                   reads=['tmpf', 'small', 'gq'], writes=['qnb'])
                            if half == 1:
                                transposes_to(128, qnb[:, 0:128], 'qnb', 1, kccT_s[:, ch:ch + 1, ct * 128:(ct + 1) * 128], 'kccT_s')
                        else:
                            sch.op('act', lambda e: e.copy(out=vcc_s[0:ncol, ct, g, 0:64], in_=tmpf[0:ncol, 0:64]), reads=['tmpf'], writes=['vcc_s'])
                            sch.op('dve', lambda e: e.memset(vcc_s[0:ncol, ct, g, 64:65], 1.0), writes=['vcc_s'])
            load_gz(bs, 0)
            for g in range(4):
                half, ch = g % 2, g // 2
                hs = slice(half * 64, half * 64 + 64)
                qrhs = qv(qT, 8, NTP)[hs, 4 * ch:4 * ch + 4, 4 * bs:4 * bs + 4]
                o, ores = OO.next()
                o2, o2res = TF
                for ct in range(4):
                    att_tile(128, 4, kccT_s[hs, ch, ct * 128:(ct + 1) * 128], 'kccT_s', qrhs, GsT[:, TI_CMP + ct, 4 * g:4 * g + 4, :], 'GsT',
                             [], vcc_s[:, ct, g, :], 'vcc_s', o, ores, ct == 0, ct == 3, 65,
                             score=(ovs[:, ct, :], 'ovs', o2, o2res))
                flush_pv()
                branch_out_s(o, ores, g, True)
                sch.op('dve', lambda e: e.tensor_scalar(out=scs[:, :], in0=o2[:4, 0:128], scalar1=rden[:4, 0:1], scalar2=None, op0=ALU.mult),
                       reads=[o2res, 'rden'], writes=['scs'])
                for r in range(1, 4):
                    sch.op('dve', lambda e: e.scalar_tensor_tensor(out=scs[:, :], in0=o2[:4, r * 128:(r + 1) * 128], scalar=rden[:4, r:r + 1],
                                                                   in1=scs[:, :], op0=ALU.mult, op1=ALU.add),
                           reads=[o2res, 'rden', 'scs'], writes=['scs'])
                sch.op('dve', lambda e: e.tensor_copy(out=sc1s[:, :], in_=selBs[:, :]), reads=['selBs'], writes=['sc1s'])
                sch.op('dve', lambda e: e.tensor_tensor(out=sc1s[:, 0:128], in0=scs[:, :], in1=selBs[:, 0:128], op=ALU.add),
                       reads=['scs', 'selBs'], writes=['sc1s'])
                sch.op('dve', lambda e: e.max(out=mx8s[:, 0:8], in_=sc1s[:, :]), reads=['sc1s'], writes=['mx8s'])
                sch.op('dve', lambda e: e.match_replace(out=sc2s[:, :], in_to_replace=mx8s[:, 0:8], in_values=sc1s[:, :], imm_value=-2.0),
                       reads=['sc1s', 'mx8s'], writes=['sc2s'])
                sch.op('dve', lambda e: e.max(out=mx8s[:, 8:16], in_=sc2s[:, :]), reads=['sc2s'], writes=['mx8s'])
                sch.op('dve', lambda e: e.tensor_scalar(out=sc2s[:, :], in0=sc1s[:, :], scalar1=mx8s[:, 15:16], scalar2=None,
                                                        op0=ALU.is_ge), reads=['sc1s', 'mx8s'], writes=['sc2s'])
                sch.op('dve', lambda e: e.tensor_scalar(out=mneg_s[:, g, :], in0=sc2s[:, 0:128], scalar1=-1.0, scalar2=-MASKV,
                                                        op0=ALU.add, op1=ALU.mult), reads=['sc2s'], writes=['mneg_s'])
            sch.dma('sp', AP(msk_d, bs * 2048, [[512, 4], [128, 4], [1, 128]]), mneg_s[:, :, :], reads=['mneg_s'], writes=['msk_d'])
            sch.dma('sp', Xm[0:1, :], AP(msk_d, bs * 2048, [[0, 1], [1, 2048]]), reads=['msk_d'], writes=['Xm'])
            load_gz(bs, 1)
            tiles = []
            for a in range(64):
                tab = TI_SLC + max(a, 40) - 40
                tiles.append((page_loader(cache_slc_flat, bs * 64 + a), 128, tab, a, 0))
            tiles.append((dram_loader(o_ss[li, 4 * bs:4 * bs + 4, :], ('o_ss',)), 4, TI_SLC + 24, None, 0))
            acc = samp_multi(bs, tiles, qv(qT, 8, NTP), lambda g, grp: 4 * (g // 2), (0, 1, 2, 3), True)
            for g in range(4):
                branch_out_s(acc[g][0], acc[g][1], g, False)
            load_gz(bs, 2)
            tiles = []
            for a in range(4):
                tiles.append((dram_loader(st_a[li, bs, 128 * a:128 * a + 128, :]), 128, TI_WIN + a, None, 0))
            tiles.append((dram_loader(o_sw[li, bs, 508:512, :], ('o_sw',)), 4, TI_WIN + 4, None, 0))
            acc = samp_multi(bs, tiles, qv(qT, 8, NTP), lambda g, grp: 4 * (g // 2), (0, 1, 2, 3), False)
            for g in range(4):
                branch_out_s(acc[g][0], acc[g][1], g, False)
            samp_finish(bs, xT, 'xT')

    def b_sample(j, gp):
        init_vat()
        q3T = LT['q3T']
        groups = (2 * gp, 2 * gp + 1)
        for bs in range(4):
            sch.dma('sp', LT['gzs'][:4, :], gz_d[S + 4 * bs:S + 4 * bs + 4, 0:1024], reads=['gz_d'], writes=['gzs'])
            tiles = []
            ti = TI_B
            for grp, (d, a0) in enumerate(((1, 15), (4, 12), (16, 0))):
                for a in range(a0, 16):
                    tiles.append((dram_loader(st_b[bs, 128 * a:128 * a + 128, :]), 128, ti, None, grp))
                    ti += 1
                tiles.append((dram_loader(o_sb[bs, 2044:2048, :], ('o_sb',)), 4, ti, None, grp))
                ti += 1
            acc = samp_multi(bs, tiles, qv(q3T, 12, NTP), lambda g, grp: 4 * grp, groups, False)
            for g in groups:
                branch_out_s(acc[g][0], acc[g][1], g, True)
            samp_finish(bs, qT, 'qT', groups)

    def state_copies():
        for li in range(2):
            for b in range(4):
                sch.dma('pool', o_sw[li, b, 0:508, :], st_a[li, b, 4:512, :], writes=['o_sw'])
        for b in range(4):
            sch.dma('pool', o_sb[b, 0:2044, :], st_b[b, 4:2048, :], writes=['o_sb'])

    sch.op('dve', lambda e: e.memset(vs_aug[:, :, :, 64:65], 1.0), writes=['vs_aug'])
    sch.op('dve', lambda e: e.memset(vw_aug[:, :, :, 64:65], 1.0), writes=['vw_aug'])

    state_copies()
    sample_tables()
    for layer in range(4):
        if layer >= STAGE:
            break
        with Scope(SC_NORM, 'norm'):
            phase_norm(h_src(layer), norm_g[layer])
        if layer < 2:
            with Scope(SC_PROJ, 'projA'):
                a_project(layer)
                compress_prompt(layer)
            with Scope(SC_ATT, 'attA'):
                a_attention_prompt(layer)
            with Scope(SC_SAMPA, 'sampA'):
                a_sample(layer)
            with Scope(SC_OUT, 'out'):
                out_phase(layer, a_w_out[layer])
            if layer == 1 and STAGE > 2:
                with Scope(SC_NORM, 'norm'):
                    phase_norm(h_src(2), kv_norm_g)
                with Scope(SC_PROJB, 'projB'):
                    shared_kv_phase()
        else:
            j = layer - 2
            with Scope(SC_PROJB, 'projB'):
                b_z_phase(j)
            with Scope(SC_Q3):
                for gp in range(2):
                    with Scope(SC_PROJB, 'projB'):
                        b_q_phase(j, gp)
                    if gp == 1:
                        pass
                    with Scope(SC_ATTB, 'attB'):
                        b_attention_prompt(j, gp)
                    with Scope(SC_PREP, 'sampB'):
                        b_sample(j, gp)
            with Scope(SC_OUT, 'out'):
                out_phase(layer, b_w_out[j], qT)

    sch.barrier()
    return pg, hc


_CACHE = {}


def kernel(**inputs):
    if 'prog' not in _CACHE:
        _CACHE['prog'] = build()
    pg, hc = _CACHE['prog']
    f = lambda a: np.ascontiguousarray(np.asarray(a))
    x_prompt = f(inputs['x_prompt'])
    x_sample = f(inputs['x_sample'])
    cache_cmp = f(inputs['cache_a_cmp']).reshape(2, NPHYS * 128, 512)
    cache_slc = f(inputs['cache_a_slc']).reshape(2, NPHYS * 128, 512)
    st_a = f(inputs['state_a_win'])
    st_b = f(inputs['state_b_win'])
    ptab = f(inputs['page_table']).astype(np.int32)
    p_prompt = f(inputs['p_prompt'])
    p_sample = f(inputs['p_sample'])
    shared = {
        'cache_cmp': cache_cmp, 'cache_slc': cache_slc,
        'rel_bias': f(inputs['rel_bias']), 'norm_g': f(inputs['norm_g']), 'a_w_in': f(inputs['a_w_in']),
        'a_gate_b': f(inputs['a_gate_b']).reshape(2, 48), 'a_qk_norm': f(inputs['a_qk_norm']),
        'a_phi_w1': f(inputs['a_phi_w1']), 'a_phi_b1': f(inputs['a_phi_b1']), 'a_phi_w2': f(inputs['a_phi_w2']),
        'a_phi_b2': f(inputs['a_phi_b2']), 'a_w_out': f(inputs['a_w_out']), 'kv_norm_g': f(inputs['kv_norm_g']),
        'b_w_kv': f(inputs['b_w_kv']), 'b_k_norm': f(inputs['b_k_norm']), 'b_w_in': f(inputs['b_w_in']),
        'b_q_norm': f(inputs['b_q_norm']), 'b_w_out': f(inputs['b_w_out']), 'ple_w': f(inputs['ple_w']),
        'ple_gate_w': f(inputs['ple_gate_w']),
    }
    for k, v in hc.items():
        shared['c_' + k] = v
    in_maps = []
    for c in range(8):
        m = dict(shared)
        m['x_p'] = x_prompt[c]
        m['x_s'] = x_sample[4 * c:4 * c + 4].reshape(16, D)
        m['p_p'] = p_prompt[:, c]
        m['p_s'] = p_sample[:, 4 * c:4 * c + 4].reshape(4, 16, 256)
        m['st_a'] = st_a[:, 4 * c:4 * c + 4].reshape(2, 4, 512, 512)
        m['st_b'] = st_b[4 * c:4 * c + 4].reshape(4, 2048, 512)
        m['ptab'] = ptab[4 * c:4 * c + 4]
        in_maps.append({k: m[k] for k in pg.din_names})
    res = run_bass_kernel_spmd(pg.nc, in_maps, core_ids=list(range(8)))
    R = res.results
    cat = lambda k, ax=0: np.stack([np.asarray(r[k]) for r in R], axis=ax)
    y_prompt = cat('y_p').reshape(8, S, D)
    y_sample = cat('y_s').reshape(32, 4, D)
    pr_c = cat('o_pc', 1).reshape(2, 8, S, 2, 4, 64)
    pr_s = cat('o_ps', 1).reshape(2, 8, S, 2, 4, 64)
    pr_w = cat('o_pw', 1).reshape(2, 8, 512, 2, 4, 64)
    pr_b = cat('o_pb').reshape(8, S, 2, 4, 64)
    sm_c = cat('o_sc', 1).reshape(2, 32, 4, 2, 4, 64)
    sm_s = cat('o_ss', 1).reshape(2, 32, 4, 2, 4, 64)
    sm_w = cat('o_sw', 1).reshape(2, 32, 512, 2, 4, 64)
    sm_b = cat('o_sb').reshape(32, 2048, 2, 4, 64)
    return (y_prompt, y_sample, pr_c, pr_s, pr_w, pr_b, sm_c, sm_s, sm_w, sm_b)
```

```python
import math
import numpy as np
import ml_dtypes
import concourse.bass as bass
import concourse.mybir as mybir
from concourse.bass_utils import run_bass_kernel_spmd

F32 = mybir.dt.float32
BF16 = mybir.dt.bfloat16
I32 = mybir.dt.int32
AF = mybir.ActivationFunctionType
ALU = mybir.AluOpType
AX = mybir.AxisListType

D = 1024
S = 2048
NTP = 16
NT = 17
NSAMP = 16
NTOK = S + NSAMP
EPS = 1e-6
SCALE = 0.125
MASKV = -30000.0
NDS = 40
PAST = 8192
NPHYS = 2560

STAGE = 99


def tile_info(t):
    if t < NTP:
        return 128, t * 128
    return NSAMP, S


class Sch:
    def __init__(self, nc):
        self.nc = nc
        self.E = {'pe': nc.tensor, 'act': nc.scalar, 'dve': nc.vector, 'pool': nc.gpsimd, 'sp': nc.sync}
        self.sems = {}
        self.ccnt = {}
        for e in ('pe', 'act', 'dve', 'pool'):
            self.sems['c_' + e] = nc.alloc_semaphore('c_' + e)
            self.ccnt[e] = 0
        self.dcnt = [0] * NDS
        for i in range(NDS):
            self.sems['d%d' % i] = nc.alloc_semaphore('d%d' % i)
        self.di = 0
        self.lastw = {}
        self.readers = {}
        self.waited = {e: {} for e in self.E}
        self.n_inst = 0
        self.alias = {}

    def _need(self, eng, reads, writes):
        toks = []
        for r in reads:
            t = self.lastw.get(r)
            if t is not None and not (t[2] == 'pe' and eng == 'pe'):
                toks.append(t)
        for w in writes:
            t = self.lastw.get(w)
            if t is not None and t[2] != eng:
                toks.append(t)
            rd = self.readers.get(w)
            if rd:
                for sem, (val, src) in rd.items():
                    if src != eng:
                        toks.append((sem, val, src))
        return toks

    def _wait(self, eng, toks):
        wd = self.waited[eng]
        best = {}
        for sem, val, src in toks:
            if wd.get(sem, 0) >= val:
                continue
            if best.get(sem, 0) < val:
                best[sem] = val
        for sem, val in best.items():
            self.E[eng].wait_ge(self.sems[sem], val)
            wd[sem] = val
            self.n_inst += 1

    def _record(self, tok, reads, writes):
        for r in reads:
            d = self.readers.setdefault(r, {})
            d[tok[0]] = (tok[1], tok[2])
        for w in writes:
            self.lastw[w] = tok
            self.readers[w] = {}

    def op(self, eng, fn, reads=(), writes=()):
        reads = [self.alias.get(r, r) for r in reads]
        writes = [self.alias.get(w, w) for w in writes]
        self._wait(eng, self._need(eng, reads, writes))
        inst = fn(self.E[eng])
        self.ccnt[eng] += 1
        inst.then_inc(self.sems['c_' + eng], 1)
        self._record(('c_' + eng, self.ccnt[eng], eng), reads, writes)
        self.n_inst += 1

    def dma(self, q, out, in_, reads=(), writes=(), **kw):
        reads = [self.alias.get(r, r) for r in reads]
        writes = [self.alias.get(w, w) for w in writes]
        i = self.di
        self.di = (self.di + 1) % NDS
        name = 'd%d' % i
        toks = self._need(q, reads, writes)
        if self.dcnt[i] > 0:
            toks.append((name, self.dcnt[i], 'dma'))
        self._wait(q, toks)
        inst = self.E[q].dma_start(out=out, in_=in_, **kw)
        self.dcnt[i] += 16
        inst.then_inc(self.sems[name], 16)
        self._record((name, self.dcnt[i], 'dma'), reads, writes)
        self.n_inst += 1

    def idma(self, out, in_, idx_ap, reads=(), writes=()):
        q = 'pool'
        i = self.di
        self.di = (self.di + 1) % NDS
        name = 'd%d' % i
        toks = self._need(q, reads, writes)
        if self.dcnt[i] > 0:
            toks.append((name, self.dcnt[i], 'dma'))
        self._wait(q, toks)
        inst = self.E[q].indirect_dma_start(out=out, out_offset=None, in_=in_,
                                            in_offset=bass.IndirectOffsetOnAxis(ap=idx_ap, axis=0))
        self.dcnt[i] += 16
        inst.then_inc(self.sems[name], 16)
        self._record((name, self.dcnt[i], 'dma'), reads, writes)
        self.n_inst += 1

    def raw_tok_wait(self, eng, res_list):
        toks = []
        for r in res_list:
            t = self.lastw.get(r)
            if t is not None:
                toks.append(t)
        self._wait(eng, toks)

    def barrier(self):
        toks = []
        for e in ('pe', 'act', 'dve', 'pool'):
            if self.ccnt[e] > 0:
                toks.append(('c_' + e, self.ccnt[e], 'x'))
        for i in range(NDS):
            if self.dcnt[i] > 0:
                toks.append(('d%d' % i, self.dcnt[i], 'dma'))
        for e in self.E:
            self._wait(e, toks)


class Ring:
    def __init__(self, items):
        self.items = items
        self.i = 0

    def next(self):
        it = self.items[self.i]
        self.i = (self.i + 1) % len(self.items)
        return it


def _bucket_table(nmax):
    import jax
    import jax.numpy as jnp
    cpu = jax.devices('cpu')[0]
    with jax.default_device(cpu):
        n = jnp.arange(nmax, dtype=jnp.int32)
        exact = 16
        nf = jnp.maximum(n, 1).astype(jnp.float32)
        big = exact + (jnp.log(nf / exact) / math.log(4096 / exact) * (32 - exact)).astype(jnp.int32)
        b = jnp.where(n < exact, n, jnp.minimum(big, 31))
        return np.asarray(b).astype(np.int64)


def host_constants():
    c = {}
    bk = _bucket_table(8448)

    def onehot(ns, valid):
        L = len(ns)
        oh = np.zeros((33, L), np.float32)
        nn = np.clip(ns, 0, len(bk) - 1)
        idx = np.where(valid, bk[nn], 32)
        oh[idx, np.arange(L)] = 1.0
        return oh

    n = np.arange(8448) - 127
    c['oh_slc'] = onehot(n, n >= 0)
    n = np.arange(768) - 127
    c['oh_win'] = onehot(n, (n >= 0) & (n <= 512))
    n = np.arange(4096) - 2063
    c['oh_cmp'] = onehot(n, n >= 0)
    for d, win, L in ((1, 128, 384), (4, 512, 768), (16, 2048, 2304)):
        n = np.arange(L) - 127
        c['oh_d%d' % d] = onehot(n, (n >= 0) & (n <= win) & (n % d == 0))
    c['identb'] = np.eye(128, dtype=np.float32).astype(ml_dtypes.bfloat16)
    c['identf'] = np.eye(128, dtype=np.float32)
    c['antib'] = np.ascontiguousarray(np.eye(128, dtype=np.float32)[::-1]).astype(ml_dtypes.bfloat16)
    t = np.arange(S)
    cur = t // 64
    j = np.arange(32)[None, :]
    valid = (j <= cur[:, None])
    forced = (j == 0) | (j == cur[:, None]) | (j == cur[:, None] - 1)
    A = valid.astype(np.float32)
    Bc = np.where(valid, np.where(forced, 1000.0, 0.0), -1.0).astype(np.float32)
    c['selA'] = np.ascontiguousarray(A.reshape(16, 128, 32).transpose(1, 0, 2))
    c['selB'] = np.ascontiguousarray(Bc.reshape(16, 128, 32).transpose(1, 0, 2))
    cs = np.arange(128) * 16
    ss = np.arange(32) * 64
    ov = np.clip(np.minimum(cs[:, None] + 32, ss[None, :] + 64) - np.maximum(cs[:, None], ss[None, :]), 0, None) / 32.0
    ov[127] = 0
    c['ov_p'] = ov.astype(np.float32).astype(ml_dtypes.bfloat16)
    es = np.zeros((32, 16, 128), np.float32)
    for a in range(16):
        es[2 * a, a, :64] = 1
        es[2 * a + 1, a, 64:] = 1
    c['esel'] = es.astype(ml_dtypes.bfloat16)
    cs = np.arange(512) * 16
    ss = np.arange(128) * 64
    ovs = np.clip(np.minimum(cs[:, None] + 32, ss[None, :] + 64) - np.maximum(cs[:, None], ss[None, :]), 0, None) / 32.0
    ovs[511] = 0
    c['ov_s'] = np.ascontiguousarray(ovs.reshape(4, 128, 128).transpose(1, 0, 2)).astype(np.float32).astype(ml_dtypes.bfloat16)
    sb_ = np.zeros((4, 136), np.float32)
    sb_[:, [0, 127, 128]] = 1000.0
    sb_[:, 129:] = -1.0
    c['selB_s'] = sb_
    e2 = np.zeros((2, 128), np.float32)
    e2[0, :64] = 1
    e2[1, 64:] = 1
    c['e2'] = e2.astype(ml_dtypes.bfloat16)
    c['pidx'] = np.arange(128, dtype=np.float32).reshape(128, 1)
    return c


class Prog:
    def __init__(self):
        self.nc = bass.Bass("TRN2", target_bir_lowering=False)
        self.sch = Sch(self.nc)
        self.din_names = []

    def din(self, name, shape, dt=F32):
        self.din_names.append(name)
        return self.nc.dram_tensor(name, list(shape), dt, kind="ExternalInput").ap()

    def dout(self, name, shape, dt=F32):
        return self.nc.dram_tensor(name, list(shape), dt, kind="ExternalOutput").ap()

    def dint(self, name, shape, dt=F32):
        return self.nc.dram_tensor(name, list(shape), dt, kind="Internal").ap()

    def sb(self, name, shape, dt=F32):
        return self.nc.alloc_sbuf_tensor(name, list(shape), dt).ap()


def AP(ap, off, dims):
    return bass.AP(ap.tensor, ap.offset + off, [list(d) for d in dims])


def build():
    pg = Prog()
    nc = pg.nc
    sch = pg.sch
    hc = host_constants()

    x_p = pg.din('x_p', [S, D])
    x_s = pg.din('x_s', [NSAMP, D])
    p_p = pg.din('p_p', [4, S, 256])
    p_s = pg.din('p_s', [4, NSAMP, 256])
    cache_cmp = pg.din('cache_cmp', [2, NPHYS * 128, 512])
    cache_slc = pg.din('cache_slc', [2, NPHYS * 128, 512])
    st_a = pg.din('st_a', [2, 4, 512, 512])
    st_b = pg.din('st_b', [4, 2048, 512])
    ptab = pg.din('ptab', [4, 64], I32)
    rel_bias = pg.din('rel_bias', [32, 16])
    norm_g = pg.din('norm_g', [4, D])
    a_w_in = pg.din('a_w_in', [2, D, 5680])
    a_gate_b = pg.din('a_gate_b', [2, 48])
    a_qk_norm = pg.din('a_qk_norm', [2, 4, 64])
    a_phi_w1 = pg.din('a_phi_w1', [2, 2, 2, 1024, 256])
    a_phi_b1 = pg.din('a_phi_b1', [2, 2, 256])
    a_phi_w2 = pg.din('a_phi_w2', [2, 2, 256, 64])
    a_phi_b2 = pg.din('a_phi_b2', [2, 2, 64])
    a_w_out = pg.din('a_w_out', [2, D, D])
    kv_norm_g = pg.din('kv_norm_g', [D])
    b_w_kv = pg.din('b_w_kv', [D, 512])
    b_k_norm = pg.din('b_k_norm', [64])
    b_w_in = pg.din('b_w_in', [2, D, 4096])
    b_q_norm = pg.din('b_q_norm', [2, 3, 64])
    b_w_out = pg.din('b_w_out', [2, D, D])
    ple_w = pg.din('ple_w', [4, 256, D])
    ple_gate_w = pg.din('ple_gate_w', [4, D, D])
    cdram = {}
    for k, v in hc.items():
        dt = BF16 if v.dtype == ml_dtypes.bfloat16 else F32
        cdram[k] = pg.din('c_' + k, v.shape, dt)

    y_p = pg.dout('y_p', [S, D])
    y_s = pg.dout('y_s', [NSAMP, D])
    o_pc = pg.dout('o_pc', [2, S, 512])
    o_ps = pg.dout('o_ps', [2, S, 512])
    o_pw = pg.dout('o_pw', [2, 512, 512])
    o_pb = pg.dout('o_pb', [S, 512])
    o_sc = pg.dout('o_sc', [2, NSAMP, 512])
    o_ss = pg.dout('o_ss', [2, NSAMP, 512])
    o_sw = pg.dout('o_sw', [2, 4, 512, 512])
    o_sb = pg.dout('o_sb', [4, 2048, 512])

    gz_d = pg.dint('gz_d', [NTOK, 3072])
    fd = {}
    for k in ('slc', 'win', 'cmp', 'd1', 'd4', 'd16'):
        fd[k] = pg.dint('fd_' + k, [16, hc['oh_' + k].shape[1]], BF16)

    def psum(name, dt=F32, n=512):
        return nc.alloc_psum_tensor(name, [128, n], dt).ap()
    G = Ring([(psum('G0'), 'G0'), (psum('G1'), 'G1')])
    SS = Ring([(psum('S0'), 'S0'), (psum('S1'), 'S1')])
    OO = Ring([(psum('O0'), 'O0'), (psum('O1'), 'O1')])
    SS4 = Ring(SS.items + G.items)
    TB = (psum('TB', BF16, 1024), 'TB')
    TF = (psum('TF'), 'TF')

    identb = pg.sb('identb', [128, 128], BF16)
    antib = pg.sb('antib', [128, 128], BF16)
    tab33 = pg.sb('tab33', [33, 16], F32)
    xT = pg.sb('xT', [128, 8, NTOK], BF16)
    qT = pg.sb('qT', [128, 8, NTOK], BF16)
    ksT = pg.sb('ksT', [128, 2, S], BF16)
    kwT = pg.sb('kwT', [128, 2, S], BF16)
    vs_aug = pg.sb('vs_aug', [128, 16, 4, 65], BF16)
    vw_aug = pg.sb('vw_aug', [128, 16, 4, 65], BF16)
    gates = pg.sb('gates', [128, NT, 48], F32)
    wbf_i = [0]
    hin_i = [0]
    xnb = pg.sb('xnb', [128, D], BF16)
    class Prox:
        def __init__(self, name, base):
            self.name = name
            self.base = base
            self.members = [(base, name)]
            self.i = 0

        def cur(self):
            return self.members[self.i % len(self.members)]

        def __getitem__(self, k):
            return self.cur()[0][k]

        @property
        def tensor(self):
            return self.cur()[0].tensor

        @property
        def offset(self):
            return self.cur()[0].offset

        def rot(self):
            self.i += 1
            sch.alias[self.name] = self.cur()[1]

        def set_members(self, extra):
            self.members = [(self.base, self.name)] + list(extra)
            self.i = 0
            sch.alias[self.name] = self.name

    sq = Prox('sq', pg.sb('sq', [128, 512], F32))
    tmpf = Prox('tmpf', pg.sb('tmpf', [128, 512], F32))
    qnb = Prox('qnb', pg.sb('qnb', [128, 512], BF16))
    small = Prox('small', pg.sb('small', [128, 64], F32))
    PROXIES = [sq, tmpf, qnb, small]

    def rot_scratch():
        for p_ in PROXIES:
            p_.rot()
    rowb = [pg.sb('rowb%d' % i, [128, 512], F32) for i in range(2)]
    rowb_i = [0]
    gq = pg.sb('gq', [128, 4, 64], F32)
    gbq = pg.sb('gbq', [128, 3, 64], F32)
    gateb = pg.sb('gateb', [128, 48], F32)
    kccT = pg.sb('kccT', [128, 2, 128], BF16)
    vcc_aug = pg.sb('vcc_aug', [128, 4, 97], BF16)
    w2s = pg.sb('w2s', [128, 2, 64], BF16)
    b1s = pg.sb('b1s', [128, 2], F32)
    b2s = pg.sb('b2s', [128, 64], F32)
    hidT = pg.sb('hidT', [128, 2, 128], BF16)
    ovp = pg.sb('ovp', [128, 32], BF16)
    Pt_i = [0]
    yacc = pg.sb('yacc', [128, 256], F32)
    ybf = pg.sb('ybf', [128, 256], BF16)
    osb = pg.sb('osb', [128, 4, 100], F32)
    rden = pg.sb('rden', [128, 4], F32)
    sc1 = pg.sb('sc1', [128, 32], F32)
    sc2 = pg.sb('sc2', [128, 32], F32)
    mx8 = pg.sb('mx8', [128, 16], F32)
    mnegb = pg.sb('mnegb', [128, 32], BF16)
    mnegT = pg.sb('mnegT', [128, 4, 128], BF16)
    h2_i = [0]

    from contextlib import ExitStack
    LT = {}
    _uid = [0]

    class Scope:
        def __init__(self, spec, name=None):
            self.spec = spec
            self.es = ExitStack()
            self.name = name

        def __enter__(self):
            if self.name:
                self.es.enter_context(nc.named_scope(self.name))
            for name, shape, dt in self.spec:
                _uid[0] += 1
                h = self.es.enter_context(nc.sbuf_tensor('%s_u%d' % (name, _uid[0]), list(shape), dt))
                LT[name] = h.ap() if hasattr(h, 'ap') and callable(getattr(h, 'ap')) else h
            names = [n for n, _, _ in self.spec]
            for p_ in PROXIES:
                ex = [(LT[n], n) for n in names if n.startswith(p_.name + '_x')]
                if ex:
                    p_.set_members(ex)
            if 'tbring' in names:
                cur_TB[0] = Ring([TB, (TF[0].bitcast(BF16), 'TF')])
                cur_G[0] = Ring(G.items + SS.items + OO.items)
            return self

        def __exit__(self, *a):
            sch.barrier()
            names = [n for n, _, _ in self.spec]
            for p_ in PROXIES:
                if any(n.startswith(p_.name + '_x') for n in names):
                    p_.set_members([])
            if 'tbring' in names:
                cur_TB[0] = Ring([TB])
                cur_G[0] = G
            self.es.close()
            return False

    cur_TB = [None]
    cur_G = [G]
    SC_NORM = [('gt', [128, D], F32), ('junk', [128, D], F32), ('hin0', [128, D], F32), ('hin1', [128, D], F32)]
    SC_PROJ = [('wst', [128, 8, 512], F32), ('wbf', [128, 2, 8, 512], BF16),
               ('kcT', [128, 2, S], BF16), ('vcT', [128, 2, S], BF16),
               ('sq_x1', [128, 512], F32), ('sq_x2', [128, 512], F32), ('tmpf_x1', [128, 512], F32), ('tmpf_x2', [128, 512], F32),
               ('qnb_x1', [128, 512], BF16), ('qnb_x2', [128, 512], BF16), ('small_x1', [128, 64], F32), ('small_x2', [128, 64], F32), ('tbring', [128, 1], F32)]
    SC_ATT = [('G1_0', [128, 16, 4, 128], BF16), ('ksZ', [128, 2, 2, S], BF16), ('kwZ', [128, 2, 2, S], BF16), ('kccZ', [128, 2, 2, 128], BF16), ('G2_0', [128, 5, 4, 128], BF16),
              ('G2_1', [128, 5, 4, 128], BF16), ('G3_0', [128, 4, 128], BF16), ('G3_1', [128, 4, 128], BF16),
              ('gzt0', [128, 3, 256], F32), ('gzt1', [128, 3, 256], F32), ('selA', [128, 16, 32], F32), ('selB', [128, 16, 32], F32),
              ('esel', [128, 16, 128], BF16), ('Pt0', [128, 512], BF16), ('Pt1', [128, 512], BF16), ('Pt2', [128, 512], BF16), ('Pt3', [128, 512], BF16)]
    SC_OUT = [('wst', [128, 8, 512], F32), ('wo_bf', [128, 8, D], BF16), ('wg_bf', [128, 8, D], BF16), ('wp_bf', [128, 2, D], BF16),
              ('h1', [128, D], F32), ('h1b', [128, D], BF16), ('h1T', [128, 8, 128], BF16), ('sg', [128, D], F32),
              ('h2_0', [128, D], F32), ('hin0', [128, D], F32),
              ('pin', [128, 256], F32), ('pinb', [128, 256], BF16), ('pT', [128, 2, 128], BF16)]

    cur_TB[0] = Ring([TB])
    sch.dma('sp', identb[:], cdram['identb'][:], writes=['identb'])
    sch.dma('sp', antib[:], cdram['antib'][:], writes=['identb'])
    sch.dma('sp', ovp[:], cdram['ov_p'][:], writes=['ovp'])
    sch.op('dve', lambda e: e.memset(tab33[:], MASKV), writes=['tab33'])
    sch.dma('sp', tab33[0:32, :], rel_bias[:], writes=['tab33'])
    with Scope([('ohs', [33, 512], F32), ('fstage', [16, 512], BF16)]):
        ohs, fstage = LT['ohs'], LT['fstage']
        for k in ('slc', 'win', 'cmp', 'd1', 'd4', 'd16'):
            L = hc['oh_' + k].shape[1]
            for c0 in range(0, L, 512):
                cw = min(512, L - c0)
                sch.dma('sp', ohs[:, :cw], cdram['oh_' + k][:, c0:c0 + cw], writes=['ohs'])
                g, gr = G.next()
                sch.op('pe', lambda e: e.matmul(g[0:16, :cw], lhsT=tab33[:, :], rhs=ohs[:, :cw], start=True, stop=True),
                       reads=['tab33', 'ohs'], writes=[gr])
                sch.op('act', lambda e: e.copy(out=fstage[:, :cw], in_=g[0:16, :cw]), reads=[gr], writes=['fstage'])
                sch.dma('sp', fd[k][:, c0:c0 + cw], fstage[:, :cw], reads=['fstage'], writes=['fd_' + k])

    def qv(base, nblk, t):
        ps_ = nblk * NTOK
        if t < NTP:
            return AP(base, t * nblk * 128, [[ps_, 128], [128, nblk], [1, 128]])
        return AP(base, NTP * nblk * 128, [[ps_, 128], [NSAMP, nblk], [1, NSAMP]])

    def bc_row(ap1, P=128):
        n = ap1.shape[-1]
        return AP(ap1, 0, [[0, P], [1, n]])

    def load_w(W2, c0, cw, nk=8):
        wst = LT['wst']
        wbf = [LT['wbf'][:, 0], LT['wbf'][:, 1]]
        N = W2.shape[1]
        i = wbf_i[0]
        wbf_i[0] ^= 1
        nst = wst.shape[1]
        for k0 in range(0, nk, nst):
            kn = min(nst, nk - k0)
            src = AP(W2, c0 + k0 * 128 * N, [[N, 128], [128 * N, kn], [1, cw]])
            sch.dma('sp', wst[:, :kn, :cw], src, writes=['wst'])
            sch.op('pool', lambda e: e.tensor_copy(out=wbf[i][:, k0:k0 + kn, :cw], in_=wst[:, :kn, :cw]),
                   reads=['wst'], writes=['wbf%d' % i])
        return wbf[i], 'wbf%d' % i

    def gemm(lhsT_of, lres, W2, chunks, tiles, consumer, nk=8):
        seq = [(ci, t) for ci in range(len(chunks)) for t in tiles]
        wstate = {}

        def issue(idx):
            ci, t = seq[idx]
            c0, cw = chunks[ci]
            if ci not in wstate:
                wstate[ci] = load_w(W2, c0, cw, nk)
            wb, wres = wstate[ci]
            P, tok0 = tile_info(t)
            g, gr = cur_G[0].next()
            for kc in range(nk):
                sch.op('pe', lambda e: e.matmul(g[:P, :cw], lhsT=lhsT_of(kc, t), rhs=wb[:, kc, :cw],
                                                start=(kc == 0), stop=(kc == nk - 1)),
                       reads=[lres, wres], writes=[gr])
            return (ci, t, P, tok0, g, gr)
        cur = issue(0)
        for idx in range(len(seq)):
            nxt = issue(idx + 1) if idx + 1 < len(seq) else None
            rot_scratch()
            consumer(*cur)
            cur = nxt

    def rstd_from_ss(P, n, ss_ap, out_ap, inv):
        sch.op('dve', lambda e: e.tensor_scalar(out=ss_ap, in0=ss_ap, scalar1=inv, scalar2=EPS,
                                                op0=ALU.mult, op1=ALU.add), reads=['small'], writes=['small'])
        sch.op('act', lambda e: e.sqrt(out=ss_ap, in_=ss_ap), reads=['small'], writes=['small'])
        sch.op('dve', lambda e: e.reciprocal(out=out_ap, in_=ss_ap), reads=['small'], writes=['small'])

    def headnorm(P, src3, nh, gain3, out3, out_res, src_res, extra_reads=()):
        sq3 = sq[:P, :nh * 64].rearrange("p (h d) -> p h d", d=64)
        sch.op('act', lambda e: e.activation(out=sq3, in_=src3, func=AF.Square), reads=[src_res], writes=['sq'])
        sch.op('dve', lambda e: e.tensor_reduce(out=small[:P, 0:nh], in_=sq3, axis=AX.X, op=ALU.add),
               reads=['sq'], writes=['small'])
        rstd_from_ss(P, nh, small[:P, 0:nh], small[:P, 16:16 + nh], 1.0 / 64)
        t3 = tmpf[:P, :nh * 64].rearrange("p (h d) -> p h d", d=64)
        sch.op('dve', lambda e: e.tensor_tensor(out=t3, in0=src3, in1=small[:P, 16:16 + nh].to_broadcast((P, nh, 64)) if False else AP(small, 16, [[64, P], [1, nh], [0, 64]]),
                                                op=ALU.mult), reads=[src_res, 'small'], writes=['tmpf'])
        sch.op('dve', lambda e: e.tensor_tensor(out=out3, in0=t3, in1=gain3, op=ALU.mult),
               reads=['tmpf'] + list(extra_reads), writes=[out_res])

    def transposes_to(P, src2, src_res, nblk, dst3, dst_res, src_dims=None):
        tb, tbr = cur_TB[0].next()
        for b in range(nblk):
            sch.op('pe', lambda e: e.transpose(out=tb[:, b * 128:b * 128 + P], in_=src2[:, b * 128:(b + 1) * 128],
                                               identity=identb[:P, :P]),
                   reads=[src_res, 'identb'], writes=[tbr])
        tb3 = AP(tb, 0, [[1024, 128], [128, nblk], [1, P]]) if src_dims is None else AP(tb, 0, [[1024, 128]] + src_dims)
        sch.op('act', lambda e: e.copy(out=dst3, in_=tb3), reads=[tbr], writes=[dst_res])

    def phase_norm(src_of, gain_ap):
        gt = LT['gt']
        junk = LT['junk']
        hin = [LT['hin0'], LT['hin1']]
        sch.dma('sp', gt[:], bc_row(gain_ap), writes=['gt'])
        for t in range(NT):
            P, tok0 = tile_info(t)
            i = hin_i[0]
            hin_i[0] ^= 1
            hr = 'hin%d' % i
            sch.dma('sp', hin[i][:P], src_of(t), reads=['hd%d' % t], writes=[hr])
            sch.op('act', lambda e: e.activation(out=junk[:P], in_=hin[i][:P], func=AF.Square,
                                                 accum_out=small[:P, 0:1]), reads=[hr], writes=['junk', 'small'])
            rstd_from_ss(P, 1, small[:P, 0:1], small[:P, 1:2], 1.0 / D)
            sch.op('dve', lambda e: e.scalar_tensor_tensor(out=xnb[:P], in0=hin[i][:P], scalar=small[:P, 1:2],
                                                           in1=gt[:P], op0=ALU.mult, op1=ALU.mult),
                   reads=[hr, 'small', 'gt'], writes=['xnb'])
            transposes_to(P, xnb[:P], 'xnb', 8, xT[:, :, tok0:tok0 + P], 'xT')

    def h_src(layer):
        def f(t):
            P, tok0 = tile_info(t)
            if layer == 0:
                return x_p[tok0:tok0 + P, :] if t < NTP else x_s[:, :]
            return y_p[tok0:tok0 + P, :] if t < NTP else y_s[:, :]
        return f

    def a_project(li):
        W = a_w_in[li]
        kcT = LT['kcT']
        vcT = LT['vcT']
        sch.dma('sp', gq[:].rearrange("p a d -> p (a d)"), bc_row(a_qk_norm[li].rearrange("a d -> (a d)")), writes=['gq'])
        sch.op('dve', lambda e: e.tensor_scalar(out=gq[:, 0, :], in0=gq[:, 0, :], scalar1=SCALE, scalar2=None,
                                                op0=ALU.mult), reads=['gq'], writes=['gq'])
        sch.dma('sp', gateb[:], bc_row(a_gate_b[li]), writes=['gateb'])
        chunks = [(0, 512), (512, 512), (1024, 512), (1536, 512), (2048, 512), (2560, 48)] + \
                 [(2608 + 512 * z, 512) for z in range(6)]

        def consumer(ci, t, P, tok0, g, gr):
            if ci < 2:
                src3 = g[:P, :512].rearrange("p (h d) -> p h d", d=64)
                gain3 = AP(gq, 0, [[256, P], [0, 8], [1, 64]])
                out4 = AP(qnb, 0, [[512, P], [64, 2], [128, 4], [1, 64]])
                sq3 = sq[:P, :512].rearrange("p (h d) -> p h d", d=64)
                sch.op('act', lambda e: e.activation(out=sq3, in_=src3, func=AF.Square), reads=[gr], writes=['sq'])
                sch.op('dve', lambda e: e.tensor_reduce(out=small[:P, 0:8], in_=sq3, axis=AX.X, op=ALU.add),
                       reads=['sq'], writes=['small'])
                rstd_from_ss(P, 8, small[:P, 0:8], small[:P, 16:24], 1.0 / 64)
                t3 = tmpf[:P, :512].rearrange("p (h d) -> p h d", d=64)
                sch.op('dve', lambda e: e.tensor_tensor(out=t3, in0=src3, in1=AP(small, 16, [[64, P], [1, 8], [0, 64]]),
                                                        op=ALU.mult), reads=[gr, 'small'], writes=['tmpf'])
                t4 = tmpf[:P, :512].rearrange("p (a r d) -> p a r d", a=2, r=4)
                g4 = AP(gq, 0, [[256, P], [0, 2], [0, 4], [1, 64]])
                sch.op('dve', lambda e: e.tensor_tensor(out=out4, in0=t4, in1=g4, op=ALU.mult),
                       reads=['tmpf', 'gq'], writes=['qnb'])
                transposes_to(P, qnb[:P], 'qnb', 4, qv(qT, 8, t)[:, 4 * ci:4 * ci + 4, :], 'qT')
            elif ci < 5:
                kind = ci - 2
                ri = rowb_i[0]
                rowb_i[0] ^= 1
                rb = rowb[ri]
                rr = 'rowb%d' % ri
                if kind == 0:
                    sch.op('act', lambda e: e.copy(out=rb[:P, :], in_=g[:P, :512]), reads=[gr], writes=[rr])
                else:
                    src3 = g[:P, 0:256].rearrange("p (h d) -> p h d", d=64)
                    gain3 = AP(gq, (1 + kind) * 64, [[256, P], [0, 4], [1, 64]])
                    out3 = rb[:P, 0:256].rearrange("p (h d) -> p h d", d=64)
                    headnorm(P, src3, 4, gain3, out3, rr, gr, extra_reads=['gq'])
                    sch.op('act', lambda e: e.copy(out=rb[:P, 256:512], in_=g[:P, 256:512]), reads=[gr], writes=[rr])
                if t < NTP:
                    if kind == 0:
                        sch.dma('pool', o_pc[li, tok0:tok0 + P, :], rb[:P, :], reads=[rr], writes=['o_pc'])
                    elif kind == 1:
                        sch.dma('pool', o_ps[li, tok0:tok0 + P, :], rb[:P, :], reads=[rr], writes=['o_ps'])
                    elif t >= 12:
                        sch.dma('pool', o_pw[li, tok0 - 1536:tok0 - 1536 + P, :], rb[:P, :], reads=[rr], writes=['o_pw'])
                else:
                    if kind == 0:
                        sch.dma('pool', o_sc[li, :, :], rb[:P, :], reads=[rr], writes=['o_sc'])
                    elif kind == 1:
                        sch.dma('pool', o_ss[li, :, :], rb[:P, :], reads=[rr], writes=['o_ss'])
                    else:
                        for b in range(4):
                            sch.dma('pool', o_sw[li, b, 508:512, :], rb[4 * b:4 * b + 4, :], reads=[rr], writes=['o_sw'])
                if t < NTP:
                    sch.op('dve', lambda e: e.tensor_copy(out=qnb[:P, :], in_=rb[:P, :]), reads=[rr], writes=['qnb'])
                    if kind == 0:
                        transposes_to(P, qnb[:P, 0:256], 'qnb', 2, kcT[:, :, tok0:tok0 + P], 'kcT')
                        transposes_to(P, qnb[:P, 256:512], 'qnb', 2, vcT[:, :, tok0:tok0 + P], 'vcT')
                    else:
                        kt, ktr, va, var = (ksT, 'ksT', vs_aug, 'vs_aug') if kind == 1 else (kwT, 'kwT', vw_aug, 'vw_aug')
                        transposes_to(P, qnb[:P, 0:256], 'qnb', 2, kt[:, :, tok0:tok0 + P], ktr)
                        sch.op('pool', lambda e: e.tensor_copy(out=va[:, t, :, 0:64],
                                                                in_=qnb[:, 256:512].rearrange("p (g d) -> p g d", d=64)),
                               reads=['qnb'], writes=[var])
            elif ci == 5:
                sch.op('dve', lambda e: e.tensor_tensor(out=gates[:P, t, :], in0=g[:P, :48], in1=gateb[:P, :], op=ALU.add),
                       reads=[gr, 'gateb'], writes=['gates'])
                sch.op('act', lambda e: e.activation(out=gates[:P, t, :], in_=gates[:P, t, :], func=AF.Sigmoid),
                       reads=['gates'], writes=['gates'])
            else:
                zc = ci - 6
                br, hh = zc // 2, zc % 2
                ri = rowb_i[0]
                rowb_i[0] ^= 1
                rb = rowb[ri]
                rr = 'rowb%d' % ri
                sch.op('act', lambda e: e.activation(out=tmpf[:P, :], in_=g[:P, :512], func=AF.Silu), reads=[gr], writes=['tmpf'])
                gb = AP(gates, t * 48 + br * 16 + hh * 8, [[NT * 48, P], [1, 8], [0, 64]])
                sch.op('dve', lambda e: e.tensor_tensor(out=rb[:P, :].rearrange("p (h d) -> p h d", d=64),
                                                        in0=tmpf[:P, :].rearrange("p (h d) -> p h d", d=64), in1=gb, op=ALU.mult),
                       reads=['tmpf', 'gates'], writes=[rr])
                sch.dma('pool', gz_d[tok0:tok0 + P, zc * 512:(zc + 1) * 512], rb[:P, :], reads=[rr], writes=['gz_d'])

        gemm(lambda kc, t: xT[:, kc, tile_info(t)[1]:tile_info(t)[1] + tile_info(t)[0]], 'xT', W, chunks, list(range(NT)), consumer)

    def compress_prompt(li):
        wst = LT['wst']
        w1s = LT['wbf'].rearrange("p a k (b h) -> p (a k b) h", h=256)
        kcT = LT['kcT']
        vcT = LT['vcT']
        sch.op('dve', lambda e: e.memset(kccT[:], 0.0), writes=['kccT'])
        sch.op('dve', lambda e: e.memset(vcc_aug[:], 0.0), writes=['vcc_aug'])
        for kv in range(2):
            srcT, sres = (kcT, 'kcT') if kv == 0 else (vcT, 'vcT')
            for half in range(2):
                for ab in range(2):
                    src = AP(a_phi_w1[li, kv, ab], 0, [[256, 64], [64 * 256, 16], [1, 256]])
                    sch.dma('sp', wst[half * 64:half * 64 + 64, 0:8, :].rearrange("p a (b h) -> p (a b) h", h=256), src, writes=['wst'])
                    sch.op('pool', lambda e: e.tensor_copy(out=w1s[half * 64:half * 64 + 64, ab * 16:(ab + 1) * 16, :],
                                                           in_=wst[half * 64:half * 64 + 64, 0:8, :].rearrange("p a (b h) -> p (a b) h", h=256)),
                           reads=['wst'], writes=['wbf0', 'wbf1'])
            sch.dma('sp', wst[:, 0, 0:128].rearrange("p (a d) -> p a d", d=64),
                    AP(a_phi_w2[li, kv], 0, [[64, 128], [128 * 64, 2], [1, 64]]), writes=['wst'])
            sch.op('pool', lambda e: e.tensor_copy(out=w2s[:], in_=wst[:, 0, 0:128].rearrange("p (a d) -> p a d", d=64)),
                   reads=['wst'], writes=['w2s'])
            for hh_ in range(2):
                sch.dma('sp', b1s[:, hh_:hh_ + 1], AP(a_phi_b1[li, kv], hh_ * 128, [[1, 128], [1, 1]]), writes=['b1s'])
            sch.dma('sp', b2s[:], bc_row(a_phi_b2[li, kv]), writes=['b2s'])
            for g in range(4):
                half, ch = g % 2, g // 2
                hs = slice(half * 64, half * 64 + 64)
                for hh in range(2):
                    ps, psr = SS.next()
                    n = 0
                    for ab in range(2):
                        for s in range(16):
                            rhs = AP(srcT, half * 64 * (2 * S) + ch * S + 16 * ab + s, [[2 * S, 64], [16, 127]])
                            sch.op('pe', lambda e: e.matmul(ps[:, 0:127], lhsT=w1s[hs, ab * 16 + s, hh * 128:(hh + 1) * 128],
                                                            rhs=rhs, start=(n == 0), stop=(n == 31)),
                                   reads=['wbf0', 'wbf1', sres], writes=[psr])
                            n += 1
                    sch.op('act', lambda e: e.activation(out=hidT[:, hh, 0:127], in_=ps[:, 0:127], func=AF.Silu,
                                                         bias=b1s[:, hh:hh + 1]), reads=[psr, 'b1s'], writes=['hidT'])
                po, por = OO.next()
                for hh in range(2):
                    sch.op('pe', lambda e: e.matmul(po[0:127, 0:64], lhsT=hidT[:, hh, 0:127], rhs=w2s[:, hh, :],
                                                    start=(hh == 0), stop=(hh == 1)), reads=['hidT', 'w2s'], writes=[por])
                sch.op('dve', lambda e: e.tensor_tensor(out=tmpf[0:127, 0:64], in0=po[0:127, 0:64], in1=b2s[0:127, :], op=ALU.add),
                       reads=[por, 'b2s'], writes=['tmpf'])
                if kv == 0:
                    src3 = tmpf[0:127, 0:64].rearrange("p (h d) -> p h d", d=64)
                    gain3 = AP(gq, 64, [[256, 127], [0, 1], [1, 64]])
                    sq3 = sq[:127, :64].rearrange("p (h d) -> p h d", d=64)
                    sch.op('act', lambda e: e.activation(out=sq3, in_=src3, func=AF.Square), reads=['tmpf'], writes=['sq'])
                    sch.op('dve', lambda e: e.tensor_reduce(out=small[:127, 0:1], in_=sq3, axis=AX.X, op=ALU.add),
                           reads=['sq'], writes=['small'])
                    rstd_from_ss(127, 1, small[:127, 0:1], small[:127, 16:17], 1.0 / 64)
                    if half == 0:
                        sch.op('dve', lambda e: e.memset(qnb[:, 0:128], 0.0), writes=['qnb'])
                    sch.op('dve', lambda e: e.scalar_tensor_tensor(out=qnb[0:127, half * 64:half * 64 + 64], in0=tmpf[0:127, 0:64],
                                                                   scalar=small[0:127, 16:17], in1=AP(gq, 64, [[256, 127], [1, 64]]),
                                                                   op0=ALU.mult, op1=ALU.mult),
                           reads=['tmpf', 'small', 'gq'], writes=['qnb'])
                    if half == 1:
                        transposes_to(128, qnb[:, 0:128], 'qnb', 1, kccT[:, ch:ch + 1, :], 'kccT')
                else:
                    sch.op('act', lambda e: e.copy(out=vcc_aug[0:127, g, 0:64], in_=tmpf[0:127, 0:64]), reads=['tmpf'], writes=['vcc_aug'])
        for g in range(4):
            sch.op('dve', lambda e: e.memset(vcc_aug[0:127, g, 64:65], 1.0), writes=['vcc_aug'])
            sch.op('pool', lambda e: e.tensor_copy(out=vcc_aug[:, g, 65:97], in_=ovp[:, :]), reads=['ovp'], writes=['vcc_aug'])

    pend = [None]
    cur_SS = [SS]
    late = []

    def run_late():
        while late:
            late.pop(0)()

    def flush_pv():
        p = pend[0]
        pend[0] = None
        if p is not None:
            p[0]()
            if p[1] is not None:
                p[1]()

    def flush_all():
        flush_pv()
        run_late()

    def att_tile(nk, P, kT_ap, kres, qT_rhs, G_rhs, gres, extra, v_rhs, vres, o, ores, first, last, vw, score=None, after=None):
        N = 4 * P
        Pt = [LT['Pt0'], LT['Pt1'], LT['Pt2'], LT['Pt3']]
        s, sr = cur_SS[0].next()
        nmm = 2 + len(extra)
        sch.op('pe', lambda e: e.matmul(s[:nk, :N], lhsT=kT_ap, rhs=qT_rhs, start=True, stop=False),
               reads=[kres, 'qT'], writes=[sr])
        sch.op('pe', lambda e: e.matmul(s[:nk, :N], lhsT=antib[0:nk, 128 - nk:128], rhs=G_rhs, start=False, stop=(nmm == 2)),
               reads=['identb', gres], writes=[sr])
        for xi, xt in enumerate(extra):
            if len(xt) == 3:
                xl, xr, xres = xt
                sout = s[:nk, :N]
            else:
                hf_, xl, xr, xres = xt
                sout = s[64 * hf_:64 * hf_ + 64, :N]
            sch.op('pe', lambda e: e.matmul(sout, lhsT=xl, rhs=xr, start=False, stop=(xi == len(extra) - 1)),
                   reads=xres, writes=[sr])
        pi = Pt_i[0]
        Pt_i[0] = (pi + 1) % 4
        pt = Pt[pi]
        pr = 'Pt%d' % pi
        sch.op('act', lambda e: e.activation(out=pt[:nk, :N], in_=s[:nk, :N], func=AF.Exp), reads=[sr], writes=[pr])

        def pv():
            for r in range(4):
                sch.op('pe', lambda e: e.matmul(o[:P, r * 128:r * 128 + vw], lhsT=pt[:nk, r * P:(r + 1) * P], rhs=v_rhs,
                                                start=(first and r == 0), stop=(last and r == 3)), reads=[pr, vres], writes=[ores])
            if score is not None:
                srhs, sres2, o2, o2res = score
                for r in range(4):
                    sch.op('pe', lambda e: e.matmul(o2[:P, r * 128:(r + 1) * 128], lhsT=pt[:nk, r * P:(r + 1) * P], rhs=srhs,
                                                    start=(first and r == 0), stop=(last and r == 3)), reads=[pr, sres2], writes=[o2res])
        flush_pv()
        pend[0] = (pv, after)

    def toeplitz_load(dst, dres, fdk, base_off, pstep, nh_off, L, width):
        src = AP(fd[fdk], nh_off * L + base_off, [[pstep, 128], [L, 4], [1, width]])
        sch.dma('sp', dst, src, reads=['fd_' + fdk], writes=[dres])

    def a_attention_prompt(li):
        cur_SS[0] = SS4
        G1 = [LT['G1_0'], LT['G1_0']]
        ksZ, kwZ, kccZ = LT['ksZ'], LT['kwZ'], LT['kccZ']
        G2 = [LT['G2_0'], LT['G2_1']]
        G3 = [LT['G3_0'], LT['G3_1']]
        gzt = [LT['gzt0'], LT['gzt1']]
        selA, selB, esel = LT['selA'], LT['selB'], LT['esel']
        sch.dma('sp', selA[:], cdram['selA'][:], writes=['selA'])
        sch.dma('sp', selB[:], cdram['selB'][:], writes=['selB'])
        sch.op('dve', lambda e: e.memset(esel[:], 0.0), writes=['esel'])
        sch.dma('sp', esel[0:32], cdram['esel'][:], writes=['esel'])
        for Z_, src_, sres_, zres_, eng_ in ((ksZ, ksT, 'ksT', 'ksZ', 'pool'), (kwZ, kwT, 'kwT', 'kwZ', 'dve'), (kccZ, kccT, 'kccT', 'kccZ', 'dve')):
            sch.op(eng_, lambda e: e.memset(Z_[:], 0.0), writes=[zres_])
            for ch_ in range(2):
                sch.op('act', lambda e: e.copy(out=Z_[0:64, ch_, 0, :], in_=src_[0:64, ch_, :]), reads=[sres_], writes=[zres_])
                sch.op('dve', lambda e: e.tensor_copy(out=Z_[64:128, ch_, 1, :], in_=src_[64:128, ch_, :]), reads=[sres_], writes=[zres_])
        Lslc = hc['oh_slc'].shape[1]
        for g in range(4):
            half, ch = g % 2, g // 2
            hs = slice(half * 64, half * 64 + 64)
            gi = g % 2
            for dl in range(16):
                toeplitz_load(G1[gi][:, dl], 'G1_0', 'slc', 128 * dl, 1, 4 * g, Lslc, 128)
            for dl in range(5):
                toeplitz_load(G2[gi][:, dl], 'G2_%d' % gi, 'win', 128 * dl, 1, 4 * g, 768, 128)
            for b in range(NTP):
                P, tok0 = 128, b * 128
                qrhs = qv(qT, 8, b)[:, 4 * ch:4 * ch + 4, :]
                bi = b % 2
                toeplitz_load(G3[bi][:], 'G3_%d' % bi, 'cmp', tok0, 16, 4 * g, 4096, 128)
                sch.dma('sp', gzt[bi][:], AP(gz_d, tok0 * 3072 + g * 256, [[3072, 128], [1024, 3], [1, 256]]),
                        reads=['gz_d'], writes=['gzt%d' % bi])
                o, ores = OO.next()

                def after_cmp(o=o, ores=ores, bi=bi, b=b):
                    o3 = AP(o, 0, [[512, 128], [128, 4], [1, 97]])
                    sch.op('dve', lambda e: e.tensor_scalar(out=rden[:, :], in0=AP(o, 64, [[512, 128], [128, 4]]), scalar1=1e-30,
                                                            scalar2=None, op0=ALU.max), reads=[ores], writes=['rden'])
                    sch.op('dve', lambda e: e.reciprocal(out=rden[:, :], in_=rden[:, :]), reads=['rden'], writes=['rden'])
                    sch.op('dve', lambda e: e.tensor_tensor(out=osb[:, :, 0:97], in0=o3, in1=AP(rden, 0, [[4, 128], [1, 4], [0, 97]]),
                                                            op=ALU.mult), reads=[ores, 'rden'], writes=['osb'])
                    sch.op('dve', lambda e: e.tensor_tensor(out=yacc[:, :].rearrange("p (r d) -> p r d", d=64), in0=osb[:, :, 0:64],
                                                            in1=gzt[bi][:, 0, :].rearrange("p (r d) -> p r d", d=64), op=ALU.mult),
                           reads=['osb', 'gzt%d' % bi], writes=['yacc'])
                    sch.op('dve', lambda e: e.tensor_reduce(out=sc1[:, :], in_=AP(osb, 65, [[400, 128], [1, 32], [100, 4]]),
                                                            axis=AX.X, op=ALU.add), reads=['osb'], writes=['sc1'])
                    sch.op('dve', lambda e: e.tensor_tensor(out=sc1[:, :], in0=sc1[:, :], in1=selA[:, b, :], op=ALU.mult),
                           reads=['sc1', 'selA'], writes=['sc1'])
                    sch.op('dve', lambda e: e.tensor_tensor(out=sc1[:, :], in0=sc1[:, :], in1=selB[:, b, :], op=ALU.add),
                           reads=['sc1', 'selB'], writes=['sc1'])
                    sch.op('dve', lambda e: e.max(out=mx8[:, 0:8], in_=sc1[:, :]), reads=['sc1'], writes=['mx8'])
                    sch.op('dve', lambda e: e.match_replace(out=sc2[:, :], in_to_replace=mx8[:, 0:8], in_values=sc1[:, :], imm_value=-2.0),
                           reads=['sc1', 'mx8'], writes=['sc2'])
                    sch.op('dve', lambda e: e.max(out=mx8[:, 8:16], in_=sc2[:, :]), reads=['sc2'], writes=['mx8'])
                    sch.op('dve', lambda e: e.tensor_scalar(out=sc2[:, :], in0=sc1[:, :], scalar1=mx8[:, 15:16], scalar2=None,
                                                            op0=ALU.is_ge), reads=['sc1', 'mx8'], writes=['sc2'])
                    sch.op('dve', lambda e: e.tensor_scalar(out=mnegb[:, :], in0=sc2[:, :], scalar1=-1.0, scalar2=-MASKV,
                                                            op0=ALU.add, op1=ALU.mult), reads=['sc2'], writes=['mnegb'])

                    def late_cmp():
                        tb, tbr = TB
                        sch.op('pe', lambda e: e.transpose(out=tb[0:32, 0:128], in_=mnegb[:, :], identity=identb[:, :]),
                               reads=['mnegb', 'identb'], writes=[tbr])
                        sch.op('act', lambda e: e.copy(out=mnegT[0:32, :, :], in_=AP(tb, 0, [[1024, 32], [0, 4], [1, 128]])), reads=[tbr], writes=['mnegT'])
                    late.append(late_cmp)
                att_tile(128, P, kccZ[:, ch, half, :], 'kccZ', qrhs, G3[bi][:, :, :], 'G3_%d' % bi, [],
                         vcc_aug[:, g, :], 'vcc_aug', o, ores, True, True, 97, after=after_cmp)
                o, ores = OO.next()
                a0 = max(0, b - 4)

                def after_win(o=o, ores=ores, bi=bi):
                    branch_out(o, ores, gzt[bi][:, 2, :], 'gzt%d' % bi, False)
                for a in range(a0, b + 1):
                    att_tile(128, P, kwZ[:, ch, half, a * 128:(a + 1) * 128], 'kwZ', qrhs,
                             G2[gi][:, b - a], 'G2_%d' % gi, [],
                             vw_aug[:, a, g, :], 'vw_aug', o, ores, a == a0, a == b, 65, after=(after_win if a == b else None))
                o, ores = OO.next()
                mrhs = mnegT[:, :, :]

                def after_slc(o=o, ores=ores, bi=bi, g=g, tok0=tok0, P=P):
                    branch_out(o, ores, gzt[bi][:, 1, :], 'gzt%d' % bi, False)
                    sch.op('act', lambda e: e.copy(out=ybf[:, :], in_=yacc[:, :]), reads=['yacc'], writes=['ybf'])

                    def late_slc():
                        transposes_to(128, ybf[:, :], 'ybf', 2, xT[:, 2 * g:2 * g + 2, tok0:tok0 + P], 'xT')
                    late.append(late_slc)
                flush_pv() if False else None
                for a in range(b + 1):
                    if a == 0:
                        flush_pv()
                        run_late()
                    att_tile(128, P, ksZ[:, ch, half, a * 128:(a + 1) * 128], 'ksZ', qrhs,
                             G1[gi][:, b - a], 'G1_0',
                             [(esel[:, a, :], mrhs, ['esel', 'mnegT'])],
                             vs_aug[:, a, g, :], 'vs_aug', o, ores, a == 0, a == b, 65, after=(after_slc if a == b else None))
        flush_all()
        cur_SS[0] = SS

    def branch_out(o, ores, gz2, gzres, first, P=128):
        o3 = AP(o, 0, [[512, P], [128, 4], [1, 64]])
        sch.op('dve', lambda e: e.tensor_scalar(out=rden[:P, :], in0=AP(o, 64, [[512, P], [128, 4]]), scalar1=1e-30,
                                                scalar2=None, op0=ALU.max), reads=[ores], writes=['rden'])
        sch.op('dve', lambda e: e.reciprocal(out=rden[:P, :], in_=rden[:P, :]), reads=['rden'], writes=['rden'])
        sch.op('dve', lambda e: e.tensor_tensor(out=osb[:P, :, 0:64], in0=o3, in1=AP(rden, 0, [[4, P], [1, 4], [0, 64]]),
                                                op=ALU.mult), reads=[ores, 'rden'], writes=['osb'])
        sch.op('dve', lambda e: e.tensor_tensor(out=osb[:P, :, 0:64], in0=osb[:P, :, 0:64],
                                                in1=gz2.rearrange("p (r d) -> p r d", d=64), op=ALU.mult),
               reads=['osb', gzres], writes=['osb'])
        y3 = yacc[:P, :].rearrange("p (r d) -> p r d", d=64)
        if first:
            sch.op('dve', lambda e: e.tensor_copy(out=y3, in_=osb[:P, :, 0:64]), reads=['osb'], writes=['yacc'])
        else:
            sch.op('dve', lambda e: e.tensor_tensor(out=y3, in0=y3, in1=osb[:P, :, 0:64], op=ALU.add),
                   reads=['osb', 'yacc'], writes=['yacc'])

    def load_w_full(dst, dres, W2, nk):
        wst = LT['wst']
        N = W2.shape[1]
        for c0 in range(0, N, 512):
            src = AP(W2, c0, [[N, 128], [128 * N, nk], [1, 512]])
            sch.dma('sp', wst[:, :nk, :], src, writes=['wst'])
            sch.op('pool', lambda e: e.tensor_copy(out=dst[:, :, c0:c0 + 512], in_=wst[:, :nk, :]), reads=['wst'], writes=[dres])

    def out_phase(layer, wout2, yT=None):
        yT = xT if yT is None else yT
        wo_bf, wg_bf, wp_bf, h1, h1b, h1T, sg = (LT[k] for k in ('wo_bf', 'wg_bf', 'wp_bf', 'h1', 'h1b', 'h1T', 'sg'))
        h2 = [LT['h2_0'], LT['h2_0']]
        hin = [LT['hin0'], LT['hin0']]
        pin, pinb, pT = LT['pin'], LT['pinb'], LT['pT']
        load_w_full(wo_bf, 'wo_bf', wout2, 8)
        load_w_full(wg_bf, 'wg_bf', ple_gate_w[layer], 8)
        load_w_full(wp_bf, 'wp_bf', ple_w[layer], 2)
        hs = h_src(layer)
        for t in range(NT):
            P, tok0 = tile_info(t)
            i = hin_i[0]
            hin_i[0] ^= 1
            hr = 'hin0'
            sch.dma('sp', hin[i][:P], hs(t), reads=['hd%d' % t], writes=[hr])
            psrc = p_p[layer, tok0:tok0 + P, :] if t < NTP else p_s[layer, :, :]
            sch.dma('sp', pin[:P], psrc, writes=['pin'])
            for c in range(2):
                g, gr = G.next()
                for kc in range(8):
                    sch.op('pe', lambda e: e.matmul(g[:P, :], lhsT=yT[:, kc, tok0:tok0 + P], rhs=wo_bf[:, kc, c * 512:(c + 1) * 512],
                                                    start=(kc == 0), stop=(kc == 7)), reads=['xT', 'qT', 'wo_bf'], writes=[gr])
                sch.op('dve', lambda e: e.tensor_tensor(out=h1[:P, c * 512:(c + 1) * 512], in0=g[:P, :], in1=hin[i][:P, c * 512:(c + 1) * 512],
                                                        op=ALU.add), reads=[gr, hr], writes=['h1'])
            sch.op('act', lambda e: e.copy(out=h1b[:P, :], in_=h1[:P, :]), reads=['h1'], writes=['h1b'])
            transposes_to(P, h1b[:P], 'h1b', 8, h1T[:, :, :P], 'h1T')
            sch.op('dve', lambda e: e.tensor_copy(out=pinb[:P, :], in_=pin[:P, :]), reads=['pin'], writes=['pinb'])
            transposes_to(P, pinb[:P], 'pinb', 2, pT[:, :, :P], 'pT')
            j = h2_i[0]
            h2_i[0] ^= 1
            h2r = 'h2_0'
            for c in range(2):
                g, gr = G.next()
                for kc in range(8):
                    sch.op('pe', lambda e: e.matmul(g[:P, :], lhsT=h1T[:, kc, :P], rhs=wg_bf[:, kc, c * 512:(c + 1) * 512],
                                                    start=(kc == 0), stop=(kc == 7)), reads=['h1T', 'wg_bf'], writes=[gr])
                sch.op('act', lambda e: e.activation(out=sg[:P, c * 512:(c + 1) * 512], in_=g[:P, :], func=AF.Sigmoid),
                       reads=[gr], writes=['sg'])
                g, gr = G.next()
                for kc in range(2):
                    sch.op('pe', lambda e: e.matmul(g[:P, :], lhsT=pT[:, kc, :P], rhs=wp_bf[:, kc, c * 512:(c + 1) * 512],
                                                    start=(kc == 0), stop=(kc == 1)), reads=['pT', 'wp_bf'], writes=[gr])
                sch.op('dve', lambda e: e.tensor_tensor(out=sg[:P, c * 512:(c + 1) * 512], in0=g[:P, :], in1=sg[:P, c * 512:(c + 1) * 512],
                                                        op=ALU.mult), reads=[gr, 'sg'], writes=['sg'])
                sch.op('dve', lambda e: e.tensor_tensor(out=h2[j][:P, c * 512:(c + 1) * 512], in0=sg[:P, c * 512:(c + 1) * 512],
                                                        in1=h1[:P, c * 512:(c + 1) * 512], op=ALU.add), reads=['sg', 'h1'], writes=[h2r])
            dst = y_p[tok0:tok0 + P, :] if t < NTP else y_s[:, :]
            sch.dma('pool', dst, h2[j][:P, :], reads=[h2r], writes=['hd%d' % t])


    def shared_kv_phase():
        sch.dma('sp', gq[:, 1, :], bc_row(b_k_norm), writes=['gq'])

        def consumer(ci, t, P, tok0, g, gr):
            ri = rowb_i[0]
            rowb_i[0] ^= 1
            rb = rowb[ri]
            rr = 'rowb%d' % ri
            src3 = g[:P, 0:256].rearrange("p (h d) -> p h d", d=64)
            gain3 = AP(gq, 64, [[256, P], [0, 4], [1, 64]])
            out3 = rb[:P, 0:256].rearrange("p (h d) -> p h d", d=64)
            headnorm(P, src3, 4, gain3, out3, rr, gr, extra_reads=['gq'])
            sch.op('act', lambda e: e.copy(out=rb[:P, 256:512], in_=g[:P, 256:512]), reads=[gr], writes=[rr])
            if t < NTP:
                sch.dma('pool', o_pb[tok0:tok0 + P, :], rb[:P, :], reads=[rr], writes=['o_pb'])
                sch.op('dve', lambda e: e.tensor_copy(out=qnb[:P, :], in_=rb[:P, :]), reads=[rr], writes=['qnb'])
                transposes_to(P, qnb[:P, 0:256], 'qnb', 2, ksT[:, :, tok0:tok0 + P], 'ksT')
                sch.op('pool', lambda e: e.tensor_copy(out=vs_aug[:, t, :, 0:64],
                                                        in_=qnb[:, 256:512].rearrange("p (g d) -> p g d", d=64)),
                       reads=['qnb'], writes=['vs_aug'])
            else:
                for b in range(4):
                    sch.dma('pool', o_sb[b, 2044:2048, :], rb[4 * b:4 * b + 4, :], reads=[rr], writes=['o_sb'])
        gemm(lambda kc, t: xT[:, kc, tile_info(t)[1]:tile_info(t)[1] + tile_info(t)[0]], 'xT', b_w_kv, [(0, 512)],
             list(range(NT)), consumer)

    def q_consume(P, g, gr, gain_off, dst3, dres):
        src3 = g[:P, :512].rearrange("p (h d) -> p h d", d=64)
        out4 = AP(qnb, 0, [[512, P], [64, 2], [128, 4], [1, 64]])
        sq3 = sq[:P, :512].rearrange("p (h d) -> p h d", d=64)
        sch.op('act', lambda e: e.activation(out=sq3, in_=src3, func=AF.Square), reads=[gr], writes=['sq'])
        sch.op('dve', lambda e: e.tensor_reduce(out=small[:P, 0:8], in_=sq3, axis=AX.X, op=ALU.add),
               reads=['sq'], writes=['small'])
        rstd_from_ss(P, 8, small[:P, 0:8], small[:P, 16:24], 1.0 / 64)
        t3 = tmpf[:P, :512].rearrange("p (h d) -> p h d", d=64)
        sch.op('dve', lambda e: e.tensor_tensor(out=t3, in0=src3, in1=AP(small, 16, [[64, P], [1, 8], [0, 64]]),
                                                op=ALU.mult), reads=[gr, 'small'], writes=['tmpf'])
        t4 = tmpf[:P, :512].rearrange("p (a r d) -> p a r d", a=2, r=4)
        g4 = AP(gbq, gain_off, [[192, P], [0, 2], [0, 4], [1, 64]])
        sch.op('dve', lambda e: e.tensor_tensor(out=out4, in0=t4, in1=g4, op=ALU.mult),
               reads=['tmpf', 'gbq'], writes=['qnb'])
        transposes_to(P, qnb[:P], 'qnb', 4, dst3, dres)

    def b_z_phase(j):
        W = b_w_in[j]

        def consumer(ci, t, P, tok0, g, gr):
            ri = rowb_i[0]
            rowb_i[0] ^= 1
            rb = rowb[ri]
            rr = 'rowb%d' % ri
            sch.op('act', lambda e: e.activation(out=rb[:P, :], in_=g[:P, :512], func=AF.Silu), reads=[gr], writes=[rr])
            sch.dma('pool', gz_d[tok0:tok0 + P, ci * 512:(ci + 1) * 512], rb[:P, :], reads=[rr], writes=['gz_d'])
        gemm(lambda kc, t: xT[:, kc, tile_info(t)[1]:tile_info(t)[1] + tile_info(t)[0]], 'xT', W,
             [(3072, 512), (3584, 512)], list(range(NT)), consumer)

    def b_q_phase(j, gp):
        W = b_w_in[j]
        q3T = LT['q3T']
        sch.dma('sp', gbq[:].rearrange("p a d -> p (a d)"), bc_row(b_q_norm[j].rearrange("a d -> (a d)")), writes=['gbq'])
        sch.op('dve', lambda e: e.tensor_scalar(out=gbq[:], in0=gbq[:], scalar1=SCALE, scalar2=None, op0=ALU.mult),
               reads=['gbq'], writes=['gbq'])

        def consumer(ci, t, P, tok0, g, gr):
            q_consume(P, g, gr, ci * 64, qv(q3T, 12, t)[:, 4 * ci:4 * ci + 4, :], 'qT')
        gemm(lambda kc, t: xT[:, kc, tile_info(t)[1]:tile_info(t)[1] + tile_info(t)[0]], 'xT', W,
             [(grp * 1024 + gp * 512, 512) for grp in range(3)], list(range(NT)), consumer)

    kshZ = [(kwT, 'kwT'), (AP(vw_aug, 0, [[16 * 4 * 65, 128], [S, 2], [1, S]]), 'vw_aug')]

    def build_kshZ():
        for ch_ in range(2):
            Z_, zres_ = kshZ[ch_]
            sch.op('dve' if ch_ == 0 else 'pool', lambda e: e.memset(Z_[:, :, :], 0.0), writes=[zres_])
            sch.op('act', lambda e: e.copy(out=Z_[0:64, 0, :], in_=ksT[0:64, ch_, :]), reads=['ksT'], writes=[zres_])
            sch.op('dve', lambda e: e.tensor_copy(out=Z_[64:128, 1, :], in_=ksT[64:128, ch_, :]), reads=['ksT'], writes=[zres_])

    def b_attention_prompt(j, gp):
        cur_SS[0] = SS4
        if j == 0 and gp == 0:
            build_kshZ()
        q3T = LT['q3T']
        Gd = {1: LT['Gd1'], 4: LT['Gd4'], 16: LT['Gd16']}
        gzt = [LT['gzt0'], LT['gzt1']]
        Ld = {1: 384, 4: 768, 16: 2304}
        Wd = {1: 256, 4: 640, 16: 2048}
        for g in (2 * gp, 2 * gp + 1):
            half, ch = g % 2, g // 2
            hs = slice(half * 64, half * 64 + 64)
            for d in (1, 4, 16):
                for dl in range(Wd[d] // 128):
                    toeplitz_load(Gd[d][:, dl], 'Gd%d' % d, 'd%d' % d, 128 * dl, 1, 4 * g, Ld[d], 128)
            for b in range(NTP):
                P, tok0 = 128, b * 128
                bi = b % 2
                sch.dma('sp', gzt[bi][:, 0, :], gz_d[tok0:tok0 + P, g * 256:(g + 1) * 256], reads=['gz_d'], writes=['gzt%d' % bi])
                o, ores = OO.next()
                tiles = []
                for grp, d, na in ((0, 1, 2), (1, 4, 5), (2, 16, 99)):
                    for a in range(max(0, b - na + 1), b + 1):
                        tiles.append((grp, d, a))
                def after_b(o=o, ores=ores, bi=bi, g=g, tok0=tok0, P=P):
                    branch_out(o, ores, gzt[bi][:, 0, :], 'gzt%d' % bi, True)
                    sch.op('act', lambda e: e.copy(out=ybf[:, :], in_=yacc[:, :]), reads=['yacc'], writes=['ybf'])

                    def late_b():
                        transposes_to(128, ybf[:, :], 'ybf', 2, qT[:, 2 * g:2 * g + 2, tok0:tok0 + P], 'qT')
                    late.append(late_b)
                for ti, (grp, d, a) in enumerate(tiles):
                    if ti == min(3, len(tiles) - 1):
                        run_late()
                    qrhs = qv(q3T, 12, b)[:, 4 * grp:4 * grp + 4, :]
                    att_tile(128, P, kshZ[ch][0][:, half, a * 128:(a + 1) * 128], kshZ[ch][1], qrhs,
                             Gd[d][:, b - a], 'Gd%d' % d, [],
                             vs_aug[:, a, g, :], 'vs_aug', o, ores, ti == 0, ti == len(tiles) - 1, 65,
                             after=(after_b if ti == len(tiles) - 1 else None))
        flush_all()
        cur_SS[0] = SS

    SC_PROJB = [('wst', [128, 4, 512], F32), ('wbf', [128, 2, 8, 512], BF16), ('sq_x1', [128, 512], F32), ('tmpf_x1', [128, 512], F32),
                ('qnb_x1', [128, 512], BF16), ('qnb_x2', [128, 512], BF16), ('small_x1', [128, 64], F32), ('small_x2', [128, 64], F32), ('tbring', [128, 1], F32)]
    SC_Q3 = [('q3T', [128, 12, NTOK], BF16)]
    SC_ATTB = [('Gd1', [128, 2, 4, 128], BF16), ('Gd4', [128, 5, 4, 128], BF16), ('Gd16', [128, 16, 4, 128], BF16),
               ('gzt0', [128, 1, 256], F32), ('gzt1', [128, 1, 256], F32),
               ('Pt0', [128, 512], BF16), ('Pt1', [128, 512], BF16), ('Pt2', [128, 512], BF16), ('Pt3', [128, 512], BF16)]


    GsT = pg.sb('GsT', [128, 58, 16, 4], BF16)
    msk_d = pg.dint('msk_d', [4, 2048], BF16)
    cache_cmp_flat = AP(cache_cmp, 0, [[512, 2 * NPHYS * 128], [1, 512]])
    cache_slc_flat = AP(cache_slc, 0, [[512, 2 * NPHYS * 128], [1, 512]])
    TI_SLC, TI_WIN, TI_CMP, TI_B = 0, 25, 30, 34

    def load_gs(ti, fdk, L, base, pstep, nk):
        src = AP(fd[fdk], base, [[pstep, nk], [L, 16], [1, 4]])
        sch.dma('sp', GsT[:nk, ti, :, :], src, reads=['fd_' + fdk], writes=['GsT'])

    def sample_tables():
        Ls = hc['oh_slc'].shape[1]
        for a in range(40, 64):
            load_gs(TI_SLC + a - 40, 'slc', Ls, 8192 - 128 * a, 1, 128)
        load_gs(TI_SLC + 24, 'slc', Ls, 124, 1, 4)
        for a in range(4):
            load_gs(TI_WIN + a, 'win', 768, 512 - 128 * a, 1, 128)
        load_gs(TI_WIN + 4, 'win', 768, 124, 1, 4)
        for ct in range(4):
            load_gs(TI_CMP + ct, 'slc', Ls, 6256 - 2048 * ct, 16, 128)
        ti = TI_B
        for d, L, a0 in ((1, 384, 15), (4, 768, 12), (16, 2304, 0)):
            for a in range(a0, 16):
                load_gs(ti, 'd%d' % d, L, 2048 - 128 * a, 1, 128)
                ti += 1
            load_gs(ti, 'd%d' % d, L, 124, 1, 4)
            ti += 1

    SC_PREP = [('rows0', [128, 512], F32), ('rows1', [128, 512], F32), ('kb0', [128, 256], BF16), ('kb1', [128, 256], BF16),
               ('kTt0', [128, 2, 128], BF16), ('kTt1', [128, 2, 128], BF16), ('kTt2', [128, 2, 128], BF16),
               ('vat0', [128, 4, 65], BF16), ('vat1', [128, 4, 65], BF16), ('vat2', [128, 4, 65], BF16),
               ('Pt0', [128, 512], BF16), ('Pt1', [128, 512], BF16), ('Pt2', [128, 512], BF16), ('Pt3', [128, 512], BF16),
               ('gzs', [4, 1024], F32), ('ys', [4, 4, 256], F32)]
    SC_SAMPA = SC_PREP + [('idx', [128, 256], I32), ('ptab_i', [128, 256], I32), ('pidx', [128, 1], F32),
                          ('w1s', [128, 32, 256], BF16), ('w2s2', [128, 2, 2, 64], BF16), ('b1s2', [128, 4], F32),
                          ('b2s2', [128, 2, 64], F32), ('wst4', [128, 4, 256], F32), ('chT', [128, 4, 2176], BF16),
                          ('rowsb', [128, 512], BF16), ('kccT_s', [128, 2, 512], BF16), ('vcc_s', [128, 4, 4, 65], BF16),
                          ('ovs', [128, 4, 128], BF16), ('selBs', [4, 136], F32), ('sc1s', [4, 136], F32), ('sc2s', [4, 136], F32),
                          ('scs', [4, 128], F32), ('mx8s', [4, 16], F32), ('mneg_s', [4, 4, 128], BF16), ('Xm', [1, 2048], BF16),
                          ('e2', [1, 64], BF16)]
    prep_i = [0, 0]

    def prep_tile(load_fn, nk):
        ri = prep_i[0]
        prep_i[0] ^= 1
        ki = prep_i[1]
        prep_i[1] = (ki + 1) % 3
        rows, kb, kTt, vat = LT['rows%d' % ri], LT['kb%d' % ri], LT['kTt%d' % ki], LT['vat%d' % ki]
        rr, kbr, ktr, var = 'rows%d' % ri, 'kb%d' % ri, 'kTt%d' % ki, 'vat%d' % ki
        load_fn(rows[:nk, :], rr)
        sch.op('dve', lambda e: e.tensor_copy(out=kb[:nk, :], in_=rows[:nk, 0:256]), reads=[rr], writes=[kbr])
        sch.op('act', lambda e: e.copy(out=vat[:nk, :, 0:64], in_=rows[:nk, 256:512].rearrange("p (g d) -> p g d", d=64)),
               reads=[rr], writes=[var])
        transposes_to(nk, kb[:nk, :], kbr, 2, kTt[:, :, :nk], ktr)
        return kTt, ktr, vat, var

    def dram_loader(src_ap, res=()):
        def f(rows_ap, rr):
            sch.dma('sp', rows_ap, src_ap, reads=list(res), writes=[rr])
        return f

    def page_loader(flat, col):
        def f(rows_ap, rr):
            sch.idma(rows_ap, flat, LT['idx'][:, col:col + 1], reads=['idx'], writes=[rr])
        return f

    def branch_out_s(o, ores, g, first):
        P = 4
        gzs, ys = LT['gzs'], LT['ys']
        o3 = AP(o, 0, [[512, P], [128, 4], [1, 64]])
        sch.op('dve', lambda e: e.tensor_scalar(out=rden[:P, :], in0=AP(o, 64, [[512, P], [128, 4]]), scalar1=1e-30,
                                                scalar2=None, op0=ALU.max), reads=[ores], writes=['rden'])
        sch.op('dve', lambda e: e.reciprocal(out=rden[:P, :], in_=rden[:P, :]), reads=['rden'], writes=['rden'])
        sch.op('dve', lambda e: e.tensor_tensor(out=osb[:P, :, 0:64], in0=o3, in1=AP(rden, 0, [[4, P], [1, 4], [0, 64]]),
                                                op=ALU.mult), reads=[ores, 'rden'], writes=['osb'])
        gz3 = gzs[:P, g * 256:(g + 1) * 256].rearrange("p (r d) -> p r d", d=64)
        sch.op('dve', lambda e: e.tensor_tensor(out=osb[:P, :, 0:64], in0=osb[:P, :, 0:64], in1=gz3, op=ALU.mult),
               reads=['osb', 'gzs'], writes=['osb'])
        y3 = ys[:P, g, :].rearrange("p (r d) -> p r d", d=64)
        if first:
            sch.op('dve', lambda e: e.tensor_copy(out=y3, in_=osb[:P, :, 0:64]), reads=['osb'], writes=['ys'])
        else:
            sch.op('dve', lambda e: e.tensor_tensor(out=y3, in0=y3, in1=osb[:P, :, 0:64], op=ALU.add),
                   reads=['osb', 'ys'], writes=['ys'])

    def samp_finish(bs, ydst, yres, groups=(0, 1, 2, 3)):
        ys = LT['ys']
        qs = S + 4 * bs
        for g in groups:
            sch.op('act', lambda e: e.copy(out=ybf[:4, :], in_=ys[:4, g, :]), reads=['ys'], writes=['ybf'])
            transposes_to(4, ybf[:4, :], 'ybf', 2, ydst[:, 2 * g:2 * g + 2, qs:qs + 4], yres)

    def load_gz(bs, br):
        sch.dma('sp', LT['gzs'][:4, :], gz_d[S + 4 * bs:S + 4 * bs + 4, br * 1024:(br + 1) * 1024], reads=['gz_d'], writes=['gzs'])

    def samp_multi(bs, tiles, qsrc, qblk_of, groups, masked):
        acc = {0: G.items[0], 1: G.items[1], 2: OO.items[0], 3: OO.items[1]}
        qs = S + 4 * bs
        for ti, (load_fn, nk, tab, ea, grp) in enumerate(tiles):
            kTt, ktr, vat, var = prep_tile(load_fn, nk)
            for g in groups:
                half, ch = g % 2, g // 2
                hs = slice(half * 64, half * 64 + 64)
                qb = qblk_of(g, grp)
                qrhs = qsrc[hs, qb:qb + 4, 4 * bs:4 * bs + 4]
                extra = []
                if masked and ea is not None:
                    extra = [(hf_, LT['e2'][0:1, 0:64], AP(LT['Xm'], g * 128 + 2 * ea + hf_, [[2048, 1], [0, 4], [512, 4]]), ['e2', 'Xm'])
                             for hf_ in range(2)]
                o, ores = acc[g]
                att_tile(nk, 4, kTt[hs, ch, :nk], ktr, qrhs, GsT[:nk, tab, 4 * g:4 * g + 4, :], 'GsT', extra,
                         vat[:nk, g, :], var, o, ores, ti == 0, ti == len(tiles) - 1, 65)
        flush_pv()
        return acc

    def init_vat():
        for i_ in range(3):
            sch.op('dve', lambda e: e.memset(LT['vat%d' % i_][:, :, 64:65], 1.0), writes=['vat%d' % i_])

    def a_sample(li):
        init_vat()
        idx, ptab_i, pidx = LT['idx'], LT['ptab_i'], LT['pidx']
        w1s, w2s2, b1s2, b2s2, wst4 = LT['w1s'], LT['w2s2'], LT['b1s2'], LT['b2s2'], LT['wst4']
        chT, rowsb, kccT_s, vcc_s, ovs = LT['chT'], LT['rowsb'], LT['kccT_s'], LT['vcc_s'], LT['ovs']
        selBs, sc1s, sc2s, scs, mx8s, mneg_s, Xm, e2 = (LT[k] for k in ('selBs', 'sc1s', 'sc2s', 'scs', 'mx8s', 'mneg_s', 'Xm', 'e2'))
        sch.dma('sp', ptab_i[:, :], AP(ptab, 0, [[0, 128], [1, 256]]), writes=['ptab_i'])
        sch.dma('sp', pidx[:, :], cdram['pidx'][:, :], writes=['pidx'])
        sch.dma('sp', ovs[:], cdram['ov_s'][:], writes=['ovs'])
        sch.dma('sp', selBs[:], cdram['selB_s'][:], writes=['selBs'])
        sch.op('dve', lambda e: e.memset(e2[:], 1.0), writes=['e2'])
        sch.op('dve', lambda e: e.tensor_scalar(out=idx[:, :], in0=ptab_i[:, :], scalar1=128.0, scalar2=pidx[:, 0:1],
                                                op0=ALU.mult, op1=ALU.add), reads=['ptab_i', 'pidx'], writes=['idx'])
        if li == 1:
            sch.op('dve', lambda e: e.tensor_scalar(out=idx[:, :], in0=idx[:, :], scalar1=float(NPHYS * 128), scalar2=None,
                                                    op0=ALU.add), reads=['idx'], writes=['idx'])
        for kv in range(2):
            sch.dma('sp', wst4[:, 0, 0:128].rearrange("p (a d) -> p a d", d=64),
                    AP(a_phi_w2[li, kv], 0, [[64, 128], [128 * 64, 2], [1, 64]]), writes=['wst4'])
            sch.op('pool', lambda e: e.tensor_copy(out=w2s2[:, kv, :, :], in_=wst4[:, 0, 0:128].rearrange("p (a d) -> p a d", d=64)),
                   reads=['wst4'], writes=['w2s2'])
            for hh_ in range(2):
                sch.dma('sp', b1s2[:, 2 * kv + hh_:2 * kv + hh_ + 1], AP(a_phi_b1[li, kv], hh_ * 128, [[1, 128], [1, 1]]), writes=['b1s2'])
            sch.dma('sp', b2s2[:, kv, :], bc_row(a_phi_b2[li, kv]), writes=['b2s2'])

        ksv = ksT[:, :, :].rearrange("p a (b h) -> p (a b) h", h=256)
        kwv = kwT[:, :, :].rearrange("p a (b h) -> p (a b) h", h=256)

        def w1_of(kv, ab):
            if kv == 0:
                return w1s[:, ab * 16:(ab + 1) * 16, :], 'w1s'
            return (ksv, 'ksT') if ab == 0 else (kwv, 'kwT')

        for kv in range(2):
            for ab in range(2):
                wv, wres = w1_of(kv, ab)
                for sq4 in range(4):
                    for half in range(2):
                        src = AP(a_phi_w1[li, kv, ab], sq4 * 4 * 64 * 256, [[256, 64], [64 * 256, 4], [1, 256]])
                        sch.dma('sp', wst4[half * 64:half * 64 + 64, :, :], src, writes=['wst4'])
                    sch.op('pool', lambda e: e.tensor_copy(out=wv[:, sq4 * 4:sq4 * 4 + 4, :], in_=wst4[:, :, :]),
                           reads=['wst4'], writes=[wres])

        for bs in range(4):
            qs = S + 4 * bs
            sch.op('dve', lambda e: e.memset(kccT_s[:], 0.0), writes=['kccT_s'])
            sch.op('dve', lambda e: e.memset(vcc_s[:], 0.0), writes=['vcc_s'])
            for ct in range(4):
                npg = 17 if ct < 3 else 16
                ncol = 128 if ct < 3 else 127
                for pi in range(npg):
                    ri = prep_i[0]
                    prep_i[0] ^= 1
                    rows, rr = LT['rows%d' % ri], 'rows%d' % ri
                    sch.idma(rows[:, :], cache_cmp_flat, idx[:, bs * 64 + 16 * ct + pi:bs * 64 + 16 * ct + pi + 1],
                             reads=['idx'], writes=[rr])
                    if pi % 2 == 0:
                        sch.op('dve', lambda e: e.tensor_copy(out=rowsb[:, :], in_=rows[:, :]), reads=[rr], writes=['rowsb'])
                    else:
                        sch.op('act', lambda e: e.copy(out=rowsb[:, :], in_=rows[:, :]), reads=[rr], writes=['rowsb'])
                    transposes_to(128, rowsb[:, :], 'rowsb', 4, AP(chT, pi * 8, [[4 * 2176, 128], [2176, 4], [1, 8], [136, 16]]), 'chT',
                                  src_dims=[[128, 4], [16, 8], [1, 16]])
                for kv in range(2):
                    for g in range(4):
                        half, ch = g % 2, g // 2
                        hs = slice(half * 64, half * 64 + 64)
                        blk = kv * 2 + ch
                        for hh in range(2):
                            ps, psr = SS.next()
                            n = 0
                            for ab in range(2):
                                for s_ in range(16):
                                    rhs = AP(chT, half * 64 * (4 * 2176) + blk * 2176 + s_ * 136 + ab, [[4 * 2176, 64], [1, ncol]])
                                    wv_, wres_ = w1_of(kv, ab)
                                    sch.op('pe', lambda e: e.matmul(ps[:, 0:ncol], lhsT=wv_[hs, s_, hh * 128:(hh + 1) * 128],
                                                                    rhs=rhs, start=(n == 0), stop=(n == 31)),
                                           reads=[wres_, 'chT'], writes=[psr])
                                    n += 1
                            sch.op('act', lambda e: e.activation(out=hidT[:, hh, 0:ncol], in_=ps[:, 0:ncol], func=AF.Silu,
                                                                 bias=b1s2[:, 2 * kv + hh:2 * kv + hh + 1]), reads=[psr, 'b1s2'], writes=['hidT'])
                        po, por = OO.next()
                        for hh in range(2):
                            sch.op('pe', lambda e: e.matmul(po[0:ncol, 0:64], lhsT=hidT[:, hh, 0:ncol], rhs=w2s2[:, kv, hh, :],
                                                            start=(hh == 0), stop=(hh == 1)), reads=['hidT', 'w2s2'], writes=[por])
                        sch.op('dve', lambda e: e.tensor_tensor(out=tmpf[0:ncol, 0:64], in0=po[0:ncol, 0:64], in1=b2s2[0:ncol, kv, :], op=ALU.add),
                               reads=[por, 'b2s2'], writes=['tmpf'])
                        if kv == 0:
                            src3 = tmpf[0:ncol, 0:64].rearrange("p (h d) -> p h d", d=64)
                            sq3 = sq[:ncol, :64].rearrange("p (h d) -> p h d", d=64)
                            sch.op('act', lambda e: e.activation(out=sq3, in_=src3, func=AF.Square), reads=['tmpf'], writes=['sq'])
                            sch.op('dve', lambda e: e.tensor_reduce(out=small[:ncol, 0:1], in_=sq3, axis=AX.X, op=ALU.add),
                                   reads=['sq'], writes=['small'])
                            rstd_from_ss(ncol, 1, small[:ncol, 0:1], small[:ncol, 16:17], 1.0 / 64)
                            if half == 0:
                                sch.op('dve', lambda e: e.memset(qnb[:, 0:128], 0.0), writes=['qnb'])
                            sch.op('dve', lambda e: e.scalar_tensor_tensor(out=qnb[0:ncol, half * 64:half * 64 + 64], in0=tmpf[0:ncol, 0:64],
                                                                           scalar=small[0:ncol, 16:17], in1=AP(gq, 64, [[256, ncol], [1, 64]]),
                                                                           op0=ALU.mult, op1=ALU.mult),
                                   reads=['tmpf', 'small', 'gq'], writes=['qnb'])
                            if half == 1:
                                transposes_to(128, qnb[:, 0:128], 'qnb', 1, kccT_s[:, ch:ch + 1, ct * 128:(ct + 1) * 128], 'kccT_s')
                        else:
                            sch.op('act', lambda e: e.copy(out=vcc_s[0:ncol, ct, g, 0:64], in_=tmpf[0:ncol, 0:64]), reads=['tmpf'], writes=['vcc_s'])
                            sch.op('dve', lambda e: e.memset(vcc_s[0:ncol, ct, g, 64:65], 1.0), writes=['vcc_s'])
            load_gz(bs, 0)
            for g in range(4):
                half, ch = g % 2, g // 2
                hs = slice(half * 64, half * 64 + 64)
                qrhs = qv(qT, 8, NTP)[hs, 4 * ch:4 * ch + 4, 4 * bs:4 * bs + 4]
                o, ores = OO.next()
                o2, o2res = TF
                for ct in range(4):
                    att_tile(128, 4, kccT_s[hs, ch, ct * 128:(ct + 1) * 128], 'kccT_s', qrhs, GsT[:, TI_CMP + ct, 4 * g:4 * g + 4, :], 'GsT',
                             [], vcc_s[:, ct, g, :], 'vcc_s', o, ores, ct == 0, ct == 3, 65,
                             score=(ovs[:, ct, :], 'ovs', o2, o2res))
                flush_pv()
                branch_out_s(o, ores, g, True)
                sch.op('dve', lambda e: e.tensor_scalar(out=scs[:, :], in0=o2[:4, 0:128], scalar1=rden[:4, 0:1], scalar2=None, op0=ALU.mult),
                       reads=[o2res, 'rden'], writes=['scs'])
                for r in range(1, 4):
                    sch.op('dve', lambda e: e.scalar_tensor_tensor(out=scs[:, :], in0=o2[:4, r * 128:(r + 1) * 128], scalar=rden[:4, r:r + 1],
                                                                   in1=scs[:, :], op0=ALU.mult, op1=ALU.add),
                           reads=[o2res, 'rden', 'scs'], writes=['scs'])
                sch.op('dve', lambda e: e.tensor_copy(out=sc1s[:, :], in_=selBs[:, :]), reads=['selBs'], writes=['sc1s'])
                sch.op('dve', lambda e: e.tensor_tensor(out=sc1s[:, 0:128], in0=scs[:, :], in1=selBs[:, 0:128], op=ALU.add),
                       reads=['scs', 'selBs'], writes=['sc1s'])
                sch.op('dve', lambda e: e.max(out=mx8s[:, 0:8], in_=sc1s[:, :]), reads=['sc1s'], writes=['mx8s'])
                sch.op('dve', lambda e: e.match_replace(out=sc2s[:, :], in_to_replace=mx8s[:, 0:8], in_values=sc1s[:, :], imm_value=-2.0),
                       reads=['sc1s', 'mx8s'], writes=['sc2s'])
                sch.op('dve', lambda e: e.max(out=mx8s[:, 8:16], in_=sc2s[:, :]), reads=['sc2s'], writes=['mx8s'])
                sch.op('dve', lambda e: e.tensor_scalar(out=sc2s[:, :], in0=sc1s[:, :], scalar1=mx8s[:, 15:16], scalar2=None,
                                                        op0=ALU.is_ge), reads=['sc1s', 'mx8s'], writes=['sc2s'])
                sch.op('dve', lambda e: e.tensor_scalar(out=mneg_s[:, g, :], in0=sc2s[:, 0:128], scalar1=-1.0, scalar2=-MASKV,
                                                        op0=ALU.add, op1=ALU.mult), reads=['sc2s'], writes=['mneg_s'])
            sch.dma('sp', AP(msk_d, bs * 2048, [[512, 4], [128, 4], [1, 128]]), mneg_s[:, :, :], reads=['mneg_s'], writes=['msk_d'])
            sch.dma('sp', Xm[0:1, :], AP(msk_d, bs * 2048, [[0, 1], [1, 2048]]), reads=['msk_d'], writes=['Xm'])
            load_gz(bs, 1)
            tiles = []
            for a in range(64):
                tab = TI_SLC + max(a, 40) - 40
                tiles.append((page_loader(cache_slc_flat, bs * 64 + a), 128, tab, a, 0))
            tiles.append((dram_loader(o_ss[li, 4 * bs:4 * bs + 4, :], ('o_ss',)), 4, TI_SLC + 24, None, 0))
            acc = samp_multi(bs, tiles, qv(qT, 8, NTP), lambda g, grp: 4 * (g // 2), (0, 1, 2, 3), True)
            for g in range(4):
                branch_out_s(acc[g][0], acc[g][1], g, False)
            load_gz(bs, 2)
            tiles = []
            for a in range(4):
                tiles.append((dram_loader(st_a[li, bs, 128 * a:128 * a + 128, :]), 128, TI_WIN + a, None, 0))
            tiles.append((dram_loader(o_sw[li, bs, 508:512, :], ('o_sw',)), 4, TI_WIN + 4, None, 0))
            acc = samp_multi(bs, tiles, qv(qT, 8, NTP), lambda g, grp: 4 * (g // 2), (0, 1, 2, 3), False)
            for g in range(4):
                branch_out_s(acc[g][0], acc[g][1], g, False)
            samp_finish(bs, xT, 'xT')

    def b_sample(j, gp):
        init_vat()
        q3T = LT['q3T']
        groups = (2 * gp, 2 * gp + 1)
        for bs in range(4):
            sch.dma('sp', LT['gzs'][:4, :], gz_d[S + 4 * bs:S + 4 * bs + 4, 0:1024], reads=['gz_d'], writes=['gzs'])
            tiles = []
            ti = TI_B
            for grp, (d, a0) in enumerate(((1, 15), (4, 12), (16, 0))):
                for a in range(a0, 16):
                    tiles.append((dram_loader(st_b[bs, 128 * a:128 * a + 128, :]), 128, ti, None, grp))
                    ti += 1
                tiles.append((dram_loader(o_sb[bs, 2044:2048, :], ('o_sb',)), 4, ti, None, grp))
                ti += 1
            acc = samp_multi(bs, tiles, qv(q3T, 12, NTP), lambda g, grp: 4 * grp, groups, False)
            for g in groups:
                branch_out_s(acc[g][0], acc[g][1], g, True)
            samp_finish(bs, qT, 'qT', groups)

    def state_copies():
        for li in range(2):
            for b in range(4):
                sch.dma('pool', o_sw[li, b, 0:508, :], st_a[li, b, 4:512, :], writes=['o_sw'])
        for b in range(4):
            sch.dma('pool', o_sb[b, 0:2044, :], st_b[b, 4:2048, :], writes=['o_sb'])

    sch.op('dve', lambda e: e.memset(vs_aug[:, :, :, 64:65], 1.0), writes=['vs_aug'])
    sch.op('dve', lambda e: e.memset(vw_aug[:, :, :, 64:65], 1.0), writes=['vw_aug'])

    sch.op('dve', lambda e: e.memset(mnegT[:], 0.0), writes=['mnegT'])
    state_copies()
    sample_tables()
    for layer in range(4):
        if layer >= STAGE:
            break
        with Scope(SC_NORM, 'norm'):
            phase_norm(h_src(layer), norm_g[layer])
        if layer < 2:
            with Scope(SC_PROJ, 'projA'):
                a_project(layer)
                compress_prompt(layer)
            with Scope(SC_ATT, 'attA'):
                a_attention_prompt(layer)
            with Scope(SC_SAMPA, 'sampA'):
                a_sample(layer)
            with Scope(SC_OUT, 'out'):
                out_phase(layer, a_w_out[layer])
            if layer == 1 and STAGE > 2:
                with Scope(SC_NORM, 'norm'):
                    phase_norm(h_src(2), kv_norm_g)
                with Scope(SC_PROJB, 'projB'):
                    shared_kv_phase()
        else:
            j = layer - 2
            with Scope(SC_PROJB, 'projB'):
                b_z_phase(j)
            with Scope(SC_Q3):
                for gp in range(2):
                    with Scope(SC_PROJB, 'projB'):
                        b_q_phase(j, gp)
                    if gp == 1:
                        pass
                    with Scope(SC_ATTB, 'attB'):
                        b_attention_prompt(j, gp)
                    with Scope(SC_PREP, 'sampB'):
                        b_sample(j, gp)
            with Scope(SC_OUT, 'out'):
                out_phase(layer, b_w_out[j], qT)

    sch.barrier()
    return pg, hc


_CACHE = {}


def kernel(**inputs):
    if 'prog' not in _CACHE:
        _CACHE['prog'] = build()
    pg, hc = _CACHE['prog']
    f = lambda a: np.ascontiguousarray(np.asarray(a))
    x_prompt = f(inputs['x_prompt'])
    x_sample = f(inputs['x_sample'])
    cache_cmp = f(inputs['cache_a_cmp']).reshape(2, NPHYS * 128, 512)
    cache_slc = f(inputs['cache_a_slc']).reshape(2, NPHYS * 128, 512)
    st_a = f(inputs['state_a_win'])
    st_b = f(inputs['state_b_win'])
    ptab = f(inputs['page_table']).astype(np.int32)
    p_prompt = f(inputs['p_prompt'])
    p_sample = f(inputs['p_sample'])
    shared = {
        'cache_cmp': cache_cmp, 'cache_slc': cache_slc,
        'rel_bias': f(inputs['rel_bias']), 'norm_g': f(inputs['norm_g']), 'a_w_in': f(inputs['a_w_in']),
        'a_gate_b': f(inputs['a_gate_b']).reshape(2, 48), 'a_qk_norm': f(inputs['a_qk_norm']),
        'a_phi_w1': f(inputs['a_phi_w1']), 'a_phi_b1': f(inputs['a_phi_b1']), 'a_phi_w2': f(inputs['a_phi_w2']),
        'a_phi_b2': f(inputs['a_phi_b2']), 'a_w_out': f(inputs['a_w_out']), 'kv_norm_g': f(inputs['kv_norm_g']),
        'b_w_kv': f(inputs['b_w_kv']), 'b_k_norm': f(inputs['b_k_norm']), 'b_w_in': f(inputs['b_w_in']),
        'b_q_norm': f(inputs['b_q_norm']), 'b_w_out': f(inputs['b_w_out']), 'ple_w': f(inputs['ple_w']),
        'ple_gate_w': f(inputs['ple_gate_w']),
    }
    for k, v in hc.items():
        shared['c_' + k] = v
    in_maps = []
    for c in range(8):
        m = dict(shared)
        m['x_p'] = x_prompt[c]
        m['x_s'] = x_sample[4 * c:4 * c + 4].reshape(16, D)
        m['p_p'] = p_prompt[:, c]
        m['p_s'] = p_sample[:, 4 * c:4 * c + 4].reshape(4, 16, 256)
        m['st_a'] = st_a[:, 4 * c:4 * c + 4].reshape(2, 4, 512, 512)
        m['st_b'] = st_b[4 * c:4 * c + 4].reshape(4, 2048, 512)
        m['ptab'] = ptab[4 * c:4 * c + 4]
        in_maps.append({k: m[k] for k in pg.din_names})
    res = run_bass_kernel_spmd(pg.nc, in_maps, core_ids=list(range(8)))
    R = res.results
    cat = lambda k, ax=0: np.stack([np.asarray(r[k]) for r in R], axis=ax)
    y_prompt = cat('y_p').reshape(8, S, D)
    y_sample = cat('y_s').reshape(32, 4, D)
    pr_c = cat('o_pc', 1).reshape(2, 8, S, 2, 4, 64)
    pr_s = cat('o_ps', 1).reshape(2, 8, S, 2, 4, 64)
    pr_w = cat('o_pw', 1).reshape(2, 8, 512, 2, 4, 64)
    pr_b = cat('o_pb').reshape(8, S, 2, 4, 64)
    sm_c = cat('o_sc', 1).reshape(2, 32, 4, 2, 4, 64)
    sm_s = cat('o_ss', 1).reshape(2, 32, 4, 2, 4, 64)
    sm_w = cat('o_sw', 1).reshape(2, 32, 512, 2, 4, 64)
    sm_b = cat('o_sb').reshape(32, 2048, 2, 4, 64)
    return (y_prompt, y_sample, pr_c, pr_s, pr_w, pr_b, sm_c, sm_s, sm_w, sm_b)
```
